# Optimizing a Trainium2 kernel written in Bass

```python
import jax, jax.numpy as jnp
from jax import lax
import numpy as np

D_MODEL = 1024
BATCH = 2
SEQ = 8192
DEPTH = 1

EPS = 1e-6
N_HEADS_ATT = 8
N_KV_HEADS = 2
HEAD_DIM_ATT = 64
GROUP_SIZE = N_HEADS_ATT // N_KV_HEADS
WINDOW = 128
BLOCK = 128
N_HEADS_M = 4
HEAD_DIM_M = 128
CHUNK = 128
CONV_WIDTH = 3
ATT_Q = N_HEADS_ATT * HEAD_DIM_ATT
ATT_KV = N_KV_HEADS * HEAD_DIM_ATT
M_W = N_HEADS_M * HEAD_DIM_M
MIX_WIDTH = ATT_Q + M_W
N_GATES = 4 * N_HEADS_M
IN_SIZES = [ATT_Q, ATT_KV, ATT_KV, M_W, M_W, M_W, M_W, N_GATES]
IN_COLS = int(sum(IN_SIZES))
IN_SPLITS = [int(s) for s in np.cumsum(IN_SIZES)[:-1]]
D_FF = ((8 * D_MODEL // 3 + 255) // 256) * 256
N_MOD = 6

kernel_name = "hymba_style_attn_mlstm_encoder_layer"


def rms_norm(x, g):
    xf = x.astype(jnp.float32)
    out = xf * lax.rsqrt(jnp.mean(xf * xf, axis=-1, keepdims=True) + EPS)
    return (out * g.astype(jnp.float32)).astype(x.dtype)


def centred_dwconv(u, w, b):
    k = w.shape[0]
    pad = k // 2
    s = u.shape[1]
    up = jnp.pad(u, ((0, 0), (pad, pad), (0, 0)))
    out = up[:, 0:s] * w[0]
    for j in range(1, k):
        out = out + up[:, j:j + s] * w[j]
    return out + b


def banded_attention(q, k, v, sink):
    b, s, _, hd = q.shape
    nb = s // BLOCK
    qb = q.astype(jnp.float32).reshape(b, nb, BLOCK, N_KV_HEADS, GROUP_SIZE, hd)

    def key_blocks(t):
        tp = jnp.pad(t.astype(jnp.float32), ((0, 0), (BLOCK, BLOCK), (0, 0), (0, 0)))
        tp = tp.reshape(b, nb + 2, BLOCK, N_KV_HEADS, hd)
        return jnp.concatenate([tp[:, :-2], tp[:, 1:-1], tp[:, 2:]], axis=2)

    kb = key_blocks(k)
    vb = key_blocks(v)
    scores = jnp.einsum('bnqkgd,bnskd->bnkgqs', qb, kb) * (hd ** -0.5)
    qpos = jnp.arange(nb)[:, None] * BLOCK + jnp.arange(BLOCK)[None, :]
    kpos = jnp.arange(nb)[:, None] * BLOCK - BLOCK + jnp.arange(3 * BLOCK)[None]
    dist = jnp.abs(kpos[:, None, :] - qpos[:, :, None])
    valid = (dist <= WINDOW) & (kpos >= 0)[:, None, :] & (kpos < s)[:, None, :]
    slopes = 2.0 ** (-8.0 * (jnp.arange(N_HEADS_ATT, dtype=jnp.float32) + 1.0) / N_HEADS_ATT)
    slopes = slopes.reshape(N_KV_HEADS, GROUP_SIZE)
    logits = scores - slopes[None, None, :, :, None, None] * dist[None, :, None, None].astype(jnp.float32)
    logits = jnp.where(valid[None, :, None, None], logits, -jnp.inf)
    sink_l = sink.astype(jnp.float32).reshape(N_KV_HEADS, GROUP_SIZE)[None, None, :, :, None]
    mx = jnp.maximum(jnp.max(logits, axis=-1), sink_l)
    p = jnp.exp(logits - mx[..., None])
    den = jnp.sum(p, axis=-1) + jnp.exp(sink_l - mx)
    p = p / den[..., None]
    out = jnp.einsum('bnkgqs,bnskd->bnqkgd', p, vb)
    return out.reshape(b, s, N_HEADS_ATT * hd)


def mlstm_chunkwise(q, k, v, i_pre, f_pre):
    b, s, h, d = q.shape
    nc = s // CHUNK

    def blocks(t):
        t = t.reshape((b, nc, CHUNK, h) + t.shape[3:])
        return jnp.moveaxis(t, 3, 2)

    qc = blocks(q) * (d ** -0.5)
    kc = blocks(k)
    vc = blocks(v)
    ig = blocks(i_pre)
    logf = blocks(jax.nn.log_sigmoid(f_pre))
    bcum = jnp.cumsum(logf, axis=-1)
    b_last = bcum[..., -1]
    a = b_last[..., None] - bcum + ig
    a_max = jnp.max(a, axis=-1)
    w = jnp.exp(a - a_max[..., None])
    kv_c = jnp.einsum('bchs,bchsk,bchsv->bchkv', w, kc, vc)
    n_c = jnp.einsum('bchs,bchsk->bchk', w, kc)

    def step(carry, inp):
        c_st, n_st, m_st = carry
        bl, am, kv, nn = inp
        m_new = jnp.maximum(bl + m_st, am)
        decay = jnp.exp(bl + m_st - m_new)
        inj = jnp.exp(am - m_new)
        c_new = decay[..., None, None] * c_st + inj[..., None, None] * kv
        n_new = decay[..., None] * n_st + inj[..., None] * nn
        return (c_new, n_new, m_new), (c_st, n_st, m_st)

    init = (jnp.zeros((b, h, d, d), jnp.float32),
            jnp.zeros((b, h, d), jnp.float32),
            jnp.zeros((b, h), jnp.float32))
    xs = (jnp.moveaxis(b_last, 1, 0), jnp.moveaxis(a_max, 1, 0),
          jnp.moveaxis(kv_c, 1, 0), jnp.moveaxis(n_c, 1, 0))
    _, (c_prev, n_prev, m_prev) = lax.scan(step, init, xs)
    c_prev = jnp.moveaxis(c_prev, 0, 1)
    n_prev = jnp.moveaxis(n_prev, 0, 1)
    m_prev = jnp.moveaxis(m_prev, 0, 1)

    tril = jnp.tril(jnp.ones((CHUNK, CHUNK), dtype=bool))
    dlog = bcum[..., :, None] - bcum[..., None, :] + ig[..., None, :]
    dlog = jnp.where(tril, dlog, -jnp.inf)
    g = bcum + m_prev[..., None]
    m_t = jnp.maximum(g, jnp.max(dlog, axis=-1))
    dw = jnp.exp(dlog - m_t[..., None])
    sc = jnp.einsum('bchtd,bchsd->bchts', qc, kc) * dw
    inter = jnp.exp(g - m_t)
    num = (jnp.einsum('bchts,bchsv->bchtv', sc, vc)
           + inter[..., None] * jnp.einsum('bchtk,bchkv->bchtv', qc, c_prev))
    den = jnp.sum(sc, axis=-1) + inter * jnp.einsum('bchtk,bchk->bcht', qc, n_prev)
    hout = num / jnp.maximum(jnp.abs(den), jnp.exp(-m_t))[..., None]
    return jnp.moveaxis(hout, 2, 3).reshape(b, s, h, d)


def hybrid_layer(x, c, w_mod, b_mod, g_norm1, w_in, conv_w, conv_b, b_gates, sink,
                 g_attn_out, g_mlstm_out, w_out, g_norm2, w_ffn_in, w_ffn_out):
    b, s, _ = x.shape
    mod = jax.nn.silu(c) @ w_mod + b_mod
    shift1, scale1, gate1, shift2, scale2, gate2 = [m[:, None, :] for m in jnp.split(mod, N_MOD, axis=-1)]

    hmix = rms_norm(x, g_norm1) * (1.0 + scale1) + shift1
    proj = hmix @ w_in
    q_a, k_a, v_a, q_m, k_m, v_m, o_m, gates = jnp.split(proj, IN_SPLITS, axis=-1)

    att = banded_attention(q_a.reshape(b, s, N_HEADS_ATT, HEAD_DIM_ATT),
                           k_a.reshape(b, s, N_KV_HEADS, HEAD_DIM_ATT),
                           v_a.reshape(b, s, N_KV_HEADS, HEAD_DIM_ATT), sink)
    att = rms_norm(att, g_attn_out).astype(x.dtype)

    qk = jax.nn.silu(centred_dwconv(jnp.concatenate([q_m, k_m], axis=-1), conv_w, conv_b))
    q_m, k_m = jnp.split(qk, 2, axis=-1)
    gates = gates.astype(jnp.float32) + b_gates.astype(jnp.float32)
    i_f, f_f, i_b, f_b = jnp.split(gates, 4, axis=-1)
    qm = q_m.astype(jnp.float32).reshape(b, s, N_HEADS_M, HEAD_DIM_M)
    km = k_m.astype(jnp.float32).reshape(b, s, N_HEADS_M, HEAD_DIM_M)
    vm = v_m.astype(jnp.float32).reshape(b, s, N_HEADS_M, HEAD_DIM_M)
    flip = lambda t: jnp.flip(t, axis=1)
    h_fwd = mlstm_chunkwise(qm, km, vm, i_f, f_f)
    h_bwd = flip(mlstm_chunkwise(flip(qm), flip(km), flip(vm), flip(i_b), flip(f_b)))
    h_m = rms_norm(h_fwd + h_bwd, g_mlstm_out.reshape(N_HEADS_M, HEAD_DIM_M))
    h_m = (jax.nn.sigmoid(o_m.astype(jnp.float32)) * h_m.reshape(b, s, M_W)).astype(x.dtype)

    mix = jnp.concatenate([att, h_m], axis=-1) @ w_out
    x = x + gate1 * mix

    hff = rms_norm(x, g_norm2) * (1.0 + scale2) + shift2
    gte, up = jnp.split(hff @ w_ffn_in, 2, axis=-1)
    ff = (jax.nn.silu(gte) * up) @ w_ffn_out
    return x + gate2 * ff


def setup_inputs(seed: int = 0) -> dict:
    key = jax.random.key(seed)
    ks = jax.random.split(key, 20)
    nrm = lambda k, shape, scale: jax.random.normal(k, shape, jnp.float32) * scale
    gain = lambda k, shape: 1.0 + 0.02 * jax.random.normal(k, shape, jnp.float32)
    fb = jnp.linspace(3.0, 6.0, N_HEADS_M, dtype=jnp.float32)
    b_gates = jnp.concatenate([
        nrm(ks[8], (DEPTH, N_HEADS_M), 0.1),
        fb + nrm(ks[9], (DEPTH, N_HEADS_M), 0.1),
        nrm(ks[10], (DEPTH, N_HEADS_M), 0.1),
        fb + nrm(ks[11], (DEPTH, N_HEADS_M), 0.1),
    ], axis=-1)
    return {
        "x": nrm(ks[0], (BATCH, SEQ, D_MODEL), 1.0),
        "c": nrm(ks[1], (BATCH, D_MODEL), 1.0),
        "w_mod": nrm(ks[2], (DEPTH, D_MODEL, N_MOD * D_MODEL), D_MODEL ** -0.5),
        "b_mod": nrm(ks[3], (DEPTH, N_MOD * D_MODEL), 0.02),
        "g_norm1": gain(ks[4], (DEPTH, D_MODEL)),
        "w_in": nrm(ks[5], (DEPTH, D_MODEL, IN_COLS), D_MODEL ** -0.5),
        "conv_w": nrm(ks[6], (DEPTH, CONV_WIDTH, 2 * M_W), CONV_WIDTH ** -0.5),
        "conv_b": nrm(ks[7], (DEPTH, 2 * M_W), 0.02),
        "b_gates": b_gates,
        "sink": nrm(ks[12], (DEPTH, N_HEADS_ATT), 0.5),
        "g_attn_out": gain(ks[13], (DEPTH, ATT_Q)),
        "g_mlstm_out": gain(ks[14], (DEPTH, M_W)),
        "w_out": nrm(ks[15], (DEPTH, MIX_WIDTH, D_MODEL), MIX_WIDTH ** -0.5),
        "g_norm2": gain(ks[16], (DEPTH, D_MODEL)),
        "w_ffn_in": nrm(ks[17], (DEPTH, D_MODEL, 2 * D_FF), D_MODEL ** -0.5),
        "w_ffn_out": nrm(ks[18], (DEPTH, D_FF, D_MODEL), D_FF ** -0.5),
        "g_final": gain(ks[19], (D_MODEL,)),
    }


def reference(x, c, w_mod, b_mod, g_norm1, w_in, conv_w, conv_b, b_gates, sink,
              g_attn_out, g_mlstm_out, w_out, g_norm2, w_ffn_in, w_ffn_out, g_final):
    for l in range(DEPTH):
        x = hybrid_layer(x, c, w_mod[l], b_mod[l], g_norm1[l], w_in[l], conv_w[l], conv_b[l],
                         b_gates[l], sink[l], g_attn_out[l], g_mlstm_out[l], w_out[l],
                         g_norm2[l], w_ffn_in[l], w_ffn_out[l])
    return rms_norm(x, g_final)
```

```python
import contextlib
import math
import os
import numpy as np
import concourse.bass as bass
import concourse.mybir as mybir
from concourse.bass_utils import run_bass_kernel_spmd

F32 = mybir.dt.float32
BF16 = mybir.dt.bfloat16
AF = mybir.ActivationFunctionType
ALU = mybir.AluOpType

D = 1024
SEQ = 8192
NT = 16
EPS = 1e-6
DFF = 2816
NFC = DFF // 128
QA, KA, VA, QM, KM, VM, OM = 0, 512, 768, 896, 1408, 1920, 2432
NWM = 2944
LN_HALF = math.log(0.5)
LN_EB = math.log(0.5 / math.sqrt(128.0))


class Buf:
    __slots__ = ("name", "w", "r")

    def __init__(self, name):
        self.name = name
        self.w = None
        self.r = []


class Op:
    __slots__ = ("eng", "fn", "deps", "order", "sem", "val", "signal", "isdma", "idx", "seg", "cost", "fin", "done", "tag")


class Tile:
    def __init__(self, ap, name):
        self.ap = ap
        self.buf = Buf(name)


class Prog:
    ENGS = ["pe", "act", "dve", "pool", "sp"]
    SAME_RAW = ("act", "dve", "pool")
    EPOCH = 20000
    COST = {"pe": 0.12, "act": 0.5, "dve": 0.5, "pool": 1.0, "sp": 0.05}
    WINDOW = 160

    def __init__(self, nc):
        self.nc = nc
        self.all = []
        self.seg = 0
        self.ndma_sems = {"sp": 40, "pool": 4, "act": 8}

    def op(self, eng, fn, reads=(), writes=(), dma=False, c=None, tag=None):
        o = Op()
        o.tag = tag
        o.eng, o.fn, o.isdma, o.signal = eng, fn, dma, dma
        o.sem = o.val = None
        o.idx, o.seg = len(self.all), self.seg
        o.cost = (2.5 if dma else (0.0 if fn is None else self.COST[eng])) if c is None else c
        deps = {}
        for t in reads:
            b = t.buf if isinstance(t, Tile) else t
            if b.w is not None:
                deps[id(b.w)] = (b.w, True)
        for t in writes:
            b = t.buf if isinstance(t, Tile) else t
            if b.w is not None and id(b.w) not in deps:
                deps[id(b.w)] = (b.w, False)
            for r in b.r:
                if id(r) not in deps:
                    deps[id(r)] = (r, False)
        keep, order = [], []
        for d, raw in deps.values():
            if d is o:
                continue
            order.append(d)
            if d.isdma or dma or d.eng != eng:
                keep.append(d)
            elif eng in self.SAME_RAW:
                keep.append(d)
        o.deps, o.order = keep, order
        for d in keep:
            d.signal = True
        for t in writes:
            b = t.buf if isinstance(t, Tile) else t
            b.w = o
            b.r = []
        for t in reads:
            b = t.buf if isinstance(t, Tile) else t
            b.r.append(o)
        self.all.append(o)
        return o

    def fence(self):
        self.seg += 1

    def _schedule(self, ops):
        pend = {e: [o for o in ops if o.eng == e] for e in self.ENGS}
        head = {e: 0 for e in self.ENGS}
        free = {e: 0.0 for e in self.ENGS}
        out = {e: [] for e in self.ENGS}
        cur_tbl = [None]
        for o in ops:
            o.done = False
        left = len(ops)
        while left:
            best = None
            for e in self.ENGS:
                lst = pend[e]
                h = head[e]
                while h < len(lst) and lst[h].done:
                    h += 1
                head[e] = h
                n = 0
                j = h
                while j < len(lst) and n < self.WINDOW:
                    o = lst[j]
                    j += 1
                    if o.done:
                        continue
                    n += 1
                    rdy = 0.0
                    ok = True
                    for d in o.order:
                        if d.seg == o.seg and not d.done:
                            ok = False
                            break
                        if d.seg == o.seg and d.fin > rdy:
                            rdy = d.fin
                    if not ok:
                        continue
                    st = max(rdy, free[e])
                    if o.tag is not None and cur_tbl[0] is not None and o.tag != cur_tbl[0]:
                        st += 1.3
                    if best is None or st < best[0] - 1e-9 or (abs(st - best[0]) <= 1e-9 and o.idx < best[1].idx):
                        best = (st, o)
                    if st <= free[e] + 1e-9:
                        break
            st, o = best
            o.done = True
            if o.tag is not None:
                cur_tbl[0] = o.tag
            if o.isdma:
                o.fin = st + o.cost
                free[o.eng] = st + 0.05
            else:
                o.fin = st + o.cost
                free[o.eng] = o.fin
            out[o.eng].append(o)
            left -= 1
        return out

    def emit(self, stack):
        nc = self.nc
        sems = {}

        def getsem(key):
            if key not in sems:
                sems[key] = stack.enter_context(nc.semaphore("s_%s" % "_".join(str(k) for k in key)))
            return sems[key]

        nseg = self.seg + 1
        final = {e: [] for e in self.ENGS}
        dma_last = {q: [None] * n for q, n in self.ndma_sems.items()}
        dma_cnt = {q: [0] * n for q, n in self.ndma_sems.items()}
        dma_rr = {q: 0 for q in self.ndma_sems}
        last_compute = {e: None for e in self.ENGS}
        for sg in range(nseg):
            ops = [o for o in self.all if o.seg == sg]
            if sg > 0:
                lasts = [o for o in last_compute.values() if o is not None]
                dmas = [o for q in dma_last for o in dma_last[q] if o is not None]
                for e in self.ENGS:
                    f = Op()
                    f.tag = None
                    f.eng, f.fn, f.isdma, f.signal = e, None, False, False
                    f.sem = f.val = None
                    f.deps = list(lasts) + dmas
                    for d in f.deps:
                        d.signal = True
                    final[e].append(f)
            sched = self._schedule(ops)
            for e in self.ENGS:
                for o in sched[e]:
                    if o.isdma:
                        k = dma_rr[e]
                        dma_rr[e] = (k + 1) % self.ndma_sems[e]
                        prev = dma_last[e][k]
                        if prev is not None:
                            o.deps = o.deps + [prev]
                        dma_last[e][k] = o
                        dma_cnt[e][k] += 1
                        o.sem = ("dma", e, k)
                        o.val = 16 * dma_cnt[e][k]
                    elif o.fn is not None:
                        last_compute[e] = o
                    final[e].append(o)
        self.ops = final
        for e in self.ENGS:
            n = 0
            for o in self.ops[e]:
                if o.isdma or not o.signal:
                    continue
                assert o.fn is not None
                o.sem = ("eng", e, n // self.EPOCH)
                o.val = n % self.EPOCH + 1
                n += 1
        for e in self.ENGS:
            for o in self.ops[e]:
                if o.signal:
                    getsem(o.sem)
                for d in o.deps:
                    assert d.sem is not None
        block = stack.enter_context(nc.Block())

        def run(e, engobj):
            waited = {}
            for o in self.ops[e]:
                for d in o.deps:
                    if waited.get(d.sem, 0) < d.val:
                        engobj.wait_ge(sems[d.sem], d.val)
                        waited[d.sem] = d.val
                if o.fn is None:
                    continue
                ins = o.fn(engobj)
                if o.signal:
                    ins.then_inc(sems[o.sem], 16 if o.isdma else 1)

        @block.tensor
        def _(eng):
            run("pe", eng)

        @block.scalar
        def _(eng):
            run("act", eng)

        @block.vector
        def _(eng):
            run("dve", eng)

        @block.gpsimd
        def _(eng):
            run("pool", eng)

        @block.sync
        def _(eng):
            run("sp", eng)


class StopBuild(Exception):
    pass


class Arena:
    def __init__(self, nc, nbytes):
        self.t = nc.alloc_sbuf_tensor("arena", [128, nbytes // 4], F32)
        self.ap = self.t.ap()
        self.size = nbytes
        self.top = 0
        self.n = 0

    def mark(self):
        return self.top

    def release(self, m):
        self.top = m

    def tile(self, shape, dtype, name=None):
        esz = 4 if dtype == F32 else 2
        free = int(np.prod(shape[1:]))
        nb = (free * esz + 31) // 32 * 32
        assert self.top + nb <= self.size, ("SBUF arena overflow", name, self.top, nb)
        a = self.ap[:, self.top // 4:(self.top + nb) // 4]
        if dtype != F32:
            a = a.bitcast(dtype)
        a = a[:, 0:free]
        if len(shape) == 3:
            a = a.rearrange("p (a b) -> p a b", a=shape[1])
        elif len(shape) == 4:
            a = a.rearrange("p (a b c) -> p a b c", a=shape[1], b=shape[2])
        if shape[0] != 128:
            a = a[0:shape[0]]
        self.top += nb
        self.n += 1
        return Tile(a, name or ("t%d" % self.n))


def bc(ap, shape):
    return ap.broadcast_to(shape)


def build_program():
    nc = bass.Bass("TRN2", target_bir_lowering=False)

    def din(name, shape, dt=F32):
        return nc.dram_tensor(name, list(shape), dt, kind="ExternalInput").ap()

    xp = din("xp", [5, 18 * 128, D])
    cvec = din("cvec", [128, 8])
    w_mod = din("w_mod", [D, 6 * D])
    bmod_b = din("bmod_b", [128, 6 * D])
    g1_b = din("g1_b", [128, D])
    g2_b = din("g2_b", [128, D])
    gfin_b = din("gfin_b", [128, D])
    w_main = din("w_main", [D, NWM])
    wg = din("wg", [D, 40])
    bg_b = din("bg_b", [128, 40])
    cw = din("cw", [128, 5 * 8 * 3])
    cb = din("cb", [128, 8])
    sink_b = din("sink_b", [128, 8])
    rowsc = din("rowsc", [128, 8])
    w_out = din("w_out", [D, D])
    w_fi = din("w_fi", [D, 2 * DFF])
    w_fo = din("w_fo", [DFF, D])
    flags = din("flags", [128, 24])
    cmat = din("cmat", [128, 5 * 128])
    ebt = din("ebt", [128, 3 * 8 * 128])
    out = nc.dram_tensor("out", [NT * 128, D], F32, kind="ExternalOutput").ap()

    hf_s = nc.dram_tensor("hf_s", [NT, 128, 512], F32).ap()
    hb_s = nc.dram_tensor("hb_s", [NT, 128, 512], F32).ap()
    th_s = nc.dram_tensor("th_s", [NT, 128, 512], F32).ap()
    at_s = nc.dram_tensor("at_s", [NT, 128, 512], BF16).ap()
    x1_s = nc.dram_tensor("x1_s", [NT, 128, D], F32).ap()
    hf_b = [Buf("hf%d" % i) for i in range(NT)]
    hb_b = [Buf("hb%d" % i) for i in range(NT)]
    th_b = [Buf("th%d" % i) for i in range(NT)]
    at_b = [Buf("at%d" % i) for i in range(NT)]
    x1_b = [Buf("x1%d" % i) for i in range(NT)]
    out_b = [Buf("out%d" % i) for i in range(NT)]

    P = Prog(nc)
    A = Arena(nc, 207 * 1024)
    KSTOP = os.environ.get("KSTOP", "")
    dbg = nc.dram_tensor("dbg", [128, 4096], F32, kind="ExternalOutput").ap() if KSTOP else None
    dbg_b = Buf("dbg")
    dbg_off = [0]

    def dump(tile_or_ap, rd, ncols):
        if not KSTOP:
            return
        o = dbg_off[0]
        dbg_off[0] += ncols
        src = tile_or_ap.ap if isinstance(tile_or_ap, Tile) else tile_or_ap
        P.op("sp", lambda e: e.dma_start(out=dbg[:, o:o + ncols], in_=src), reads=rd, writes=[dbg_b], dma=True)

    def stop(tag):
        if KSTOP == tag:
            raise StopBuild()

    psum_t = nc.alloc_psum_tensor("psum", [128, 4096], F32)
    psum_ap = psum_t.ap()
    banks = [Tile(psum_ap[:, k * 512:(k + 1) * 512], "bank%d" % k) for k in range(8)]
    bank2 = [Tile(psum_ap[:, k * 1024:(k + 1) * 1024], "bankpair%d" % k) for k in range(4)]
    for k in range(4):
        bank2[k].bufs = [banks[2 * k].buf, banks[2 * k + 1].buf]
    rr = [0]
    busy = [False] * 8

    def ps1():
        for _ in range(8):
            k = rr[0] % 8
            rr[0] += 1
            if not busy[k]:
                busy[k] = True
                banks[k].held = [k]
                return banks[k], [banks[k].buf]
        raise RuntimeError("no free PSUM bank")

    def ps2():
        for _ in range(8):
            if rr[0] % 2:
                rr[0] += 1
            k = (rr[0] % 8) // 2
            rr[0] += 2
            if not busy[2 * k] and not busy[2 * k + 1]:
                busy[2 * k] = busy[2 * k + 1] = True
                bank2[k].held = [2 * k, 2 * k + 1]
                return bank2[k], bank2[k].bufs
        raise RuntimeError("no free PSUM bank pair")

    def pfree(pt):
        for k in pt.held:
            assert busy[k]
            busy[k] = False

    def load(dst, src, wr=None, rd=(), q="sp"):
        P.op(q, lambda e: e.dma_start(out=dst.ap if isinstance(dst, Tile) else dst, in_=src),
             reads=list(rd), writes=[dst] if wr is None else wr, dma=True)

    def store(dst_ap, dst_buf, src, q="sp"):
        P.op(q, lambda e: e.dma_start(out=dst_ap, in_=src.ap if isinstance(src, Tile) else src),
             reads=[src], writes=[dst_buf], dma=True)

    try:
        cm_f = A.tile([128, 5, 128], F32, "cmat")
        load(cm_f, cmat.rearrange("p (a b) -> p a b", a=5))
        ident_f, J_f, U_f, negU_f, maskf_f = (cm_f.ap[:, i, :] for i in range(5))
        ones_f = A.tile([128, 128], F32, "ones")
        P.op("pool", lambda e: e.memset(ones_f.ap, 1.0), writes=[ones_f])
        mhalf = A.tile([128, 8], F32, "mhalf")
        P.op("pool", lambda e: e.memset(mhalf.ap, -0.5), writes=[mhalf])
        cm_b = A.tile([128, 2, 128], BF16, "cmat_bf")
        P.op("dve", lambda e: e.tensor_copy(out=cm_b.ap[:, 0, :], in_=ident_f), reads=[cm_f], writes=[cm_b])
        P.op("dve", lambda e: e.tensor_copy(out=cm_b.ap[:, 1, :], in_=maskf_f), reads=[cm_f], writes=[cm_b])
        ident_b = cm_b.ap[:, 0, :]
        maskf_b = cm_b.ap[:, 1, :]
        fl = A.tile([128, 24], F32, "flags")
        load(fl, flags)
        cw_t = A.tile([128, 5, 8, 3], F32, "cw")
        load(cw_t, cw.rearrange("p (a b c) -> p a b c", a=5, b=8))
        cb_t = A.tile([128, 8], F32, "cb")
        load(cb_t, cb)
        bg_t = A.tile([128, 5, 8], F32, "bg")
        load(bg_t, bg_b.rearrange("p (a b) -> p a b", a=5))
        rs_t = A.tile([128, 8], F32, "rowsc")
        load(rs_t, rowsc)
        esink = A.tile([128, 8], F32, "esink")
        load(esink, sink_b)
        P.op("act", lambda e: e.activation(out=esink.ap, in_=esink.ap, func=AF.Exp), reads=[esink], writes=[esink])

        pass_mark = A.mark()
        A1_b = A.tile([128, D], F32, "A1b")
        B1_b = A.tile([128, D], F32, "B1b")
        mod_s = nc.dram_tensor("mod_s", [4, 128, D], F32).ap()
        mod_sb = [Buf("mods%d" % i) for i in range(4)]

        stage = [None, None]
        stg_i = [0]
        cast_eng = ["pool", "dve", "act"]

        def cast_block(src_ap, ncols, dst_ap, dst_tile, mode=None, arg=None, engs=cast_eng, q="sp"):
            i = stg_i[0]
            stg_i[0] += 1
            st = stage[i % len(stage)]
            eng = engs[i % len(engs)]
            load(st.ap[:, 0:ncols], src_ap, wr=[st], q=q)
            sap = st.ap[:, 0:ncols]
            real_tile = dst_tile
            dst_tile = Buf("cast%d" % i)
            if not hasattr(real_tile, "cast_bufs"):
                real_tile.cast_bufs = []
            real_tile.cast_bufs.append(dst_tile)
            if mode is None:
                if eng == "act":
                    P.op("act", lambda e: e.copy(out=dst_ap, in_=sap), reads=[st], writes=[dst_tile])
                else:
                    P.op(eng, lambda e: e.tensor_copy(out=dst_ap, in_=sap), reads=[st], writes=[dst_tile])
            elif mode == "colscale":
                eng2 = "pool" if eng == "act" else eng
                P.op(eng2, lambda e: e.tensor_tensor(out=dst_ap, in0=sap, in1=arg[0], op=ALU.mult),
                     reads=[st, arg[1]], writes=[dst_tile])
            elif mode == "rowcol":
                P.op("dve", lambda e: e.scalar_tensor_tensor(out=dst_ap, in0=sap, scalar=arg[0], in1=arg[1],
                                                             op0=ALU.mult, op1=ALU.mult),
                     reads=[st, arg[2], arg[3]], writes=[dst_tile])

        def join(tile):
            P.op("pe", None, reads=tile.cast_bufs, writes=[tile])

        Wb = A.tile([128, 8, NWM], BF16, "w_main_bf")
        wmv2 = w_main.rearrange("(kc p) n -> p kc n", p=128)
        Wg_f = A.tile([128, 8, 40], F32, "wg_f")
        load(Wg_f, wg.rearrange("(kc p) n -> p kc n", p=128))
        Wg = A.tile([128, 8, 40], BF16, "wg_bf")
        P.op("dve", lambda e: e.tensor_copy(out=Wg.ap, in_=Wg_f.ap), reads=[Wg_f], writes=[Wg])
        EB = A.tile([128, 3, 8, 128], F32, "EB")
        load(EB, ebt.rearrange("p (a b c) -> p a b c", a=3, b=8))
        m0 = A.mark()
        gate1_b = A.tile([128, D], F32, "gate1b")
        gate2_b = A.tile([128, D], F32, "gate2b")
        A2_b = A.tile([128, D], F32, "A2b")
        B2_b = A.tile([128, D], F32, "B2b")
        mod_dst = [B1_b, A1_b, gate1_b, B2_b, A2_b, gate2_b]
        c_t = A.tile([128, 8], F32, "c")
        load(c_t, cvec)
        th_c = A.tile([128, 8], F32, "thc")
        sc_t = A.tile([128, 8], F32, "sc")
        P.op("act", lambda e: e.activation(out=th_c.ap, in_=c_t.ap, func=AF.Tanh, scale=0.5), reads=[c_t], writes=[th_c])
        P.op("dve", lambda e: e.scalar_tensor_tensor(out=sc_t.ap, in0=th_c.ap, scalar=1.0, in1=c_t.ap,
                                                     op0=ALU.add, op1=ALU.mult), reads=[th_c, c_t], writes=[sc_t])
        P.op("dve", lambda e: e.tensor_scalar(out=sc_t.ap, in0=sc_t.ap, scalar1=0.5, scalar2=None, op0=ALU.mult),
             reads=[sc_t], writes=[sc_t])
        scB = A.tile([128, 8, 128], F32, "scB")
        P.op("dve", lambda e: e.tensor_copy(out=scB.ap, in_=bc(sc_t.ap.unsqueeze(2), [128, 8, 128])),
             reads=[sc_t], writes=[scB])
        wm_ring = [A.tile([128, 8, 512], F32, "wmod%d" % i) for i in range(3)]
        bm_ring = [A.tile([128, 512], F32, "bmod%d" % i) for i in range(3)]
        wmv = w_mod.rearrange("(kc p) n -> p kc n", p=128)
        for nb in range(12):
            wt = wm_ring[nb % 3]
            bt = bm_ring[nb % 3]
            load(wt, wmv[:, :, nb * 512:(nb + 1) * 512])
            load(bt, bmod_b[:, nb * 512:(nb + 1) * 512])
            pt, pb = ps1()
            for kc in range(8):
                P.op("pe", lambda e, kc=kc, wt=wt, pt=pt: e.matmul(pt.ap, lhsT=scB.ap[:, kc, :], rhs=wt.ap[:, kc, :],
                                                                 start=(kc == 0), stop=(kc == 7)),
                     reads=[scB, wt], writes=pb)
            dst = mod_dst[nb // 2].ap[:, (nb % 2) * 512:(nb % 2 + 1) * 512]
            P.op("dve", lambda e, pt=pt, bt=bt, dst=dst: e.tensor_tensor(out=dst, in0=pt.ap, in1=bt.ap, op=ALU.add),
                 reads=pb + [bt], writes=[mod_dst[nb // 2]])
            pfree(pt)
        for (At, gsrc) in ((A1_b, g1_b), (A2_b, g2_b)):
            gt = A.tile([128, D], F32, "gtmp")
            load(gt, gsrc)
            P.op("dve", lambda e, At=At, gt=gt: e.scalar_tensor_tensor(out=At.ap, in0=At.ap, scalar=1.0, in1=gt.ap,
                                                                      op0=ALU.add, op1=ALU.mult),
                 reads=[At, gt], writes=[At])
        for i_, t_ in enumerate((gate1_b, gate2_b, A2_b, B2_b)):
            store(mod_s[i_], mod_sb[i_], t_)
        if KSTOP == "mod":
            dump(A1_b, [A1_b], 1024)
            dump(B1_b, [B1_b], 1024)
            dump(gate2_b, [gate2_b], 1024)
        stage[:] = [A.tile([128, 1472], F32, "stage%d" % i) for i in range(2)]
        for kc in range(8):
            for c0 in range(0, NWM, 1472):
                cast_block(wmv2[:, kc, c0:c0 + 1472], 1472, Wb.ap[:, kc, c0:c0 + 1472], Wb)
        join(Wb)
        stop("mod")
        A.release(m0)
        P.fence()
        stop("cast")

        xt_ring = [A.tile([128, D], F32, "xt%d" % i) for i in range(2)]
        plist = [4] if KSTOP.startswith("B_") else ([3, 4] if KSTOP.startswith("F_") else range(5))
        xseq = [(pi_, e_) for pi_ in plist for e_ in range(18)]
        xnext = [0]

        def ensure_x(upto):
            while xnext[0] <= min(upto, len(xseq) - 1):
                g = xnext[0]
                pi_, e_ = xseq[g]
                load(xt_ring[g % 2], xp[pi_, e_ * 128:(e_ + 1) * 128, :])
                xnext[0] += 1
        t1_ring = [A.tile([128, D], F32, "t1_%d" % i) for i in range(1)]
        hm_ring = [A.tile([128, D], BF16, "hm%d" % i) for i in range(2)]
        st_ring = [A.tile([128, 4], F32, "stat%d" % i) for i in range(4)]
        pair_ring = [A.tile([128, 8, 258], BF16, "pair%d" % i) for i in range(4)]
        halo_h = [A.tile([128, 8, 128], BF16, "haloh%d" % i) for i in range(2)]
        cv_ring = [A.tile([128, 3, 256], F32, "cv%d" % i) for i in range(2)]
        kT_ring = [A.tile([128, 4, 256], BF16, "kT%d" % i) for i in range(2)]
        qT_ring = [A.tile([128, 4, 256], BF16, "qT%d" % i) for i in range(2)]
        ktok_ring = [A.tile([128, 2, 512], BF16, "ktok%d" % i) for i in range(1)]
        vaug_ring = [A.tile([128, 2, 4, 129], BF16, "vaug%d" % i) for i in range(2)]
        wv_ring = [A.tile([128, 2, 4, 129], BF16, "wv%d" % i) for i in range(1)]
        for t in vaug_ring:
            P.op("pool", lambda e, t=t: e.memset(t.ap, 1.0), writes=[t])
        g_ring = [A.tile([128, 8, 2, 4], F32, "g%d" % i) for i in range(2)]
        nlfB_ring = [A.tile([128, 8, 128], F32, "nlfB%d" % i) for i in range(1)]
        E_ring = [A.tile([128, 4, 128], F32, "E%d" % i) for i in range(1)]
        Em_ring = [A.tile([128, 4, 128], BF16, "Em%d" % i) for i in range(2)]
        PT_ring = [A.tile([128, 4, 128], BF16, "PT%d" % i) for i in range(2)]
        tB_ring = [A.tile([128, 4, 129], F32, "tB%d" % i) for i in range(1)]
        R_ring = [A.tile([128, 4, 129], F32, "R%d" % i) for i in range(1)]
        dn_ring = [A.tile([128, 8], F32, "dn%d" % i) for i in range(2)]
        ho_ring = [A.tile([128, 4, 128], F32, "ho%d" % i) for i in range(2)]
        Cst = A.tile([128, 4, 129], F32, "Cstate")
        Cf = A.tile([128, 4, 129], F32, "Cf")
        Cb = A.tile([128, 4, 129], F32, "Cb")
        Cbf = A.tile([128, 4, 129], BF16, "Cbf")
        for t in (Cst, Cf, Cb):
            P.op("pool", lambda e, t=t: e.memset(t.ap, 0.0), writes=[t])
        qa_ring = [A.tile([128, 2, 4, 256], BF16, "qa%d" % i) for i in range(2)]
        for t in qa_ring:
            P.op("pool", lambda e, t=t: e.memset(t.ap, 0.0), writes=[t])
        kaT = A.tile([128, 2, 18 * 128], BF16, "kaT")
        vaa = A.tile([128, 18, 2, 65], BF16, "vaa")
        kaT_b = [Buf("kaT%d" % i) for i in range(18)]
        vaa_b = [Buf("vaa%d" % i) for i in range(18)]
        P.op("pool", lambda e: e.memset(vaa.ap, 1.0), writes=vaa_b)
        Ex_ring = [A.tile([128, 8, 128], F32, "Ex%d" % i) for i in range(1)]
        PA_ring = [A.tile([128, 8, 128], BF16, "PA%d" % i) for i in range(3)]
        att_ring = [A.tile([128, 8, 64], F32, "att%d" % i) for i in range(2)]
        attn_ring = [A.tile([128, 512], BF16, "attn%d" % i) for i in range(2)]
        attT_ring = [A.tile([128, 4, 128], BF16, "attT%d" % i) for i in range(2)]
        tho_ring = [A.tile([128, 512], F32, "tho%d" % i) for i in range(2)]
        cnt = {"tile": 0, "pair": 0, "chunk": 0, "att": 0, "cv": 0, "xg": 0}

        def rstd_from_ssq(ssq_ap, ssq_tile, n, width):
            P.op("dve", lambda e: e.tensor_scalar(out=ssq_ap, in0=ssq_ap, scalar1=1.0 / width, scalar2=EPS,
                                                  op0=ALU.mult, op1=ALU.add), reads=[ssq_tile], writes=[ssq_tile])
            P.op("pool", lambda e: e.tensor_tensor(out=ssq_ap, in0=ssq_ap, in1=mhalf.ap[:, 0:n], op=ALU.pow),
                 reads=[ssq_tile, mhalf], writes=[ssq_tile])

        def make_hmix_tile(src_ap, src_reads, Ab, Bb, flag_ap=None):
            i = cnt["tile"]
            cnt["tile"] += 1
            g = cnt["xg"]
            cnt["xg"] += 1
            ensure_x(g + 1)
            xt = xt_ring[g % 2]
            st = st_ring[i % 4]
            hm = hm_ring[i % 2]
            P.op("act", lambda e: e.activation(out=hm.ap, in_=xt.ap, func=AF.Square, accum_out=st.ap[:, 0:1]),
                 reads=[xt], writes=[hm, st], c=1.05)
            rstd_from_ssq(st.ap[:, 0:1], st, 1, D)
            t1 = t1_ring[0]
            P.op("dve", lambda e: e.scalar_tensor_tensor(out=t1.ap, in0=xt.ap, scalar=st.ap[:, 0:1], in1=Ab.ap,
                                                         op0=ALU.mult, op1=ALU.mult), reads=[xt, st, Ab], writes=[t1], c=1.3)
            P.op("pool", lambda e: e.tensor_tensor(out=hm.ap, in0=t1.ap, in1=Bb.ap, op=ALU.add),
                 reads=[t1, Bb], writes=[hm], c=2.4)
            if flag_ap is not None:
                P.op("pool", lambda e: e.tensor_scalar(out=hm.ap, in0=hm.ap, scalar1=flag_ap, scalar2=None, op0=ALU.mult),
                     reads=[hm, fl], writes=[hm])
            pt, pb = ps1()
            pv = pt.ap.bitcast(BF16).rearrange("p (a b) -> p a b", a=8)
            for kc in range(8):
                P.op("pe", lambda e, kc=kc: e.transpose(out=pv[:, kc, :], in_=hm.ap[:, kc * 128:(kc + 1) * 128],
                                                        identity=ident_b), reads=[hm, cm_b], writes=pb)
            return pv, pb, pt

        def run_pass(pi, mode):
            outs = mode in ("F", "B")
            full = mode == "F"
            sc_dst, sc_buf = (hf_s, hf_b) if mode == "F" else (hb_s, hb_b)

            def feat_proj(col0, hp, n, off=0):
                pt, pb = ps1()
                for kc in range(8):
                    P.op("pe", lambda e, kc=kc: e.matmul(pt.ap[:, 0:n], lhsT=Wb.ap[:, kc, col0:col0 + 128],
                                                         rhs=hp.ap[:, kc, off:off + n], start=(kc == 0), stop=(kc == 7)),
                         reads=[Wb, hp], writes=pb)
                return pt, pb

            def tok_proj(col0, ncols, lhs_ap, lhs_tile, wt=None):
                wt = Wb if wt is None else wt
                pt, pb = ps1()
                for kc in range(8):
                    P.op("pe", lambda e, kc=kc: e.matmul(pt.ap[:, 0:ncols], lhsT=lhs_ap[:, kc, :],
                                                         rhs=wt.ap[:, kc, col0:col0 + ncols], start=(kc == 0), stop=(kc == 7)),
                         reads=[wt, lhs_tile], writes=pb)
                return pt, pb

            def conv_silu(pt, pb, chunk, dst_ap, dst_tile):
                i = cnt["cv"]
                cnt["cv"] += 1
                cv = cv_ring[i % 2]
                w0, w1, w2 = (cw_t.ap[:, pi, chunk, k:k + 1] for k in range(3))
                P.op("act", lambda e: e.activation(out=cv.ap[:, 0, :], in_=pt.ap[:, 0:256], func=AF.Identity,
                                                   scale=w0, bias=cb_t.ap[:, chunk:chunk + 1]),
                     reads=pb + [cw_t, cb_t], writes=[cv])
                P.op("dve", lambda e: e.scalar_tensor_tensor(out=cv.ap[:, 1, :], in0=pt.ap[:, 1:257], scalar=w1,
                                                             in1=cv.ap[:, 0, :], op0=ALU.mult, op1=ALU.add),
                     reads=pb + [cw_t, cv], writes=[cv])
                P.op("dve", lambda e: e.scalar_tensor_tensor(out=cv.ap[:, 0, :], in0=pt.ap[:, 2:258], scalar=w2,
                                                             in1=cv.ap[:, 1, :], op0=ALU.mult, op1=ALU.add),
                     reads=pb + [cw_t, cv], writes=[cv])
                pfree(pt)
                P.op("act", lambda e: e.activation(out=cv.ap[:, 2, :], in_=cv.ap[:, 0, :], func=AF.Tanh, scale=0.5),
                     reads=[cv], writes=[cv], tag="T")
                P.op("dve", lambda e: e.scalar_tensor_tensor(out=dst_ap, in0=cv.ap[:, 2, :], scalar=1.0,
                                                             in1=cv.ap[:, 0, :], op0=ALU.add, op1=ALU.mult),
                     reads=[cv], writes=[dst_tile])

            def ka_va_proj(e_idx, lhs_ap, lhs_tile, ntok_off=None, hp=None):
                for g in range(2):
                    pt, pb = ps1()
                    for kc in range(8):
                        P.op("pe", lambda e, kc=kc, g=g, pt=pt: e.matmul(pt.ap[:, 0:128], lhsT=Wb.ap[:, kc, KA + g * 128:KA + (g + 1) * 128],
                                                                  rhs=lhs_ap[:, kc, :], start=(kc == 0), stop=(kc == 7)),
                             reads=[Wb, lhs_tile], writes=pb)
                    P.op("act", lambda e, g=g, pt=pt: e.copy(out=kaT.ap[:, g, e_idx * 128:(e_idx + 1) * 128], in_=pt.ap[:, 0:128]),
                         reads=pb, writes=[kaT_b[e_idx]])
                    pfree(pt)
                pt, pb = tok_proj(VA, 128, lhs_ap, lhs_tile)
                P.op("dve", lambda e: e.tensor_copy(out=vaa.ap[:, e_idx, :, 0:64],
                                                    in_=pt.ap[:, 0:128].rearrange("p (g d) -> p g d", g=2)),
                     reads=pb, writes=[vaa_b[e_idx]])
                pfree(pt)

            def attention_block(n):
                i = cnt["att"]
                cnt["att"] += 1
                qa = qa_ring[(n // 2) % 2]
                qc = (n % 2) * 128
                PAs = []
                for kb in range(3):
                    ek = n + kb
                    pt, pb = ps2()
                    pv = pt.ap.rearrange("p (h q) -> p h q", h=8)
                    for h in range(8):
                        par, hp4, g = h % 2, h // 2, h // 4
                        P.op("pe", lambda e, h=h, par=par, hp4=hp4, g=g, pv=pv, ek=ek: e.matmul(
                            pv[:, h, :], lhsT=kaT.ap[:, g, ek * 128:(ek + 1) * 128],
                            rhs=qa.ap[:, par, hp4, qc:qc + 128], start=True, stop=True),
                            reads=[kaT_b[ek], qa], writes=pb)
                    Ex = Ex_ring[0]
                    for hh in range(2):
                        P.op("act", lambda e, hh=hh, pv=pv, Ex=Ex: e.activation(out=Ex.ap[:, hh * 4:hh * 4 + 4, :],
                                                                              in_=pv[:, hh * 4:hh * 4 + 4, :],
                                                                              func=AF.Exp, scale=0.125),
                             reads=pb, writes=[Ex])
                    pfree(pt)
                    PA = PA_ring[kb]
                    P.op("pool", lambda e, Ex=Ex, PA=PA, kb=kb: e.tensor_tensor(out=PA.ap, in0=Ex.ap, in1=EB.ap[:, kb, :, :],
                                                                             op=ALU.mult), reads=[Ex, EB], writes=[PA], c=2.0)
                    PAs.append(PA)
                    yield
                pto, pb = ps2()
                po = pto.ap.rearrange("p (h q) -> p h q", h=8)
                for h in range(8):
                    g = h // 4
                    for kb in range(3):
                        ek = n + kb
                        P.op("pe", lambda e, h=h, g=g, kb=kb, ek=ek, po=po: e.matmul(
                            po[:, h, 0:65], lhsT=PAs[kb].ap[:, h, :], rhs=vaa.ap[:, ek, g, :],
                            start=(kb == 0), stop=(kb == 2)), reads=[PAs[kb], vaa_b[ek]], writes=pb)
                yield
                dn = dn_ring[i % 2]
                P.op("dve", lambda e: e.tensor_tensor(out=dn.ap, in0=po[:, :, 64], in1=esink.ap, op=ALU.add),
                     reads=pb + [esink], writes=[dn])
                P.op("dve", lambda e: e.reciprocal(out=dn.ap, in_=dn.ap), reads=[dn], writes=[dn])
                att = att_ring[i % 2]
                P.op("dve", lambda e: e.tensor_tensor(out=att.ap, in0=po[:, :, 0:64],
                                                      in1=bc(dn.ap.unsqueeze(2), [128, 8, 64]), op=ALU.mult),
                     reads=pb + [dn], writes=[att])
                pfree(pto)
                st = st_ring[cnt["tile"] % 4]
                cnt["tile"] += 1
                attn = attn_ring[i % 2]
                P.op("act", lambda e: e.activation(out=attn.ap, in_=att.ap.rearrange("p h d -> p (h d)"), func=AF.Square,
                                                   accum_out=st.ap[:, 0:1]), reads=[att], writes=[attn, st])
                rstd_from_ssq(st.ap[:, 0:1], st, 1, 512)
                P.op("act", lambda e: e.activation(out=attn.ap, in_=att.ap.rearrange("p h d -> p (h d)"), func=AF.Identity,
                                                   scale=st.ap[:, 0:1]), reads=[att, st], writes=[attn])
                pt, pb = ps1()
                pv = pt.ap.bitcast(BF16)[:, 0:512].rearrange("p (a b) -> p a b", a=4)
                for kc in range(4):
                    P.op("pe", lambda e, kc=kc: e.transpose(out=pv[:, kc, :], in_=attn.ap[:, kc * 128:(kc + 1) * 128],
                                                            identity=ident_b), reads=[attn, cm_b], writes=pb)
                aT = attT_ring[i % 2]
                P.op("act", lambda e: e.copy(out=aT.ap, in_=pv), reads=pb, writes=[aT])
                pfree(pt)
                store(at_s[n].rearrange("p (a b) -> p a b", a=4), at_b[n], aT)
                yield

            def gen_A(u):
                hp = pair_ring[u % 4]
                kT = kT_ring[u % 2]
                qT = qT_ring[u % 2]
                vaug = vaug_ring[u % 2]
                gt = g_ring[u % 2]
                for h in range(4):
                    pt, pb = feat_proj(KM + h * 128, hp, 258)
                    conv_silu(pt, pb, 4 + h, kT.ap[:, h, :], kT)
                    yield
                if outs:
                    for h in range(4):
                        pt, pb = feat_proj(QM + h * 128, hp, 258)
                        conv_silu(pt, pb, h, qT.ap[:, h, :], qT)
                        yield
                ptg, pbg = ps1()
                gv = ptg.ap[:, 0:16].rearrange("p (c g) -> p c g", c=2)
                for ci in range(2):
                    lhs = hp.ap[:, :, 1 + ci * 128:1 + (ci + 1) * 128]
                    pt, pb = tok_proj(VM, 512, lhs, hp)
                    P.op("act", lambda e, ci=ci, pt=pt: e.copy(out=vaug.ap[:, ci, :, 0:128],
                                                             in_=pt.ap.rearrange("p (h d) -> p h d", h=4)),
                         reads=pb, writes=[vaug])
                    pfree(pt)
                    for kc in range(8):
                        P.op("pe", lambda e, kc=kc, ci=ci, lhs=lhs: e.matmul(gv[:, ci, :], lhsT=lhs[:, kc, :],
                                                                           rhs=Wg.ap[:, kc, pi * 8:pi * 8 + 8],
                                                                           start=(kc == 0), stop=(kc == 7)),
                             reads=[Wg, hp], writes=pbg)
                    yield
                    if full:
                        pt, pb = tok_proj(OM, 512, lhs, hp)
                        c = 2 * u + ci
                        tho = tho_ring[c % 2]
                        P.op("act", lambda e, pt=pt, tho=tho: e.activation(out=tho.ap, in_=pt.ap, func=AF.Tanh, scale=0.5),
                             reads=pb, writes=[tho], tag="T")
                        pfree(pt)
                        store(th_s[c], th_b[c], tho)
                        yield
                gi, gf = gt.ap[:, 0, :, :], gt.ap[:, 1, :, :]
                bgi = bc(bg_t.ap[:, pi, 0:4].unsqueeze(1), [128, 2, 4])
                bgf = bc(bg_t.ap[:, pi, 4:8].unsqueeze(1), [128, 2, 4])
                P.op("dve", lambda e: e.tensor_tensor(out=gi, in0=gv[:, :, 0:4], in1=bgi, op=ALU.add),
                     reads=pbg + [bg_t], writes=[gt])
                P.op("dve", lambda e: e.tensor_tensor(out=gf, in0=gv[:, :, 4:8], in1=bgf, op=ALU.add),
                     reads=pbg + [bg_t], writes=[gt])
                pfree(ptg)
                yield
                if full:
                    qa = qa_ring[u % 2]
                    for hp4 in range(4):
                        pt, pb = feat_proj(QA + hp4 * 128, hp, 256, off=1)
                        P.op("dve", lambda e, hp4=hp4, pt=pt: e.tensor_copy(out=qa.ap[0:64, 0, hp4, :], in_=pt.ap[0:64, 0:256]),
                             reads=pb, writes=[qa])
                        P.op("dve", lambda e, hp4=hp4, pt=pt: e.tensor_copy(out=qa.ap[64:128, 1, hp4, :], in_=pt.ap[64:128, 0:256]),
                             reads=pb, writes=[qa])
                        pfree(pt)
                        yield
                    for ci in range(2):
                        ka_va_proj(2 * u + 1 + ci, hp.ap[:, :, 1 + ci * 128:1 + (ci + 1) * 128], hp)
                        yield

            def gen_B(u):
                kT = kT_ring[u % 2]
                qT = qT_ring[u % 2]
                vaug = vaug_ring[u % 2]
                gt = g_ring[u % 2]
                wv = wv_ring[0]
                ktok = ktok_ring[0]
                gi, gf = gt.ap[:, 0, :, :], gt.ap[:, 1, :, :]
                P.op("act", lambda e: e.activation(out=gt.ap[:, 2, :, :], in_=gf, func=AF.Exp, scale=-1.0),
                     reads=[gt], writes=[gt])
                P.op("act", lambda e: e.activation(out=gf, in_=gt.ap[:, 2, :, :], func=AF.Ln, bias=1.0),
                     reads=[gt], writes=[gt], tag="L")
                nlf2 = gt.ap[:, 1, :, :].rearrange("p c h -> p (c h)")
                ptc, pbc = ps1()
                P.op("pe", lambda e: e.matmul(ptc.ap[:, 0:8], lhsT=U_f, rhs=nlf2, start=True, stop=True),
                     reads=[cm_f, gt], writes=pbc)
                P.op("pe", lambda e: e.matmul(ptc.ap[:, 8:16], lhsT=ones_f.ap, rhs=nlf2, start=True, stop=True),
                     reads=[ones_f, gt], writes=pbc)
                Pn = ptc.ap[:, 0:8].rearrange("p (c h) -> p c h", c=2)
                Tn = ptc.ap[:, 8:16].rearrange("p (c h) -> p c h", c=2)
                biasE = gt.ap[:, 2, :, :]
                P.op("dve", lambda e: e.tensor_tensor(out=biasE, in0=Pn, in1=gi, op=ALU.add), reads=pbc + [gt], writes=[gt])
                P.op("dve", lambda e: e.scalar_tensor_tensor(out=gt.ap[:, 3, :, :], in0=biasE, scalar=LN_HALF, in1=Tn,
                                                             op0=ALU.add, op1=ALU.subtract), reads=pbc + [gt], writes=[gt])
                wq = gt.ap[:, 4, :, :]
                ebq = gt.ap[:, 5, :, :]
                dcy = gt.ap[:, 6, :, :]
                lneb = gt.ap[:, 7, :, :]
                P.op("pool", lambda e: e.memset(lneb, LN_EB), writes=[gt])
                P.op("act", lambda e: e.activation(out=wq, in_=gt.ap[:, 3, :, :], func=AF.Exp), reads=[gt], writes=[gt])
                P.op("act", lambda e: e.activation(out=ebq.rearrange("p c h -> p (c h)"), in_=ptc.ap[:, 0:8], func=AF.Exp,
                                                   scale=-1.0, bias=lneb[:, 0, 0:1]), reads=pbc + [gt], writes=[gt])
                P.op("act", lambda e: e.activation(out=dcy.rearrange("p c h -> p (c h)"), in_=ptc.ap[:, 8:16], func=AF.Exp,
                                                   scale=-1.0), reads=pbc, writes=[gt])
                pfree(ptc)
                nlfB = nlfB_ring[0]
                if outs:
                    P.op("pool", lambda e: e.tensor_copy(out=nlfB.ap, in_=bc(nlf2.unsqueeze(2), [128, 8, 128])),
                         reads=[gt], writes=[nlfB])
                yield
                for ci in range(2):
                    pt, pb = ps1()
                    pv = pt.ap.bitcast(BF16)[:, 0:512].rearrange("p (a b) -> p a b", a=4)
                    for h in range(4):
                        P.op("pe", lambda e, h=h, ci=ci, pv=pv: e.transpose(out=pv[:, h, :], in_=kT.ap[:, h, ci * 128:(ci + 1) * 128],
                                                                          identity=ident_b), reads=[kT, cm_b], writes=pb)
                    P.op("act", lambda e, ci=ci, pv=pv: e.copy(out=ktok.ap[:, ci, :], in_=pv.rearrange("p a b -> p (a b)")),
                         reads=pb, writes=[ktok])
                    pfree(pt)
                    P.op("pool", lambda e, ci=ci: e.tensor_tensor(out=wv.ap[:, ci, :, :], in0=vaug.ap[:, ci, :, :],
                                                                in1=bc(wq[:, ci, :].unsqueeze(2), [128, 4, 129]), op=ALU.mult),
                         reads=[vaug, gt], writes=[wv])
                    yield
                for ci in range(2):
                    c = 2 * u + ci
                    k = c
                    cs = slice(ci * 128, (ci + 1) * 128)
                    if outs:
                        ptr, pbr = ps1()
                        rv = ptr.ap.rearrange("p (h t) -> p h t", h=4)
                        for h in range(4):
                            P.op("pe", lambda e, h=h, ci=ci, rv=rv: e.matmul(rv[:, h, :], lhsT=nlfB.ap[:, ci * 4 + h, :], rhs=negU_f,
                                                                           start=True, stop=True), reads=[nlfB, cm_f], writes=pbr)
                        E = E_ring[0]
                        for h in range(4):
                            P.op("act", lambda e, h=h, ci=ci, rv=rv, E=E: e.activation(out=E.ap[:, h, :], in_=rv[:, h, :], func=AF.Exp,
                                                                                    bias=biasE[:, ci, h:h + 1]),
                                 reads=pbr + [gt], writes=[E])
                        pfree(ptr)
                        Em = Em_ring[k % 2]
                        P.op("pool", lambda e, E=E, Em=Em: e.tensor_tensor(out=Em.ap, in0=E.ap,
                                                                           in1=bc(maskf_f.unsqueeze(1), [128, 4, 128]), op=ALU.mult),
                             reads=[E, cm_f], writes=[Em])
                        yield
                        pts, pbs = ps1()
                        sv = pts.ap.rearrange("p (h t) -> p h t", h=4)
                        for h in range(4):
                            P.op("pe", lambda e, h=h, cs=cs, sv=sv: e.matmul(sv[:, h, :], lhsT=kT.ap[:, h, cs], rhs=qT.ap[:, h, cs],
                                                                           start=True, stop=True), reads=[kT, qT], writes=pbs)
                        PT = PT_ring[k % 2]
                        P.op("dve", lambda e, sv=sv, Em=Em, PT=PT: e.tensor_tensor(out=PT.ap, in0=sv, in1=Em.ap, op=ALU.mult),
                             reads=pbs + [Em], writes=[PT])
                        pfree(pts)
                        yield
                        ptb, pbb = ps2()
                        bv = ptb.ap.rearrange("p (h n) -> p h n", h=4)
                        for h in range(4):
                            P.op("pe", lambda e, h=h, cs=cs, bv=bv: e.matmul(bv[:, h, 0:129], lhsT=qT.ap[:, h, cs], rhs=Cbf.ap[:, h, :],
                                                                           start=True, stop=True), reads=[qT, Cbf], writes=pbb)
                        tB = tB_ring[0]
                        P.op("dve", lambda e, ci=ci, bv=bv, tB=tB: e.tensor_tensor(out=tB.ap, in0=bv[:, :, 0:129],
                                                                                 in1=bc(ebq[:, ci, :].unsqueeze(2), [128, 4, 129]),
                                                                                 op=ALU.mult), reads=pbb + [gt], writes=[tB])
                        pfree(ptb)
                        pta, pba = ps2()
                        av = pta.ap.rearrange("p (h n) -> p h n", h=4)
                        for h in range(4):
                            P.op("pe", lambda e, h=h, ci=ci, av=av, PT=PT: e.matmul(av[:, h, 0:129], lhsT=PT.ap[:, h, :],
                                                                                  rhs=vaug.ap[:, ci, h, :], start=True, stop=True),
                                 reads=[PT, vaug], writes=pba)
                        R = R_ring[0]
                        P.op("dve", lambda e, av=av, tB=tB, R=R: e.tensor_tensor(out=R.ap, in0=av[:, :, 0:129], in1=tB.ap, op=ALU.add),
                             reads=pba + [tB], writes=[R])
                        pfree(pta)
                        yield
                    ptk, pbk = ps2()
                    kv = ptk.ap.rearrange("p (h n) -> p h n", h=4)
                    for h in range(4):
                        P.op("pe", lambda e, h=h, ci=ci, kv=kv: e.matmul(kv[:, h, 0:129], lhsT=ktok.ap[:, ci, h * 128:(h + 1) * 128],
                                                                       rhs=wv.ap[:, ci, h, :], start=True, stop=True),
                             reads=[ktok, wv], writes=pbk)
                    P.op("pool", lambda e, ci=ci: e.tensor_tensor(out=Cst.ap, in0=Cst.ap, in1=bc(dcy[:, ci, :].unsqueeze(2), [128, 4, 129]),
                                                                 op=ALU.mult), reads=[Cst, gt], writes=[Cst], c=1.0)
                    P.op("dve", lambda e, kv=kv: e.tensor_tensor(out=Cst.ap, in0=kv[:, :, 0:129], in1=Cst.ap, op=ALU.add),
                         reads=pbk + [Cst], writes=[Cst])
                    pfree(ptk)
                    if outs:
                        P.op("act", lambda e: e.copy(out=Cbf.ap, in_=Cst.ap), reads=[Cst], writes=[Cbf])
                        dn = dn_ring[k % 2]
                        P.op("act", lambda e, R=R, dn=dn: e.activation(out=dn.ap[:, 0:4], in_=R.ap[:, :, 128], func=AF.Abs),
                             reads=[R], writes=[dn])
                        P.op("dve", lambda e, dn=dn: e.tensor_scalar(out=dn.ap[:, 0:4], in0=dn.ap[:, 0:4], scalar1=1.0, scalar2=None,
                                                                     op0=ALU.max), reads=[dn], writes=[dn])
                        P.op("dve", lambda e, dn=dn: e.reciprocal(out=dn.ap[:, 0:4], in_=dn.ap[:, 0:4]), reads=[dn], writes=[dn])
                        ho = ho_ring[k % 2]
                        P.op("dve", lambda e, R=R, dn=dn, ho=ho: e.tensor_tensor(out=ho.ap, in0=R.ap[:, :, 0:128],
                                                                               in1=bc(dn.ap[:, 0:4].unsqueeze(2), [128, 4, 128]),
                                                                               op=ALU.mult), reads=[R, dn], writes=[ho])
                        store(sc_dst[c].rearrange("p (h d) -> p h d", h=4), sc_buf[c], ho)
                    yield

            def gen_C(u):
                if u >= 1:
                    yield from attention_block(2 * u - 1)
                yield from attention_block(2 * u)

            def gen_T(e_idx):
                flag_ap = None
                if e_idx == 0:
                    flag_ap = fl.ap[:, 9 + pi:10 + pi]
                elif e_idx == 17:
                    flag_ap = fl.ap[:, 14 + pi:15 + pi]
                pv, pb, ptt = make_hmix_tile(xp[pi, e_idx * 128:(e_idx + 1) * 128, :], [], A1_b, B1_b, flag_ap)
                yield
                if 1 <= e_idx <= 16:
                    u, half = (e_idx - 1) // 2, (e_idx - 1) % 2
                    dstt = pair_ring[u % 4]
                    P.op("act", lambda e, dstt=dstt, half=half, pv=pv: e.copy(out=dstt.ap[:, :, 1 + half * 128:1 + (half + 1) * 128], in_=pv),
                         reads=pb, writes=[dstt])
                    src_t, src_first, src_last = dstt, dstt.ap[:, :, 1 + half * 128:2 + half * 128], dstt.ap[:, :, 128 + half * 128:129 + half * 128]
                else:
                    hh = halo_h[0 if e_idx == 0 else 1]
                    P.op("act", lambda e, hh=hh, pv=pv: e.copy(out=hh.ap, in_=pv), reads=pb, writes=[hh])
                    src_t, src_first, src_last = hh, hh.ap[:, :, 0:1], hh.ap[:, :, 127:128]
                pfree(ptt)
                if e_idx % 2 == 1 and e_idx >= 3:
                    dstt = pair_ring[((e_idx - 3) // 2) % 4]
                    P.op("pool", lambda e, dstt=dstt, src_first=src_first: e.tensor_copy(out=dstt.ap[:, :, 257:258], in_=src_first),
                         reads=[src_t], writes=[dstt], c=0.2)
                if e_idx % 2 == 0 and e_idx <= 14:
                    dstt = pair_ring[(e_idx // 2) % 4]
                    P.op("pool", lambda e, dstt=dstt, src_last=src_last: e.tensor_copy(out=dstt.ap[:, :, 0:1], in_=src_last),
                         reads=[src_t], writes=[dstt], c=0.2)
                if full and e_idx in (0, 17):
                    ka_va_proj(e_idx, hh.ap, hh)
                    fa = fl.ap[:, 9 + pi:10 + pi] if e_idx == 0 else fl.ap[:, 14 + pi:15 + pi]
                    P.op("pool", lambda e, e_idx=e_idx, fa=fa: e.tensor_scalar(out=vaa.ap[:, e_idx, :, :], in0=vaa.ap[:, e_idx, :, :],
                                                                             scalar1=fa, scalar2=None, op0=ALU.mult),
                         reads=[vaa_b[e_idx], fl], writes=[vaa_b[e_idx]])
                yield

            def chain(*gens):
                for g_ in gens:
                    yield from g_

            def interleave(gens):
                gens = [g_ for g_ in gens if g_ is not None]
                while gens:
                    for g_ in list(gens):
                        try:
                            next(g_)
                        except StopIteration:
                            gens.remove(g_)

            return {"T": gen_T, "A": gen_A, "B": gen_B, "C": gen_C, "att": attention_block, "full": full}

        def chain(*gens):
            for g_ in gens:
                yield from g_

        def interleave(gens):
            gens = [g_ for g_ in gens if g_ is not None]
            while gens:
                for g_ in list(gens):
                    try:
                        next(g_)
                    except StopIteration:
                        gens.remove(g_)

        modes = ["slot", "slot", "slot", "F", "B"]
        objs = [run_pass(pi_, modes[pi_]) for pi_ in range(5)]

        def pre_state(p):
            if p < 3:
                P.op("dve", lambda e: e.tensor_scalar(out=Cst.ap, in0=Cst.ap, scalar1=fl.ap[:, p:p + 1], scalar2=None, op0=ALU.mult),
                     reads=[Cst, fl], writes=[Cst])
            else:
                Csrc = Cf if p == 3 else Cb
                P.op("dve", lambda e: e.tensor_copy(out=Cst.ap, in_=Csrc.ap), reads=[Csrc], writes=[Cst])
                P.op("act", lambda e: e.copy(out=Cbf.ap, in_=Cst.ap), reads=[Cst], writes=[Cbf])

        def post_state(p):
            if p < 3:
                P.op("dve", lambda e: e.scalar_tensor_tensor(out=Cf.ap, in0=Cst.ap, scalar=fl.ap[:, 3 + p:4 + p], in1=Cf.ap,
                                                             op0=ALU.mult, op1=ALU.add), reads=[Cst, fl, Cf], writes=[Cf])
                P.op("dve", lambda e: e.scalar_tensor_tensor(out=Cb.ap, in0=Cst.ap, scalar=fl.ap[:, 6 + p:7 + p], in1=Cb.ap,
                                                             op0=ALU.mult, op1=ALU.add), reads=[Cst, fl, Cb], writes=[Cb])

        def gT(p, es):
            return chain(*[objs[p]["T"](e_) for e_ in es if e_ <= 17])

        interleave([gT(0, range(5))])
        for p in range(5):
            o = objs[p]
            for k_ in range(8):
                streams = []
                if k_ >= 1:
                    streams.append(o["B"](k_ - 1))
                    if o["full"]:
                        streams.append(o["C"](k_ - 1))
                elif p >= 1:
                    prev = objs[p - 1]
                    streams.append(prev["B"](7))
                    if prev["full"]:
                        streams.append(chain(prev["C"](7), prev["att"](15)))
                streams.append(o["A"](k_))
                if k_ < 7:
                    streams.append(gT(p, (2 * k_ + 5, 2 * k_ + 6)))
                elif p < 4:
                    streams.append(gT(p + 1, range(5)))
                interleave(streams)
                if k_ == 0:
                    if p >= 1:
                        post_state(p - 1)
                    pre_state(p)
        interleave([objs[4]["B"](7)])

        A.release(pass_mark)
        P.fence()
        Wfi = A.tile([128, 8, 2 * DFF], BF16, "wfi_bf")
        Wfo = A.tile([128, NFC, D], BF16, "wfo_bf")
        p2_mark = A.mark()
        wout_b = A.tile([128, 8, D], BF16, "wout_bf")
        stage[:] = [A.tile([128, 704], F32, "stageB%d" % i) for i in range(4)]
        gate1_p = A.tile([128, D], F32, "gate1p")
        gate2_p = A.tile([128, D], F32, "gate2p")
        load(gate1_p, mod_s[0], rd=[mod_sb[0]])
        load(gate2_p, mod_s[1], rd=[mod_sb[1]])
        wov = w_out.rearrange("(kc p) n -> p kc n", p=128)
        for kc in range(8):
            for c0 in (0, 512):
                cast_block(wov[:, kc, c0:c0 + 512], 512, wout_b.ap[:, kc, c0:c0 + 512], wout_b, mode="rowcol",
                           arg=(rs_t.ap[:, kc:kc + 1], gate1_p.ap[:, c0:c0 + 512], rs_t, gate1_p))
        join(wout_b)
        hf_ring = [A.tile([128, 4, 128], F32, "hfl%d" % i) for i in range(2)]
        hb_ring = [A.tile([128, 512], F32, "hbl%d" % i) for i in range(2)]
        th_ring = [A.tile([128, 512], F32, "thl%d" % i) for i in range(2)]
        x_ring = [A.tile([128, D], F32, "xl%d" % i) for i in range(2)]
        mixT_ring = [A.tile([128, 8, 128], BF16, "mixT%d" % i) for i in range(2)]
        hsum_ring = [A.tile([128, 4, 128], F32, "hsum%d" % i) for i in range(2)]
        hm2_ring = [A.tile([128, 512], BF16, "hm2_%d" % i) for i in range(2)]
        st2_ring = [A.tile([128, 4], F32, "st2_%d" % i) for i in range(2)]
        junk2 = A.tile([128, 128], BF16, "junk2")
        wfiv = w_fi.rearrange("(kc p) n -> p kc n", p=128)
        wfov = w_fo.rearrange("(f p) n -> p f n", p=128)
        wjobs = []
        for kc in range(8):
            for c0 in range(0, 2 * DFF, 704):
                wjobs.append(("fi", kc, c0))
        for f in range(NFC):
            for c0 in (0, 512):
                wjobs.append(("fo", f, c0))

        def do_wjobs(n):
            for _ in range(n):
                if not wjobs:
                    return
                kind, a, c0 = wjobs.pop(0)
                if kind == "fi":
                    cast_block(wfiv[:, a, c0:c0 + 704], 704, Wfi.ap[:, a, c0:c0 + 704], Wfi, engs=["pool", "act", "dve", "act"], q="act")
                else:
                    cast_block(wfov[:, a, c0:c0 + 512], 512, Wfo.ap[:, a, c0:c0 + 512], Wfo, mode="colscale",
                               arg=(gate2_p.ap[:, c0:c0 + 512], gate2_p), engs=["pool", "dve"], q="act")

        def p2_loads(c):
            load(hf_ring[c % 2], hf_s[c].rearrange("p (h d) -> p h d", h=4), rd=[hf_b[c]])
            load(hb_ring[c % 2], hb_s[NT - 1 - c], rd=[hb_b[NT - 1 - c]])
            load(th_ring[c % 2], th_s[c], rd=[th_b[c]])
            load(x_ring[c % 2], xp[3, 128 + c * 128:256 + c * 128, :])
            mt = mixT_ring[c % 2]
            P.op("sp", lambda e: e.dma_start(out=mt.ap[:, 0:4, :], in_=at_s[c].rearrange("p (a b) -> p a b", a=4)),
                 reads=[at_b[c]], writes=[mt], dma=True)

        p2_loads(0)
        for c in range(NT):
            if c + 1 < NT:
                p2_loads(c + 1)
            do_wjobs(7)
            hfl, hbl, thl, xl, mt = hf_ring[c % 2], hb_ring[c % 2], th_ring[c % 2], x_ring[c % 2], mixT_ring[c % 2]
            pt, pb = ps1()
            P.op("pe", lambda e, pt=pt, hbl=hbl: e.matmul(pt.ap, lhsT=J_f, rhs=hbl.ap, start=True, stop=True),
                 reads=[cm_f, hbl], writes=pb)
            hs = hsum_ring[c % 2]
            P.op("dve", lambda e, pt=pt, hfl=hfl, hs=hs: e.tensor_tensor(out=hs.ap, in0=pt.ap.rearrange("p (h d) -> p h d", h=4),
                                                                       in1=hfl.ap, op=ALU.add), reads=pb + [hfl], writes=[hs])
            pfree(pt)
            st = st2_ring[c % 2]
            for h in range(4):
                P.op("act", lambda e, h=h, hs=hs, st=st: e.activation(out=junk2.ap, in_=hs.ap[:, h, :], func=AF.Square,
                                                                   accum_out=st.ap[:, h:h + 1]), reads=[hs], writes=[junk2, st])
            rstd_from_ssq(st.ap[:, 0:4], st, 4, 128)
            P.op("dve", lambda e, st=st: e.tensor_scalar(out=st.ap[:, 0:4], in0=st.ap[:, 0:4], scalar1=0.5, scalar2=None, op0=ALU.mult),
                 reads=[st], writes=[st])
            P.op("dve", lambda e, hs=hs, st=st: e.tensor_tensor(out=hs.ap, in0=hs.ap, in1=bc(st.ap[:, 0:4].unsqueeze(2), [128, 4, 128]),
                                                              op=ALU.mult), reads=[hs, st], writes=[hs])
            hm2 = hm2_ring[c % 2]
            P.op("dve", lambda e, hs=hs, thl=thl, hm2=hm2: e.scalar_tensor_tensor(out=hm2.ap, in0=thl.ap, scalar=1.0,
                                                                                in1=hs.ap.rearrange("p h d -> p (h d)"),
                                                                                op0=ALU.add, op1=ALU.mult), reads=[hs, thl], writes=[hm2])
            pt, pb = ps1()
            pv = pt.ap.bitcast(BF16)[:, 0:512].rearrange("p (a b) -> p a b", a=4)
            for kc in range(4):
                P.op("pe", lambda e, kc=kc, pv=pv, hm2=hm2: e.transpose(out=pv[:, kc, :], in_=hm2.ap[:, kc * 128:(kc + 1) * 128],
                                                                      identity=ident_b), reads=[hm2, cm_b], writes=pb)
            P.op("act", lambda e, pv=pv, mt=mt: e.copy(out=mt.ap[:, 4:8, :], in_=pv), reads=pb, writes=[mt])
            pfree(pt)
            x1o = xl
            for half in range(2):
                pt, pb = ps1()
                for kc in range(8):
                    P.op("pe", lambda e, kc=kc, pt=pt, mt=mt, half=half: e.matmul(pt.ap, lhsT=mt.ap[:, kc, :],
                                                                                rhs=wout_b.ap[:, kc, half * 512:(half + 1) * 512],
                                                                                start=(kc == 0), stop=(kc == 7)),
                         reads=[mt, wout_b], writes=pb)
                P.op("dve", lambda e, pt=pt, half=half, xl=xl, x1o=x1o: e.tensor_tensor(out=x1o.ap[:, half * 512:(half + 1) * 512], in0=pt.ap,
                                                                                      in1=xl.ap[:, half * 512:(half + 1) * 512], op=ALU.add),
                     reads=pb + [xl], writes=[x1o])
                pfree(pt)
            store(x1_s[c], x1_b[c], x1o)
        do_wjobs(1000)
        join(Wfi)
        join(Wfo)

        A.release(p2_mark)
        P.fence()
        gfin_t = A.tile([128, D], F32, "gfin")
        load(gfin_t, gfin_b)
        A2_p = A.tile([128, D], F32, "A2p")
        B2_p = A.tile([128, D], F32, "B2p")
        load(A2_p, mod_s[2], rd=[mod_sb[2]])
        load(B2_p, mod_s[3], rd=[mod_sb[3]])
        x1_ring = [A.tile([128, D], F32, "x1l%d" % i) for i in range(4)]
        junk3 = A.tile([128, D], BF16, "junk3")
        t1_ring[:] = [A.tile([128, D], F32, "t1b%d" % i) for i in range(1)]
        hm_ring[:] = [A.tile([128, D], BF16, "hmb%d" % i) for i in range(2)]
        st_ring[:] = [A.tile([128, 4], F32, "statb%d" % i) for i in range(4)]
        hffT_ring = [A.tile([128, 8, 256], BF16, "hffT%d" % i) for i in range(2)]
        gu_ring = [A.tile([128, NFC, 256], BF16, "gu%d" % i) for i in range(1)]
        sg_ring = [A.tile([128, 256], F32, "sg%d" % i) for i in range(3)]
        x2_ring = [A.tile([128, D], F32, "x2_%d" % i) for i in range(1)]
        oo_ring = [A.tile([128, D], F32, "oo%d" % i) for i in range(1)]
        st3_ring = [A.tile([128, 4], F32, "st3_%d" % i) for i in range(2)]
        junk_holder = [junk3]

        def ffn_loads(gi):
            for t in range(2):
                c = 2 * gi + t
                load(x1_ring[c % 4], x1_s[c], rd=[x1_b[c]])

        def ffn_group(gi):
            hffT = hffT_ring[gi % 2]
            xts = []
            if gi + 1 < NT // 2:
                ffn_loads(gi + 1)
            for t in range(2):
                c = 2 * gi + t
                i = cnt["tile"]
                xt = x1_ring[c % 4]
                cnt["tile"] += 1
                st = st_ring[i % 4]
                P.op("act", lambda e, xt=xt, st=st: e.activation(out=junk_holder[0].ap, in_=xt.ap, func=AF.Square, accum_out=st.ap[:, 0:1]),
                     reads=[xt], writes=[junk_holder[0], st])
                rstd_from_ssq(st.ap[:, 0:1], st, 1, D)
                t1 = t1_ring[0]
                P.op("dve", lambda e, xt=xt, st=st, t1=t1: e.scalar_tensor_tensor(out=t1.ap, in0=xt.ap, scalar=st.ap[:, 0:1], in1=A2_p.ap,
                                                                                op0=ALU.mult, op1=ALU.mult), reads=[xt, st, A2_p], writes=[t1])
                hm = hm_ring[i % 2]
                P.op("pool", lambda e, t1=t1, hm=hm: e.tensor_tensor(out=hm.ap, in0=t1.ap, in1=B2_p.ap, op=ALU.add),
                     reads=[t1, B2_p], writes=[hm])
                pt, pb = ps1()
                pv = pt.ap.bitcast(BF16).rearrange("p (a b) -> p a b", a=8)
                for kc in range(8):
                    P.op("pe", lambda e, kc=kc, pv=pv, hm=hm: e.transpose(out=pv[:, kc, :], in_=hm.ap[:, kc * 128:(kc + 1) * 128],
                                                                        identity=ident_b), reads=[hm, cm_b], writes=pb)
                P.op("act", lambda e, pv=pv, t=t: e.copy(out=hffT.ap[:, :, t * 128:(t + 1) * 128], in_=pv), reads=pb, writes=[hffT])
                pfree(pt)
                xts.append(xt)
            gu = gu_ring[0]
            for f in range(NFC):
                pt, pb = ps1()
                for half in range(2):
                    col = half * DFF + f * 128
                    for kc in range(8):
                        P.op("pe", lambda e, kc=kc, pt=pt, half=half, col=col: e.matmul(pt.ap[:, half * 256:(half + 1) * 256],
                                                                                      lhsT=Wfi.ap[:, kc, col:col + 128], rhs=hffT.ap[:, kc, :],
                                                                                      start=(kc == 0), stop=(kc == 7)),
                             reads=[Wfi, hffT], writes=pb)
                sg = sg_ring[f % 3]
                P.op("act", lambda e, pt=pt, sg=sg: e.activation(out=sg.ap, in_=pt.ap[:, 0:256], func=AF.Silu), reads=pb, writes=[sg])
                P.op("dve", lambda e, pt=pt, sg=sg, f=f: e.tensor_tensor(out=gu.ap[:, f, :], in0=pt.ap[:, 256:512], in1=sg.ap, op=ALU.mult),
                     reads=pb + [sg], writes=[gu])
                pfree(pt)
            for t in range(2):
                c = 2 * gi + t
                xt = xts[t]
                x2 = x2_ring[0]
                for half in range(2):
                    pt, pb = ps1()
                    for f in range(NFC):
                        P.op("pe", lambda e, f=f, pt=pt, half=half, t=t: e.matmul(pt.ap, lhsT=gu.ap[:, f, t * 128:(t + 1) * 128],
                                                                                rhs=Wfo.ap[:, f, half * 512:(half + 1) * 512],
                                                                                start=(f == 0), stop=(f == NFC - 1)),
                             reads=[gu, Wfo], writes=pb)
                    P.op("dve", lambda e, pt=pt, half=half, xt=xt, x2=x2: e.tensor_tensor(out=x2.ap[:, half * 512:(half + 1) * 512], in0=pt.ap,
                                                                                        in1=xt.ap[:, half * 512:(half + 1) * 512], op=ALU.add),
                         reads=pb + [xt], writes=[x2])
                    pfree(pt)
                st = st3_ring[c % 2]
                oo = oo_ring[0]
                P.op("act", lambda e, x2=x2, st=st, oo=oo: e.activation(out=oo.ap, in_=x2.ap, func=AF.Square, accum_out=st.ap[:, 0:1]),
                     reads=[x2], writes=[oo, st])
                rstd_from_ssq(st.ap[:, 0:1], st, 1, D)
                P.op("dve", lambda e, x2=x2, st=st, oo=oo: e.scalar_tensor_tensor(out=oo.ap, in0=x2.ap, scalar=st.ap[:, 0:1], in1=gfin_t.ap,
                                                                                op0=ALU.mult, op1=ALU.mult), reads=[x2, st, gfin_t], writes=[oo])
                store(out[c * 128:(c + 1) * 128, :], out_b[c], oo)

        ffn_loads(0)
        for gi in range(NT // 2):
            ffn_group(gi)
        P.op("sp", None, reads=out_b)

    except StopBuild:
        P.op("sp", None, reads=[dbg_b])
    with contextlib.ExitStack() as stack:
        P.emit(stack)
    return nc


def _consts():
    s = np.arange(128)[:, None]
    t = np.arange(128)[None, :]
    ident = (s == t).astype(np.float32)
    J = (s + t == 127).astype(np.float32)
    U = (s <= t).astype(np.float32)
    negU = -U
    maskf = U * np.float32(0.25 / math.sqrt(128.0))
    cmat = np.concatenate([ident, J, U, negU, maskf], axis=1).astype(np.float32)
    slopes = (2.0 ** (-8.0 * (np.arange(8, dtype=np.float32) + 1.0) / 8.0)).astype(np.float32)
    eb = np.zeros((128, 3, 8, 128), np.float32)
    for kb in range(3):
        kpos = (kb - 1) * 128 + np.arange(128)[:, None]
        qpos = np.arange(128)[None, :]
        dist = np.abs(kpos - qpos).astype(np.float32)
        valid = dist <= 128
        for h in range(8):
            eb[:, kb, h, :] = np.where(valid, np.exp(-slopes[h] * dist), 0.0)
    return cmat, eb.reshape(128, -1)


def _rep(v):
    return np.ascontiguousarray(np.broadcast_to(np.asarray(v, np.float32).reshape(1, -1), (128, v.size)))


def _prep_inputs(x, c, w_mod, b_mod, g_norm1, w_in, conv_w, conv_b, b_gates, sink, g_attn_out, g_mlstm_out,
                 w_out, g_norm2, w_ffn_in, w_ffn_out, g_final):
    f32 = np.float32
    x = np.asarray(x, f32)
    w_in0 = np.asarray(w_in, f32)[0]
    cmat, ebt = _consts()
    ka = w_in0[:, 512:640].reshape(D, 2, 64)
    kdup = np.concatenate([ka, ka], axis=2).reshape(D, 256)
    w_main = np.ascontiguousarray(np.concatenate([w_in0[:, 0:512], kdup, w_in0[:, 640:768], w_in0[:, 768:2816]], axis=1))
    gcols = w_in0[:, 2816:2832]
    bgv = np.asarray(b_gates, f32)[0]
    cwv = np.asarray(conv_w, f32)[0]
    cbv = np.asarray(conv_b, f32)[0]
    shared = {
        "w_mod": np.ascontiguousarray(np.asarray(w_mod, f32)[0]),
        "bmod_b": _rep(np.asarray(b_mod, f32)[0]),
        "g1_b": _rep(np.asarray(g_norm1, f32)[0]),
        "g2_b": _rep(np.asarray(g_norm2, f32)[0]),
        "gfin_b": _rep(np.asarray(g_final, f32)),
        "w_main": w_main,
        "cb": np.ascontiguousarray(cbv.reshape(8, 128).T),
        "sink_b": _rep(np.asarray(sink, f32)[0]),
        "rowsc": None,
        "w_out": np.ascontiguousarray(np.asarray(w_out, f32)[0]),
        "w_fi": np.ascontiguousarray(np.asarray(w_ffn_in, f32)[0]),
        "w_fo": np.ascontiguousarray(np.asarray(w_ffn_out, f32)[0]),
        "cmat": cmat,
        "ebt": ebt,
    }
    ga = np.asarray(g_attn_out, f32)[0].reshape(4, 128).T
    gm = np.asarray(g_mlstm_out, f32)[0].reshape(4, 128).T
    shared["rowsc"] = np.ascontiguousarray(np.concatenate([ga, gm], axis=1))
    in_maps = []
    for r in range(8):
        b, j = r // 4, r % 4
        xs = x[b]

        def ext(q, flip):
            lo, hi = q * 2048 - 128, q * 2048 + 2176
            buf = np.zeros((2304, D), f32)
            a, z = max(lo, 0), min(hi, SEQ)
            buf[a - lo:z - lo] = xs[a:z]
            return buf[::-1] if flip else buf

        passes = [(q, False) for q in range(j)] + [(q, True) for q in range(3, j, -1)] + [(j, False), (j, True)]
        xpa = np.stack([ext(q, fl_) for (q, fl_) in passes]).astype(f32)
        wg = np.zeros((D, 5, 8), f32)
        bg = np.zeros((5, 8), f32)
        cwa = np.zeros((128, 5, 8, 3), f32)
        flags = np.zeros((24,), f32)
        for pi, (q, fl_) in enumerate(passes):
            o = 8 if fl_ else 0
            wg[:, pi, :] = gcols[:, o:o + 8]
            bg[pi] = bgv[o:o + 8]
            taps = cwv[::-1] if fl_ else cwv
            cwa[:, pi, :, :] = taps.reshape(3, 8, 128).transpose(2, 1, 0)
            vl, vr = (q > 0), (q < 3)
            if fl_:
                vl, vr = vr, vl
            flags[9 + pi] = float(vl)
            flags[14 + pi] = float(vr)
        dirs = [p[1] for p in passes[:3]]
        for s in range(3):
            flags[s] = 1.0 if (s > 0 and dirs[s] == dirs[s - 1]) else 0.0
        lastf = max([s for s in range(3) if not dirs[s]], default=None)
        lastb = max([s for s in range(3) if dirs[s]], default=None)
        if lastf is not None:
            flags[3 + lastf] = 1.0
        if lastb is not None:
            flags[6 + lastb] = 1.0
        m = dict(shared)
        m["xp"] = np.ascontiguousarray(xpa)
        m["cvec"] = np.ascontiguousarray(np.asarray(c, f32)[b].reshape(8, 128).T)
        m["wg"] = np.ascontiguousarray(wg.reshape(D, 40))
        m["bg_b"] = _rep(bg.reshape(-1))
        m["cw"] = np.ascontiguousarray(cwa.reshape(128, -1))
        m["flags"] = _rep(flags)
        in_maps.append(m)
    return in_maps


_NC_CACHE = []


def kernel(**inputs):
    in_maps = _prep_inputs(**inputs)
    if not _NC_CACHE:
        _NC_CACHE.append(build_program())
    nc = _NC_CACHE[0]
    res = run_bass_kernel_spmd(nc, in_maps, core_ids=list(range(8)))
    outs = [np.asarray(r["out"], np.float32) for r in res.results]
    full = np.stack(outs).reshape(2, 4 * 2048, D)
    return full
```

```python
import contextlib
import math
import os
import numpy as np
import concourse.bass as bass
import concourse.mybir as mybir
from concourse.bass_utils import run_bass_kernel_spmd

F32 = mybir.dt.float32
BF16 = mybir.dt.bfloat16
AF = mybir.ActivationFunctionType
ALU = mybir.AluOpType

D = 1024
SEQ = 8192
NT = 16
EPS = 1e-6
DFF = 2816
NFC = DFF // 128
QA, KA, VA, QM, KM, VM, OM = 0, 512, 768, 896, 1408, 1920, 2432
NWM = 2944
LN_HALF = math.log(0.5)
LN_EB = math.log(0.5 / math.sqrt(128.0))


class Buf:
    __slots__ = ("name", "w", "r")

    def __init__(self, name):
        self.name = name
        self.w = None
        self.r = []


class Op:
    __slots__ = ("eng", "fn", "deps", "order", "sem", "val", "signal", "isdma", "idx", "seg", "cost", "fin", "done")


class Tile:
    def __init__(self, ap, name):
        self.ap = ap
        self.buf = Buf(name)


class Prog:
    ENGS = ["pe", "act", "dve", "pool", "sp"]
    SAME_RAW = ("act", "dve", "pool")
    EPOCH = 20000
    COST = {"pe": 0.12, "act": 0.5, "dve": 0.5, "pool": 1.0, "sp": 0.05}
    WINDOW = 160

    def __init__(self, nc):
        self.nc = nc
        self.all = []
        self.seg = 0
        self.ndma_sems = {"sp": 40, "pool": 4, "act": 8}

    def op(self, eng, fn, reads=(), writes=(), dma=False, c=None):
        o = Op()
        o.eng, o.fn, o.isdma, o.signal = eng, fn, dma, dma
        o.sem = o.val = None
        o.idx, o.seg = len(self.all), self.seg
        o.cost = (2.5 if dma else (0.0 if fn is None else self.COST[eng])) if c is None else c
        deps = {}
        for t in reads:
            b = t.buf if isinstance(t, Tile) else t
            if b.w is not None:
                deps[id(b.w)] = (b.w, True)
        for t in writes:
            b = t.buf if isinstance(t, Tile) else t
            if b.w is not None and id(b.w) not in deps:
                deps[id(b.w)] = (b.w, False)
            for r in b.r:
                if id(r) not in deps:
                    deps[id(r)] = (r, False)
        keep, order = [], []
        for d, raw in deps.values():
            if d is o:
                continue
            order.append(d)
            if d.isdma or dma or d.eng != eng:
                keep.append(d)
            elif eng in self.SAME_RAW:
                keep.append(d)
        o.deps, o.order = keep, order
        for d in keep:
            d.signal = True
        for t in writes:
            b = t.buf if isinstance(t, Tile) else t
            b.w = o
            b.r = []
        for t in reads:
            b = t.buf if isinstance(t, Tile) else t
            b.r.append(o)
        self.all.append(o)
        return o

    def fence(self):
        self.seg += 1

    def _schedule(self, ops):
        pend = {e: [o for o in ops if o.eng == e] for e in self.ENGS}
        head = {e: 0 for e in self.ENGS}
        free = {e: 0.0 for e in self.ENGS}
        out = {e: [] for e in self.ENGS}
        for o in ops:
            o.done = False
        left = len(ops)
        while left:
            best = None
            for e in self.ENGS:
                lst = pend[e]
                h = head[e]
                while h < len(lst) and lst[h].done:
                    h += 1
                head[e] = h
                n = 0
                j = h
                while j < len(lst) and n < self.WINDOW:
                    o = lst[j]
                    j += 1
                    if o.done:
                        continue
                    n += 1
                    rdy = 0.0
                    ok = True
                    for d in o.order:
                        if d.seg == o.seg and not d.done:
                            ok = False
                            break
                        if d.seg == o.seg and d.fin > rdy:
                            rdy = d.fin
                    if not ok:
                        continue
                    st = max(rdy, free[e])
                    if best is None or st < best[0] - 1e-9 or (abs(st - best[0]) <= 1e-9 and o.idx < best[1].idx):
                        best = (st, o)
                    if st <= free[e] + 1e-9:
                        break
            st, o = best
            o.done = True
            if o.isdma:
                o.fin = st + o.cost
                free[o.eng] = st + 0.05
            else:
                o.fin = st + o.cost
                free[o.eng] = o.fin
            out[o.eng].append(o)
            left -= 1
        return out

    def emit(self, stack):
        nc = self.nc
        sems = {}

        def getsem(key):
            if key not in sems:
                sems[key] = stack.enter_context(nc.semaphore("s_%s" % "_".join(str(k) for k in key)))
            return sems[key]

        nseg = self.seg + 1
        final = {e: [] for e in self.ENGS}
        dma_last = {q: [None] * n for q, n in self.ndma_sems.items()}
        dma_cnt = {q: [0] * n for q, n in self.ndma_sems.items()}
        dma_rr = {q: 0 for q in self.ndma_sems}
        last_compute = {e: None for e in self.ENGS}
        for sg in range(nseg):
            ops = [o for o in self.all if o.seg == sg]
            if sg > 0:
                lasts = [o for o in last_compute.values() if o is not None]
                dmas = [o for q in dma_last for o in dma_last[q] if o is not None]
                for e in self.ENGS:
                    f = Op()
                    f.eng, f.fn, f.isdma, f.signal = e, None, False, False
                    f.sem = f.val = None
                    f.deps = list(lasts) + dmas
                    for d in f.deps:
                        d.signal = True
                    final[e].append(f)
            sched = self._schedule(ops)
            for e in self.ENGS:
                for o in sched[e]:
                    if o.isdma:
                        k = dma_rr[e]
                        dma_rr[e] = (k + 1) % self.ndma_sems[e]
                        prev = dma_last[e][k]
                        if prev is not None:
                            o.deps = o.deps + [prev]
                        dma_last[e][k] = o
                        dma_cnt[e][k] += 1
                        o.sem = ("dma", e, k)
                        o.val = 16 * dma_cnt[e][k]
                    elif o.fn is not None:
                        last_compute[e] = o
                    final[e].append(o)
        self.ops = final
        for e in self.ENGS:
            n = 0
            for o in self.ops[e]:
                if o.isdma or not o.signal:
                    continue
                assert o.fn is not None
                o.sem = ("eng", e, n // self.EPOCH)
                o.val = n % self.EPOCH + 1
                n += 1
        for e in self.ENGS:
            for o in self.ops[e]:
                if o.signal:
                    getsem(o.sem)
                for d in o.deps:
                    assert d.sem is not None
        block = stack.enter_context(nc.Block())

        def run(e, engobj):
            waited = {}
            for o in self.ops[e]:
                for d in o.deps:
                    if waited.get(d.sem, 0) < d.val:
                        engobj.wait_ge(sems[d.sem], d.val)
                        waited[d.sem] = d.val
                if o.fn is None:
                    continue
                ins = o.fn(engobj)
                if o.signal:
                    ins.then_inc(sems[o.sem], 16 if o.isdma else 1)

        @block.tensor
        def _(eng):
            run("pe", eng)

        @block.scalar
        def _(eng):
            run("act", eng)

        @block.vector
        def _(eng):
            run("dve", eng)

        @block.gpsimd
        def _(eng):
            run("pool", eng)

        @block.sync
        def _(eng):
            run("sp", eng)


class StopBuild(Exception):
    pass


class Arena:
    def __init__(self, nc, nbytes):
        self.t = nc.alloc_sbuf_tensor("arena", [128, nbytes // 4], F32)
        self.ap = self.t.ap()
        self.size = nbytes
        self.top = 0
        self.n = 0

    def mark(self):
        return self.top

    def release(self, m):
        self.top = m

    def tile(self, shape, dtype, name=None):
        esz = 4 if dtype == F32 else 2
        free = int(np.prod(shape[1:]))
        nb = (free * esz + 31) // 32 * 32
        assert self.top + nb <= self.size, ("SBUF arena overflow", name, self.top, nb)
        a = self.ap[:, self.top // 4:(self.top + nb) // 4]
        if dtype != F32:
            a = a.bitcast(dtype)
        a = a[:, 0:free]
        if len(shape) == 3:
            a = a.rearrange("p (a b) -> p a b", a=shape[1])
        elif len(shape) == 4:
            a = a.rearrange("p (a b c) -> p a b c", a=shape[1], b=shape[2])
        if shape[0] != 128:
            a = a[0:shape[0]]
        self.top += nb
        self.n += 1
        return Tile(a, name or ("t%d" % self.n))


def bc(ap, shape):
    return ap.broadcast_to(shape)


def build_program():
    nc = bass.Bass("TRN2", target_bir_lowering=False)

    def din(name, shape, dt=F32):
        return nc.dram_tensor(name, list(shape), dt, kind="ExternalInput").ap()

    xp = din("xp", [5, 18 * 128, D])
    cvec = din("cvec", [128, 8])
    w_mod = din("w_mod", [D, 6 * D])
    bmod_b = din("bmod_b", [128, 6 * D])
    g1_b = din("g1_b", [128, D])
    g2_b = din("g2_b", [128, D])
    gfin_b = din("gfin_b", [128, D])
    w_main = din("w_main", [D, NWM])
    wg = din("wg", [D, 40])
    bg_b = din("bg_b", [128, 40])
    cw = din("cw", [128, 5 * 8 * 3])
    cb = din("cb", [128, 8])
    sink_b = din("sink_b", [128, 8])
    rowsc = din("rowsc", [128, 8])
    w_out = din("w_out", [D, D])
    w_fi = din("w_fi", [D, 2 * DFF])
    w_fo = din("w_fo", [DFF, D])
    flags = din("flags", [128, 24])
    cmat = din("cmat", [128, 5 * 128])
    ebt = din("ebt", [128, 3 * 8 * 128])
    out = nc.dram_tensor("out", [NT * 128, D], F32, kind="ExternalOutput").ap()

    hf_s = nc.dram_tensor("hf_s", [NT, 128, 512], F32).ap()
    hb_s = nc.dram_tensor("hb_s", [NT, 128, 512], F32).ap()
    th_s = nc.dram_tensor("th_s", [NT, 128, 512], F32).ap()
    at_s = nc.dram_tensor("at_s", [NT, 128, 512], BF16).ap()
    x1_s = nc.dram_tensor("x1_s", [NT, 128, D], F32).ap()
    hf_b = [Buf("hf%d" % i) for i in range(NT)]
    hb_b = [Buf("hb%d" % i) for i in range(NT)]
    th_b = [Buf("th%d" % i) for i in range(NT)]
    at_b = [Buf("at%d" % i) for i in range(NT)]
    x1_b = [Buf("x1%d" % i) for i in range(NT)]
    out_b = [Buf("out%d" % i) for i in range(NT)]

    P = Prog(nc)
    A = Arena(nc, 207 * 1024)
    KSTOP = os.environ.get("KSTOP", "")
    dbg = nc.dram_tensor("dbg", [128, 4096], F32, kind="ExternalOutput").ap() if KSTOP else None
    dbg_b = Buf("dbg")
    dbg_off = [0]

    def dump(tile_or_ap, rd, ncols):
        if not KSTOP:
            return
        o = dbg_off[0]
        dbg_off[0] += ncols
        src = tile_or_ap.ap if isinstance(tile_or_ap, Tile) else tile_or_ap
        P.op("sp", lambda e: e.dma_start(out=dbg[:, o:o + ncols], in_=src), reads=rd, writes=[dbg_b], dma=True)

    def stop(tag):
        if KSTOP == tag:
            raise StopBuild()

    psum_t = nc.alloc_psum_tensor("psum", [128, 4096], F32)
    psum_ap = psum_t.ap()
    banks = [Tile(psum_ap[:, k * 512:(k + 1) * 512], "bank%d" % k) for k in range(8)]
    bank2 = [Tile(psum_ap[:, k * 1024:(k + 1) * 1024], "bankpair%d" % k) for k in range(4)]
    for k in range(4):
        bank2[k].bufs = [banks[2 * k].buf, banks[2 * k + 1].buf]
    rr = [0]
    busy = [False] * 8

    def ps1():
        for _ in range(8):
            k = rr[0] % 8
            rr[0] += 1
            if not busy[k]:
                busy[k] = True
                banks[k].held = [k]
                return banks[k], [banks[k].buf]
        raise RuntimeError("no free PSUM bank")

    def ps2():
        for _ in range(8):
            if rr[0] % 2:
                rr[0] += 1
            k = (rr[0] % 8) // 2
            rr[0] += 2
            if not busy[2 * k] and not busy[2 * k + 1]:
                busy[2 * k] = busy[2 * k + 1] = True
                bank2[k].held = [2 * k, 2 * k + 1]
                return bank2[k], bank2[k].bufs
        raise RuntimeError("no free PSUM bank pair")

    def pfree(pt):
        for k in pt.held:
            assert busy[k]
            busy[k] = False

    def load(dst, src, wr=None, rd=(), q="sp"):
        P.op(q, lambda e: e.dma_start(out=dst.ap if isinstance(dst, Tile) else dst, in_=src),
             reads=list(rd), writes=[dst] if wr is None else wr, dma=True)

    def store(dst_ap, dst_buf, src, q="sp"):
        P.op(q, lambda e: e.dma_start(out=dst_ap, in_=src.ap if isinstance(src, Tile) else src),
             reads=[src], writes=[dst_buf], dma=True)

    try:
        cm_f = A.tile([128, 5, 128], F32, "cmat")
        load(cm_f, cmat.rearrange("p (a b) -> p a b", a=5))
        ident_f, J_f, U_f, negU_f, maskf_f = (cm_f.ap[:, i, :] for i in range(5))
        ones_f = A.tile([128, 128], F32, "ones")
        P.op("pool", lambda e: e.memset(ones_f.ap, 1.0), writes=[ones_f])
        mhalf = A.tile([128, 8], F32, "mhalf")
        P.op("pool", lambda e: e.memset(mhalf.ap, -0.5), writes=[mhalf])
        cm_b = A.tile([128, 2, 128], BF16, "cmat_bf")
        P.op("dve", lambda e: e.tensor_copy(out=cm_b.ap[:, 0, :], in_=ident_f), reads=[cm_f], writes=[cm_b])
        P.op("dve", lambda e: e.tensor_copy(out=cm_b.ap[:, 1, :], in_=maskf_f), reads=[cm_f], writes=[cm_b])
        ident_b = cm_b.ap[:, 0, :]
        maskf_b = cm_b.ap[:, 1, :]
        fl = A.tile([128, 24], F32, "flags")
        load(fl, flags)
        cw_t = A.tile([128, 5, 8, 3], F32, "cw")
        load(cw_t, cw.rearrange("p (a b c) -> p a b c", a=5, b=8))
        cb_t = A.tile([128, 8], F32, "cb")
        load(cb_t, cb)
        bg_t = A.tile([128, 5, 8], F32, "bg")
        load(bg_t, bg_b.rearrange("p (a b) -> p a b", a=5))
        rs_t = A.tile([128, 8], F32, "rowsc")
        load(rs_t, rowsc)
        esink = A.tile([128, 8], F32, "esink")
        load(esink, sink_b)
        P.op("act", lambda e: e.activation(out=esink.ap, in_=esink.ap, func=AF.Exp), reads=[esink], writes=[esink])

        pass_mark = A.mark()
        A1_b = A.tile([128, D], F32, "A1b")
        B1_b = A.tile([128, D], F32, "B1b")
        mod_s = nc.dram_tensor("mod_s", [4, 128, D], F32).ap()
        mod_sb = [Buf("mods%d" % i) for i in range(4)]

        stage = [None, None]
        stg_i = [0]
        cast_eng = ["pool", "dve", "act"]

        def cast_block(src_ap, ncols, dst_ap, dst_tile, mode=None, arg=None, engs=cast_eng, q="sp"):
            i = stg_i[0]
            stg_i[0] += 1
            st = stage[i % len(stage)]
            eng = engs[i % len(engs)]
            load(st.ap[:, 0:ncols], src_ap, wr=[st], q=q)
            sap = st.ap[:, 0:ncols]
            real_tile = dst_tile
            dst_tile = Buf("cast%d" % i)
            if not hasattr(real_tile, "cast_bufs"):
                real_tile.cast_bufs = []
            real_tile.cast_bufs.append(dst_tile)
            if mode is None:
                if eng == "act":
                    P.op("act", lambda e: e.copy(out=dst_ap, in_=sap), reads=[st], writes=[dst_tile])
                else:
                    P.op(eng, lambda e: e.tensor_copy(out=dst_ap, in_=sap), reads=[st], writes=[dst_tile])
            elif mode == "colscale":
                eng2 = "pool" if eng == "act" else eng
                P.op(eng2, lambda e: e.tensor_tensor(out=dst_ap, in0=sap, in1=arg[0], op=ALU.mult),
                     reads=[st, arg[1]], writes=[dst_tile])
            elif mode == "rowcol":
                P.op("dve", lambda e: e.scalar_tensor_tensor(out=dst_ap, in0=sap, scalar=arg[0], in1=arg[1],
                                                             op0=ALU.mult, op1=ALU.mult),
                     reads=[st, arg[2], arg[3]], writes=[dst_tile])

        def join(tile):
            P.op("pe", None, reads=tile.cast_bufs, writes=[tile])

        Wb = A.tile([128, 8, NWM], BF16, "w_main_bf")
        wmv2 = w_main.rearrange("(kc p) n -> p kc n", p=128)
        Wg_f = A.tile([128, 8, 40], F32, "wg_f")
        load(Wg_f, wg.rearrange("(kc p) n -> p kc n", p=128))
        Wg = A.tile([128, 8, 40], BF16, "wg_bf")
        P.op("dve", lambda e: e.tensor_copy(out=Wg.ap, in_=Wg_f.ap), reads=[Wg_f], writes=[Wg])
        EB = A.tile([128, 3, 8, 128], F32, "EB")
        load(EB, ebt.rearrange("p (a b c) -> p a b c", a=3, b=8))
        m0 = A.mark()
        gate1_b = A.tile([128, D], F32, "gate1b")
        gate2_b = A.tile([128, D], F32, "gate2b")
        A2_b = A.tile([128, D], F32, "A2b")
        B2_b = A.tile([128, D], F32, "B2b")
        mod_dst = [B1_b, A1_b, gate1_b, B2_b, A2_b, gate2_b]
        c_t = A.tile([128, 8], F32, "c")
        load(c_t, cvec)
        th_c = A.tile([128, 8], F32, "thc")
        sc_t = A.tile([128, 8], F32, "sc")
        P.op("act", lambda e: e.activation(out=th_c.ap, in_=c_t.ap, func=AF.Tanh, scale=0.5), reads=[c_t], writes=[th_c])
        P.op("dve", lambda e: e.scalar_tensor_tensor(out=sc_t.ap, in0=th_c.ap, scalar=1.0, in1=c_t.ap,
                                                     op0=ALU.add, op1=ALU.mult), reads=[th_c, c_t], writes=[sc_t])
        P.op("dve", lambda e: e.tensor_scalar(out=sc_t.ap, in0=sc_t.ap, scalar1=0.5, scalar2=None, op0=ALU.mult),
             reads=[sc_t], writes=[sc_t])
        scB = A.tile([128, 8, 128], F32, "scB")
        P.op("dve", lambda e: e.tensor_copy(out=scB.ap, in_=bc(sc_t.ap.unsqueeze(2), [128, 8, 128])),
             reads=[sc_t], writes=[scB])
        wm_ring = [A.tile([128, 8, 512], F32, "wmod%d" % i) for i in range(3)]
        bm_ring = [A.tile([128, 512], F32, "bmod%d" % i) for i in range(3)]
        wmv = w_mod.rearrange("(kc p) n -> p kc n", p=128)
        for nb in range(12):
            wt = wm_ring[nb % 3]
            bt = bm_ring[nb % 3]
            load(wt, wmv[:, :, nb * 512:(nb + 1) * 512])
            load(bt, bmod_b[:, nb * 512:(nb + 1) * 512])
            pt, pb = ps1()
            for kc in range(8):
                P.op("pe", lambda e, kc=kc, wt=wt, pt=pt: e.matmul(pt.ap, lhsT=scB.ap[:, kc, :], rhs=wt.ap[:, kc, :],
                                                                 start=(kc == 0), stop=(kc == 7)),
                     reads=[scB, wt], writes=pb)
            dst = mod_dst[nb // 2].ap[:, (nb % 2) * 512:(nb % 2 + 1) * 512]
            P.op("dve", lambda e, pt=pt, bt=bt, dst=dst: e.tensor_tensor(out=dst, in0=pt.ap, in1=bt.ap, op=ALU.add),
                 reads=pb + [bt], writes=[mod_dst[nb // 2]])
            pfree(pt)
        for (At, gsrc) in ((A1_b, g1_b), (A2_b, g2_b)):
            gt = A.tile([128, D], F32, "gtmp")
            load(gt, gsrc)
            P.op("dve", lambda e, At=At, gt=gt: e.scalar_tensor_tensor(out=At.ap, in0=At.ap, scalar=1.0, in1=gt.ap,
                                                                      op0=ALU.add, op1=ALU.mult),
                 reads=[At, gt], writes=[At])
        for i_, t_ in enumerate((gate1_b, gate2_b, A2_b, B2_b)):
            store(mod_s[i_], mod_sb[i_], t_)
        if KSTOP == "mod":
            dump(A1_b, [A1_b], 1024)
            dump(B1_b, [B1_b], 1024)
            dump(gate2_b, [gate2_b], 1024)
        stage[:] = [A.tile([128, 1472], F32, "stage%d" % i) for i in range(2)]
        for kc in range(8):
            for c0 in range(0, NWM, 1472):
                cast_block(wmv2[:, kc, c0:c0 + 1472], 1472, Wb.ap[:, kc, c0:c0 + 1472], Wb)
        join(Wb)
        stop("mod")
        A.release(m0)
        P.fence()
        stop("cast")

        xt_ring = [A.tile([128, D], F32, "xt%d" % i) for i in range(2)]
        plist = [4] if KSTOP.startswith("B_") else ([3, 4] if KSTOP.startswith("F_") else range(5))
        xseq = [(pi_, e_) for pi_ in plist for e_ in range(18)]
        xnext = [0]

        def ensure_x(upto):
            while xnext[0] <= min(upto, len(xseq) - 1):
                g = xnext[0]
                pi_, e_ = xseq[g]
                load(xt_ring[g % 2], xp[pi_, e_ * 128:(e_ + 1) * 128, :])
                xnext[0] += 1
        t1_ring = [A.tile([128, D], F32, "t1_%d" % i) for i in range(1)]
        hm_ring = [A.tile([128, D], BF16, "hm%d" % i) for i in range(2)]
        st_ring = [A.tile([128, 4], F32, "stat%d" % i) for i in range(4)]
        pair_ring = [A.tile([128, 8, 258], BF16, "pair%d" % i) for i in range(4)]
        halo_h = [A.tile([128, 8, 128], BF16, "haloh%d" % i) for i in range(2)]
        cv_ring = [A.tile([128, 3, 256], F32, "cv%d" % i) for i in range(2)]
        kT_ring = [A.tile([128, 4, 256], BF16, "kT%d" % i) for i in range(2)]
        qT_ring = [A.tile([128, 4, 256], BF16, "qT%d" % i) for i in range(2)]
        ktok_ring = [A.tile([128, 2, 512], BF16, "ktok%d" % i) for i in range(1)]
        vaug_ring = [A.tile([128, 2, 4, 129], BF16, "vaug%d" % i) for i in range(2)]
        wv_ring = [A.tile([128, 2, 4, 129], BF16, "wv%d" % i) for i in range(1)]
        for t in vaug_ring:
            P.op("pool", lambda e, t=t: e.memset(t.ap, 1.0), writes=[t])
        g_ring = [A.tile([128, 8, 2, 4], F32, "g%d" % i) for i in range(2)]
        nlfB_ring = [A.tile([128, 8, 128], F32, "nlfB%d" % i) for i in range(1)]
        E_ring = [A.tile([128, 4, 128], F32, "E%d" % i) for i in range(1)]
        Em_ring = [A.tile([128, 4, 128], BF16, "Em%d" % i) for i in range(2)]
        PT_ring = [A.tile([128, 4, 128], BF16, "PT%d" % i) for i in range(2)]
        tB_ring = [A.tile([128, 4, 129], F32, "tB%d" % i) for i in range(1)]
        R_ring = [A.tile([128, 4, 129], F32, "R%d" % i) for i in range(1)]
        dn_ring = [A.tile([128, 8], F32, "dn%d" % i) for i in range(2)]
        ho_ring = [A.tile([128, 4, 128], F32, "ho%d" % i) for i in range(2)]
        Cst = A.tile([128, 4, 129], F32, "Cstate")
        Cf = A.tile([128, 4, 129], F32, "Cf")
        Cb = A.tile([128, 4, 129], F32, "Cb")
        Cbf = A.tile([128, 4, 129], BF16, "Cbf")
        for t in (Cst, Cf, Cb):
            P.op("pool", lambda e, t=t: e.memset(t.ap, 0.0), writes=[t])
        qa_ring = [A.tile([128, 2, 4, 256], BF16, "qa%d" % i) for i in range(2)]
        for t in qa_ring:
            P.op("pool", lambda e, t=t: e.memset(t.ap, 0.0), writes=[t])
        kaT = A.tile([128, 2, 18 * 128], BF16, "kaT")
        vaa = A.tile([128, 18, 2, 65], BF16, "vaa")
        kaT_b = [Buf("kaT%d" % i) for i in range(18)]
        vaa_b = [Buf("vaa%d" % i) for i in range(18)]
        P.op("pool", lambda e: e.memset(vaa.ap, 1.0), writes=vaa_b)
        Ex_ring = [A.tile([128, 8, 128], F32, "Ex%d" % i) for i in range(1)]
        PA_ring = [A.tile([128, 8, 128], BF16, "PA%d" % i) for i in range(3)]
        att_ring = [A.tile([128, 8, 64], F32, "att%d" % i) for i in range(2)]
        attn_ring = [A.tile([128, 512], BF16, "attn%d" % i) for i in range(2)]
        attT_ring = [A.tile([128, 4, 128], BF16, "attT%d" % i) for i in range(2)]
        tho_ring = [A.tile([128, 512], F32, "tho%d" % i) for i in range(2)]
        cnt = {"tile": 0, "pair": 0, "chunk": 0, "att": 0, "cv": 0, "xg": 0}

        def rstd_from_ssq(ssq_ap, ssq_tile, n, width):
            P.op("dve", lambda e: e.tensor_scalar(out=ssq_ap, in0=ssq_ap, scalar1=1.0 / width, scalar2=EPS,
                                                  op0=ALU.mult, op1=ALU.add), reads=[ssq_tile], writes=[ssq_tile])
            P.op("pool", lambda e: e.tensor_tensor(out=ssq_ap, in0=ssq_ap, in1=mhalf.ap[:, 0:n], op=ALU.pow),
                 reads=[ssq_tile, mhalf], writes=[ssq_tile])

        def make_hmix_tile(src_ap, src_reads, Ab, Bb, flag_ap=None):
            i = cnt["tile"]
            cnt["tile"] += 1
            g = cnt["xg"]
            cnt["xg"] += 1
            ensure_x(g + 1)
            xt = xt_ring[g % 2]
            st = st_ring[i % 4]
            hm = hm_ring[i % 2]
            P.op("act", lambda e: e.activation(out=hm.ap, in_=xt.ap, func=AF.Square, accum_out=st.ap[:, 0:1]),
                 reads=[xt], writes=[hm, st], c=1.05)
            rstd_from_ssq(st.ap[:, 0:1], st, 1, D)
            t1 = t1_ring[0]
            P.op("dve", lambda e: e.scalar_tensor_tensor(out=t1.ap, in0=xt.ap, scalar=st.ap[:, 0:1], in1=Ab.ap,
                                                         op0=ALU.mult, op1=ALU.mult), reads=[xt, st, Ab], writes=[t1], c=1.3)
            P.op("pool", lambda e: e.tensor_tensor(out=hm.ap, in0=t1.ap, in1=Bb.ap, op=ALU.add),
                 reads=[t1, Bb], writes=[hm], c=2.4)
            if flag_ap is not None:
                P.op("pool", lambda e: e.tensor_scalar(out=hm.ap, in0=hm.ap, scalar1=flag_ap, scalar2=None, op0=ALU.mult),
                     reads=[hm, fl], writes=[hm])
            pt, pb = ps1()
            pv = pt.ap.bitcast(BF16).rearrange("p (a b) -> p a b", a=8)
            for kc in range(8):
                P.op("pe", lambda e, kc=kc: e.transpose(out=pv[:, kc, :], in_=hm.ap[:, kc * 128:(kc + 1) * 128],
                                                        identity=ident_b), reads=[hm, cm_b], writes=pb)
            return pv, pb, pt

        def run_pass(pi, mode):
            outs = mode in ("F", "B")
            full = mode == "F"
            sc_dst, sc_buf = (hf_s, hf_b) if mode == "F" else (hb_s, hb_b)

            def feat_proj(col0, hp, n, off=0):
                pt, pb = ps1()
                for kc in range(8):
                    P.op("pe", lambda e, kc=kc: e.matmul(pt.ap[:, 0:n], lhsT=Wb.ap[:, kc, col0:col0 + 128],
                                                         rhs=hp.ap[:, kc, off:off + n], start=(kc == 0), stop=(kc == 7)),
                         reads=[Wb, hp], writes=pb)
                return pt, pb

            def tok_proj(col0, ncols, lhs_ap, lhs_tile, wt=None):
                wt = Wb if wt is None else wt
                pt, pb = ps1()
                for kc in range(8):
                    P.op("pe", lambda e, kc=kc: e.matmul(pt.ap[:, 0:ncols], lhsT=lhs_ap[:, kc, :],
                                                         rhs=wt.ap[:, kc, col0:col0 + ncols], start=(kc == 0), stop=(kc == 7)),
                         reads=[wt, lhs_tile], writes=pb)
                return pt, pb

            def conv_silu(pt, pb, chunk, dst_ap, dst_tile):
                i = cnt["cv"]
                cnt["cv"] += 1
                cv = cv_ring[i % 2]
                w0, w1, w2 = (cw_t.ap[:, pi, chunk, k:k + 1] for k in range(3))
                P.op("act", lambda e: e.activation(out=cv.ap[:, 0, :], in_=pt.ap[:, 0:256], func=AF.Identity,
                                                   scale=w0, bias=cb_t.ap[:, chunk:chunk + 1]),
                     reads=pb + [cw_t, cb_t], writes=[cv])
                P.op("dve", lambda e: e.scalar_tensor_tensor(out=cv.ap[:, 1, :], in0=pt.ap[:, 1:257], scalar=w1,
                                                             in1=cv.ap[:, 0, :], op0=ALU.mult, op1=ALU.add),
                     reads=pb + [cw_t, cv], writes=[cv])
                P.op("dve", lambda e: e.scalar_tensor_tensor(out=cv.ap[:, 0, :], in0=pt.ap[:, 2:258], scalar=w2,
                                                             in1=cv.ap[:, 1, :], op0=ALU.mult, op1=ALU.add),
                     reads=pb + [cw_t, cv], writes=[cv])
                pfree(pt)
                P.op("act", lambda e: e.activation(out=cv.ap[:, 2, :], in_=cv.ap[:, 0, :], func=AF.Tanh, scale=0.5),
                     reads=[cv], writes=[cv])
                P.op("dve", lambda e: e.scalar_tensor_tensor(out=dst_ap, in0=cv.ap[:, 2, :], scalar=1.0,
                                                             in1=cv.ap[:, 0, :], op0=ALU.add, op1=ALU.mult),
                     reads=[cv], writes=[dst_tile])

            def ka_va_proj(e_idx, lhs_ap, lhs_tile, ntok_off=None, hp=None):
                for g in range(2):
                    pt, pb = ps1()
                    for kc in range(8):
                        P.op("pe", lambda e, kc=kc, g=g, pt=pt: e.matmul(pt.ap[:, 0:128], lhsT=Wb.ap[:, kc, KA + g * 128:KA + (g + 1) * 128],
                                                                  rhs=lhs_ap[:, kc, :], start=(kc == 0), stop=(kc == 7)),
                             reads=[Wb, lhs_tile], writes=pb)
                    P.op("act", lambda e, g=g, pt=pt: e.copy(out=kaT.ap[:, g, e_idx * 128:(e_idx + 1) * 128], in_=pt.ap[:, 0:128]),
                         reads=pb, writes=[kaT_b[e_idx]])
                    pfree(pt)
                pt, pb = tok_proj(VA, 128, lhs_ap, lhs_tile)
                P.op("dve", lambda e: e.tensor_copy(out=vaa.ap[:, e_idx, :, 0:64],
                                                    in_=pt.ap[:, 0:128].rearrange("p (g d) -> p g d", g=2)),
                     reads=pb, writes=[vaa_b[e_idx]])
                pfree(pt)

            def attention_block(n):
                i = cnt["att"]
                cnt["att"] += 1
                qa = qa_ring[(n // 2) % 2]
                qc = (n % 2) * 128
                PAs = []
                for kb in range(3):
                    ek = n + kb
                    pt, pb = ps2()
                    pv = pt.ap.rearrange("p (h q) -> p h q", h=8)
                    for h in range(8):
                        par, hp4, g = h % 2, h // 2, h // 4
                        P.op("pe", lambda e, h=h, par=par, hp4=hp4, g=g, pv=pv, ek=ek: e.matmul(
                            pv[:, h, :], lhsT=kaT.ap[:, g, ek * 128:(ek + 1) * 128],
                            rhs=qa.ap[:, par, hp4, qc:qc + 128], start=True, stop=True),
                            reads=[kaT_b[ek], qa], writes=pb)
                    Ex = Ex_ring[0]
                    for hh in range(2):
                        P.op("act", lambda e, hh=hh, pv=pv, Ex=Ex: e.activation(out=Ex.ap[:, hh * 4:hh * 4 + 4, :],
                                                                              in_=pv[:, hh * 4:hh * 4 + 4, :],
                                                                              func=AF.Exp, scale=0.125),
                             reads=pb, writes=[Ex])
                    pfree(pt)
                    PA = PA_ring[kb]
                    P.op("pool", lambda e, Ex=Ex, PA=PA, kb=kb: e.tensor_tensor(out=PA.ap, in0=Ex.ap, in1=EB.ap[:, kb, :, :],
                                                                             op=ALU.mult), reads=[Ex, EB], writes=[PA], c=2.0)
                    PAs.append(PA)
                    yield
                pto, pb = ps2()
                po = pto.ap.rearrange("p (h q) -> p h q", h=8)
                for h in range(8):
                    g = h // 4
                    for kb in range(3):
                        ek = n + kb
                        P.op("pe", lambda e, h=h, g=g, kb=kb, ek=ek, po=po: e.matmul(
                            po[:, h, 0:65], lhsT=PAs[kb].ap[:, h, :], rhs=vaa.ap[:, ek, g, :],
                            start=(kb == 0), stop=(kb == 2)), reads=[PAs[kb], vaa_b[ek]], writes=pb)
                yield
                dn = dn_ring[i % 2]
                P.op("dve", lambda e: e.tensor_tensor(out=dn.ap, in0=po[:, :, 64], in1=esink.ap, op=ALU.add),
                     reads=pb + [esink], writes=[dn])
                P.op("dve", lambda e: e.reciprocal(out=dn.ap, in_=dn.ap), reads=[dn], writes=[dn])
                att = att_ring[i % 2]
                P.op("dve", lambda e: e.tensor_tensor(out=att.ap, in0=po[:, :, 0:64],
                                                      in1=bc(dn.ap.unsqueeze(2), [128, 8, 64]), op=ALU.mult),
                     reads=pb + [dn], writes=[att])
                pfree(pto)
                st = st_ring[cnt["tile"] % 4]
                cnt["tile"] += 1
                attn = attn_ring[i % 2]
                P.op("act", lambda e: e.activation(out=attn.ap, in_=att.ap.rearrange("p h d -> p (h d)"), func=AF.Square,
                                                   accum_out=st.ap[:, 0:1]), reads=[att], writes=[attn, st])
                rstd_from_ssq(st.ap[:, 0:1], st, 1, 512)
                P.op("act", lambda e: e.activation(out=attn.ap, in_=att.ap.rearrange("p h d -> p (h d)"), func=AF.Identity,
                                                   scale=st.ap[:, 0:1]), reads=[att, st], writes=[attn])
                pt, pb = ps1()
                pv = pt.ap.bitcast(BF16)[:, 0:512].rearrange("p (a b) -> p a b", a=4)
                for kc in range(4):
                    P.op("pe", lambda e, kc=kc: e.transpose(out=pv[:, kc, :], in_=attn.ap[:, kc * 128:(kc + 1) * 128],
                                                            identity=ident_b), reads=[attn, cm_b], writes=pb)
                aT = attT_ring[i % 2]
                P.op("act", lambda e: e.copy(out=aT.ap, in_=pv), reads=pb, writes=[aT])
                pfree(pt)
                store(at_s[n].rearrange("p (a b) -> p a b", a=4), at_b[n], aT)
                yield

            def gen_A(u):
                hp = pair_ring[u % 4]
                kT = kT_ring[u % 2]
                qT = qT_ring[u % 2]
                vaug = vaug_ring[u % 2]
                gt = g_ring[u % 2]
                for h in range(4):
                    pt, pb = feat_proj(KM + h * 128, hp, 258)
                    conv_silu(pt, pb, 4 + h, kT.ap[:, h, :], kT)
                    yield
                if outs:
                    for h in range(4):
                        pt, pb = feat_proj(QM + h * 128, hp, 258)
                        conv_silu(pt, pb, h, qT.ap[:, h, :], qT)
                        yield
                ptg, pbg = ps1()
                gv = ptg.ap[:, 0:16].rearrange("p (c g) -> p c g", c=2)
                for ci in range(2):
                    lhs = hp.ap[:, :, 1 + ci * 128:1 + (ci + 1) * 128]
                    pt, pb = tok_proj(VM, 512, lhs, hp)
                    P.op("act", lambda e, ci=ci, pt=pt: e.copy(out=vaug.ap[:, ci, :, 0:128],
                                                             in_=pt.ap.rearrange("p (h d) -> p h d", h=4)),
                         reads=pb, writes=[vaug])
                    pfree(pt)
                    for kc in range(8):
                        P.op("pe", lambda e, kc=kc, ci=ci, lhs=lhs: e.matmul(gv[:, ci, :], lhsT=lhs[:, kc, :],
                                                                           rhs=Wg.ap[:, kc, pi * 8:pi * 8 + 8],
                                                                           start=(kc == 0), stop=(kc == 7)),
                             reads=[Wg, hp], writes=pbg)
                    yield
                    if full:
                        pt, pb = tok_proj(OM, 512, lhs, hp)
                        c = 2 * u + ci
                        tho = tho_ring[c % 2]
                        P.op("act", lambda e, pt=pt, tho=tho: e.activation(out=tho.ap, in_=pt.ap, func=AF.Tanh, scale=0.5),
                             reads=pb, writes=[tho])
                        pfree(pt)
                        store(th_s[c], th_b[c], tho)
                        yield
                gi, gf = gt.ap[:, 0, :, :], gt.ap[:, 1, :, :]
                bgi = bc(bg_t.ap[:, pi, 0:4].unsqueeze(1), [128, 2, 4])
                bgf = bc(bg_t.ap[:, pi, 4:8].unsqueeze(1), [128, 2, 4])
                P.op("dve", lambda e: e.tensor_tensor(out=gi, in0=gv[:, :, 0:4], in1=bgi, op=ALU.add),
                     reads=pbg + [bg_t], writes=[gt])
                P.op("dve", lambda e: e.tensor_tensor(out=gf, in0=gv[:, :, 4:8], in1=bgf, op=ALU.add),
                     reads=pbg + [bg_t], writes=[gt])
                pfree(ptg)
                yield
                if full:
                    qa = qa_ring[u % 2]
                    for hp4 in range(4):
                        pt, pb = feat_proj(QA + hp4 * 128, hp, 256, off=1)
                        P.op("dve", lambda e, hp4=hp4, pt=pt: e.tensor_copy(out=qa.ap[0:64, 0, hp4, :], in_=pt.ap[0:64, 0:256]),
                             reads=pb, writes=[qa])
                        P.op("dve", lambda e, hp4=hp4, pt=pt: e.tensor_copy(out=qa.ap[64:128, 1, hp4, :], in_=pt.ap[64:128, 0:256]),
                             reads=pb, writes=[qa])
                        pfree(pt)
                        yield
                    for ci in range(2):
                        ka_va_proj(2 * u + 1 + ci, hp.ap[:, :, 1 + ci * 128:1 + (ci + 1) * 128], hp)
                        yield

            def gen_B(u):
                kT = kT_ring[u % 2]
                qT = qT_ring[u % 2]
                vaug = vaug_ring[u % 2]
                gt = g_ring[u % 2]
                wv = wv_ring[0]
                ktok = ktok_ring[0]
                gi, gf = gt.ap[:, 0, :, :], gt.ap[:, 1, :, :]
                P.op("act", lambda e: e.activation(out=gt.ap[:, 2, :, :], in_=gf, func=AF.Exp, scale=-1.0),
                     reads=[gt], writes=[gt])
                P.op("act", lambda e: e.activation(out=gf, in_=gt.ap[:, 2, :, :], func=AF.Ln, bias=1.0),
                     reads=[gt], writes=[gt])
                nlf2 = gt.ap[:, 1, :, :].rearrange("p c h -> p (c h)")
                ptc, pbc = ps1()
                P.op("pe", lambda e: e.matmul(ptc.ap[:, 0:8], lhsT=U_f, rhs=nlf2, start=True, stop=True),
                     reads=[cm_f, gt], writes=pbc)
                P.op("pe", lambda e: e.matmul(ptc.ap[:, 8:16], lhsT=ones_f.ap, rhs=nlf2, start=True, stop=True),
                     reads=[ones_f, gt], writes=pbc)
                Pn = ptc.ap[:, 0:8].rearrange("p (c h) -> p c h", c=2)
                Tn = ptc.ap[:, 8:16].rearrange("p (c h) -> p c h", c=2)
                biasE = gt.ap[:, 2, :, :]
                P.op("dve", lambda e: e.tensor_tensor(out=biasE, in0=Pn, in1=gi, op=ALU.add), reads=pbc + [gt], writes=[gt])
                P.op("dve", lambda e: e.scalar_tensor_tensor(out=gt.ap[:, 3, :, :], in0=biasE, scalar=LN_HALF, in1=Tn,
                                                             op0=ALU.add, op1=ALU.subtract), reads=pbc + [gt], writes=[gt])
                wq = gt.ap[:, 4, :, :]
                ebq = gt.ap[:, 5, :, :]
                dcy = gt.ap[:, 6, :, :]
                lneb = gt.ap[:, 7, :, :]
                P.op("pool", lambda e: e.memset(lneb, LN_EB), writes=[gt])
                P.op("act", lambda e: e.activation(out=wq, in_=gt.ap[:, 3, :, :], func=AF.Exp), reads=[gt], writes=[gt])
                P.op("act", lambda e: e.activation(out=ebq.rearrange("p c h -> p (c h)"), in_=ptc.ap[:, 0:8], func=AF.Exp,
                                                   scale=-1.0, bias=lneb[:, 0, 0:1]), reads=pbc + [gt], writes=[gt])
                P.op("act", lambda e: e.activation(out=dcy.rearrange("p c h -> p (c h)"), in_=ptc.ap[:, 8:16], func=AF.Exp,
                                                   scale=-1.0), reads=pbc, writes=[gt])
                pfree(ptc)
                nlfB = nlfB_ring[0]
                if outs:
                    P.op("pool", lambda e: e.tensor_copy(out=nlfB.ap, in_=bc(nlf2.unsqueeze(2), [128, 8, 128])),
                         reads=[gt], writes=[nlfB])
                yield
                for ci in range(2):
                    pt, pb = ps1()
                    pv = pt.ap.bitcast(BF16)[:, 0:512].rearrange("p (a b) -> p a b", a=4)
                    for h in range(4):
                        P.op("pe", lambda e, h=h, ci=ci, pv=pv: e.transpose(out=pv[:, h, :], in_=kT.ap[:, h, ci * 128:(ci + 1) * 128],
                                                                          identity=ident_b), reads=[kT, cm_b], writes=pb)
                    P.op("act", lambda e, ci=ci, pv=pv: e.copy(out=ktok.ap[:, ci, :], in_=pv.rearrange("p a b -> p (a b)")),
                         reads=pb, writes=[ktok])
                    pfree(pt)
                    P.op("pool", lambda e, ci=ci: e.tensor_tensor(out=wv.ap[:, ci, :, :], in0=vaug.ap[:, ci, :, :],
                                                                in1=bc(wq[:, ci, :].unsqueeze(2), [128, 4, 129]), op=ALU.mult),
                         reads=[vaug, gt], writes=[wv])
                    yield
                for ci in range(2):
                    c = 2 * u + ci
                    k = c
                    cs = slice(ci * 128, (ci + 1) * 128)
                    if outs:
                        ptr, pbr = ps1()
                        rv = ptr.ap.rearrange("p (h t) -> p h t", h=4)
                        for h in range(4):
                            P.op("pe", lambda e, h=h, ci=ci, rv=rv: e.matmul(rv[:, h, :], lhsT=nlfB.ap[:, ci * 4 + h, :], rhs=negU_f,
                                                                           start=True, stop=True), reads=[nlfB, cm_f], writes=pbr)
                        E = E_ring[0]
                        for h in range(4):
                            P.op("act", lambda e, h=h, ci=ci, rv=rv, E=E: e.activation(out=E.ap[:, h, :], in_=rv[:, h, :], func=AF.Exp,
                                                                                    bias=biasE[:, ci, h:h + 1]),
                                 reads=pbr + [gt], writes=[E])
                        pfree(ptr)
                        Em = Em_ring[k % 2]
                        P.op("pool", lambda e, E=E, Em=Em: e.tensor_tensor(out=Em.ap, in0=E.ap,
                                                                           in1=bc(maskf_f.unsqueeze(1), [128, 4, 128]), op=ALU.mult),
                             reads=[E, cm_f], writes=[Em])
                        yield
                        pts, pbs = ps1()
                        sv = pts.ap.rearrange("p (h t) -> p h t", h=4)
                        for h in range(4):
                            P.op("pe", lambda e, h=h, cs=cs, sv=sv: e.matmul(sv[:, h, :], lhsT=kT.ap[:, h, cs], rhs=qT.ap[:, h, cs],
                                                                           start=True, stop=True), reads=[kT, qT], writes=pbs)
                        PT = PT_ring[k % 2]
                        P.op("dve", lambda e, sv=sv, Em=Em, PT=PT: e.tensor_tensor(out=PT.ap, in0=sv, in1=Em.ap, op=ALU.mult),
                             reads=pbs + [Em], writes=[PT])
                        pfree(pts)
                        yield
                        ptb, pbb = ps2()
                        bv = ptb.ap.rearrange("p (h n) -> p h n", h=4)
                        for h in range(4):
                            P.op("pe", lambda e, h=h, cs=cs, bv=bv: e.matmul(bv[:, h, 0:129], lhsT=qT.ap[:, h, cs], rhs=Cbf.ap[:, h, :],
                                                                           start=True, stop=True), reads=[qT, Cbf], writes=pbb)
                        tB = tB_ring[0]
                        P.op("dve", lambda e, ci=ci, bv=bv, tB=tB: e.tensor_tensor(out=tB.ap, in0=bv[:, :, 0:129],
                                                                                 in1=bc(ebq[:, ci, :].unsqueeze(2), [128, 4, 129]),
                                                                                 op=ALU.mult), reads=pbb + [gt], writes=[tB])
                        pfree(ptb)
                        pta, pba = ps2()
                        av = pta.ap.rearrange("p (h n) -> p h n", h=4)
                        for h in range(4):
                            P.op("pe", lambda e, h=h, ci=ci, av=av, PT=PT: e.matmul(av[:, h, 0:129], lhsT=PT.ap[:, h, :],
                                                                                  rhs=vaug.ap[:, ci, h, :], start=True, stop=True),
                                 reads=[PT, vaug], writes=pba)
                        R = R_ring[0]
                        P.op("dve", lambda e, av=av, tB=tB, R=R: e.tensor_tensor(out=R.ap, in0=av[:, :, 0:129], in1=tB.ap, op=ALU.add),
                             reads=pba + [tB], writes=[R])
                        pfree(pta)
                        yield
                    ptk, pbk = ps2()
                    kv = ptk.ap.rearrange("p (h n) -> p h n", h=4)
                    for h in range(4):
                        P.op("pe", lambda e, h=h, ci=ci, kv=kv: e.matmul(kv[:, h, 0:129], lhsT=ktok.ap[:, ci, h * 128:(h + 1) * 128],
                                                                       rhs=wv.ap[:, ci, h, :], start=True, stop=True),
                             reads=[ktok, wv], writes=pbk)
                    P.op("pool", lambda e, ci=ci: e.tensor_tensor(out=Cst.ap, in0=Cst.ap, in1=bc(dcy[:, ci, :].unsqueeze(2), [128, 4, 129]),
                                                                 op=ALU.mult), reads=[Cst, gt], writes=[Cst], c=1.0)
                    P.op("dve", lambda e, kv=kv: e.tensor_tensor(out=Cst.ap, in0=kv[:, :, 0:129], in1=Cst.ap, op=ALU.add),
                         reads=pbk + [Cst], writes=[Cst])
                    pfree(ptk)
                    if outs:
                        P.op("act", lambda e: e.copy(out=Cbf.ap, in_=Cst.ap), reads=[Cst], writes=[Cbf])
                        dn = dn_ring[k % 2]
                        P.op("act", lambda e, R=R, dn=dn: e.activation(out=dn.ap[:, 0:4], in_=R.ap[:, :, 128], func=AF.Abs),
                             reads=[R], writes=[dn])
                        P.op("dve", lambda e, dn=dn: e.tensor_scalar(out=dn.ap[:, 0:4], in0=dn.ap[:, 0:4], scalar1=1.0, scalar2=None,
                                                                     op0=ALU.max), reads=[dn], writes=[dn])
                        P.op("dve", lambda e, dn=dn: e.reciprocal(out=dn.ap[:, 0:4], in_=dn.ap[:, 0:4]), reads=[dn], writes=[dn])
                        ho = ho_ring[k % 2]
                        P.op("dve", lambda e, R=R, dn=dn, ho=ho: e.tensor_tensor(out=ho.ap, in0=R.ap[:, :, 0:128],
                                                                               in1=bc(dn.ap[:, 0:4].unsqueeze(2), [128, 4, 128]),
                                                                               op=ALU.mult), reads=[R, dn], writes=[ho])
                        store(sc_dst[c].rearrange("p (h d) -> p h d", h=4), sc_buf[c], ho)
                    yield

            def gen_C(u):
                if u >= 1:
                    yield from attention_block(2 * u - 1)
                yield from attention_block(2 * u)

            def gen_T(e_idx):
                flag_ap = None
                if e_idx == 0:
                    flag_ap = fl.ap[:, 9 + pi:10 + pi]
                elif e_idx == 17:
                    flag_ap = fl.ap[:, 14 + pi:15 + pi]
                pv, pb, ptt = make_hmix_tile(xp[pi, e_idx * 128:(e_idx + 1) * 128, :], [], A1_b, B1_b, flag_ap)
                yield
                if 1 <= e_idx <= 16:
                    u, half = (e_idx - 1) // 2, (e_idx - 1) % 2
                    dstt = pair_ring[u % 4]
                    P.op("act", lambda e, dstt=dstt, half=half, pv=pv: e.copy(out=dstt.ap[:, :, 1 + half * 128:1 + (half + 1) * 128], in_=pv),
                         reads=pb, writes=[dstt])
                    src_t, src_first, src_last = dstt, dstt.ap[:, :, 1 + half * 128:2 + half * 128], dstt.ap[:, :, 128 + half * 128:129 + half * 128]
                else:
                    hh = halo_h[0 if e_idx == 0 else 1]
                    P.op("act", lambda e, hh=hh, pv=pv: e.copy(out=hh.ap, in_=pv), reads=pb, writes=[hh])
                    src_t, src_first, src_last = hh, hh.ap[:, :, 0:1], hh.ap[:, :, 127:128]
                pfree(ptt)
                if e_idx % 2 == 1 and e_idx >= 3:
                    dstt = pair_ring[((e_idx - 3) // 2) % 4]
                    P.op("pool", lambda e, dstt=dstt, src_first=src_first: e.tensor_copy(out=dstt.ap[:, :, 257:258], in_=src_first),
                         reads=[src_t], writes=[dstt], c=0.2)
                if e_idx % 2 == 0 and e_idx <= 14:
                    dstt = pair_ring[(e_idx // 2) % 4]
                    P.op("pool", lambda e, dstt=dstt, src_last=src_last: e.tensor_copy(out=dstt.ap[:, :, 0:1], in_=src_last),
                         reads=[src_t], writes=[dstt], c=0.2)
                if full and e_idx in (0, 17):
                    ka_va_proj(e_idx, hh.ap, hh)
                    fa = fl.ap[:, 9 + pi:10 + pi] if e_idx == 0 else fl.ap[:, 14 + pi:15 + pi]
                    P.op("pool", lambda e, e_idx=e_idx, fa=fa: e.tensor_scalar(out=vaa.ap[:, e_idx, :, :], in0=vaa.ap[:, e_idx, :, :],
                                                                             scalar1=fa, scalar2=None, op0=ALU.mult),
                         reads=[vaa_b[e_idx], fl], writes=[vaa_b[e_idx]])
                yield

            def chain(*gens):
                for g_ in gens:
                    yield from g_

            def interleave(gens):
                gens = [g_ for g_ in gens if g_ is not None]
                while gens:
                    for g_ in list(gens):
                        try:
                            next(g_)
                        except StopIteration:
                            gens.remove(g_)

            return {"T": gen_T, "A": gen_A, "B": gen_B, "C": gen_C, "att": attention_block, "full": full}

        def chain(*gens):
            for g_ in gens:
                yield from g_

        def interleave(gens):
            gens = [g_ for g_ in gens if g_ is not None]
            while gens:
                for g_ in list(gens):
                    try:
                        next(g_)
                    except StopIteration:
                        gens.remove(g_)

        modes = ["slot", "slot", "slot", "F", "B"]
        objs = [run_pass(pi_, modes[pi_]) for pi_ in range(5)]

        def pre_state(p):
            if p < 3:
                P.op("dve", lambda e: e.tensor_scalar(out=Cst.ap, in0=Cst.ap, scalar1=fl.ap[:, p:p + 1], scalar2=None, op0=ALU.mult),
                     reads=[Cst, fl], writes=[Cst])
            else:
                Csrc = Cf if p == 3 else Cb
                P.op("dve", lambda e: e.tensor_copy(out=Cst.ap, in_=Csrc.ap), reads=[Csrc], writes=[Cst])
                P.op("act", lambda e: e.copy(out=Cbf.ap, in_=Cst.ap), reads=[Cst], writes=[Cbf])

        def post_state(p):
            if p < 3:
                P.op("dve", lambda e: e.scalar_tensor_tensor(out=Cf.ap, in0=Cst.ap, scalar=fl.ap[:, 3 + p:4 + p], in1=Cf.ap,
                                                             op0=ALU.mult, op1=ALU.add), reads=[Cst, fl, Cf], writes=[Cf])
                P.op("dve", lambda e: e.scalar_tensor_tensor(out=Cb.ap, in0=Cst.ap, scalar=fl.ap[:, 6 + p:7 + p], in1=Cb.ap,
                                                             op0=ALU.mult, op1=ALU.add), reads=[Cst, fl, Cb], writes=[Cb])

        def gT(p, es):
            return chain(*[objs[p]["T"](e_) for e_ in es if e_ <= 17])

        interleave([gT(0, range(5))])
        for p in range(5):
            o = objs[p]
            for k_ in range(8):
                streams = []
                if k_ >= 1:
                    streams.append(o["B"](k_ - 1))
                    if o["full"]:
                        streams.append(o["C"](k_ - 1))
                elif p >= 1:
                    prev = objs[p - 1]
                    streams.append(prev["B"](7))
                    if prev["full"]:
                        streams.append(chain(prev["C"](7), prev["att"](15)))
                streams.append(o["A"](k_))
                if k_ < 7:
                    streams.insert(0, gT(p, (2 * k_ + 5, 2 * k_ + 6)))
                elif p < 4:
                    streams.insert(0, gT(p + 1, range(5)))
                interleave(streams)
                if k_ == 0:
                    if p >= 1:
                        post_state(p - 1)
                    pre_state(p)
        interleave([objs[4]["B"](7)])

        A.release(pass_mark)
        P.fence()
        Wfi = A.tile([128, 8, 2 * DFF], BF16, "wfi_bf")
        Wfo = A.tile([128, NFC, D], BF16, "wfo_bf")
        p2_mark = A.mark()
        wout_b = A.tile([128, 8, D], BF16, "wout_bf")
        stage[:] = [A.tile([128, 704], F32, "stageB%d" % i) for i in range(4)]
        gate1_p = A.tile([128, D], F32, "gate1p")
        gate2_p = A.tile([128, D], F32, "gate2p")
        load(gate1_p, mod_s[0], rd=[mod_sb[0]])
        load(gate2_p, mod_s[1], rd=[mod_sb[1]])
        wov = w_out.rearrange("(kc p) n -> p kc n", p=128)
        for kc in range(8):
            for c0 in (0, 512):
                cast_block(wov[:, kc, c0:c0 + 512], 512, wout_b.ap[:, kc, c0:c0 + 512], wout_b, mode="rowcol",
                           arg=(rs_t.ap[:, kc:kc + 1], gate1_p.ap[:, c0:c0 + 512], rs_t, gate1_p))
        join(wout_b)
        hf_ring = [A.tile([128, 4, 128], F32, "hfl%d" % i) for i in range(2)]
        hb_ring = [A.tile([128, 512], F32, "hbl%d" % i) for i in range(2)]
        th_ring = [A.tile([128, 512], F32, "thl%d" % i) for i in range(2)]
        x_ring = [A.tile([128, D], F32, "xl%d" % i) for i in range(2)]
        mixT_ring = [A.tile([128, 8, 128], BF16, "mixT%d" % i) for i in range(2)]
        hsum_ring = [A.tile([128, 4, 128], F32, "hsum%d" % i) for i in range(2)]
        hm2_ring = [A.tile([128, 512], BF16, "hm2_%d" % i) for i in range(2)]
        st2_ring = [A.tile([128, 4], F32, "st2_%d" % i) for i in range(2)]
        junk2 = A.tile([128, 128], BF16, "junk2")
        wfiv = w_fi.rearrange("(kc p) n -> p kc n", p=128)
        wfov = w_fo.rearrange("(f p) n -> p f n", p=128)
        wjobs = []
        for kc in range(8):
            for c0 in range(0, 2 * DFF, 704):
                wjobs.append(("fi", kc, c0))
        for f in range(NFC):
            for c0 in (0, 512):
                wjobs.append(("fo", f, c0))

        def do_wjobs(n):
            for _ in range(n):
                if not wjobs:
                    return
                kind, a, c0 = wjobs.pop(0)
                if kind == "fi":
                    cast_block(wfiv[:, a, c0:c0 + 704], 704, Wfi.ap[:, a, c0:c0 + 704], Wfi, engs=["pool", "act", "dve", "act"], q="act")
                else:
                    cast_block(wfov[:, a, c0:c0 + 512], 512, Wfo.ap[:, a, c0:c0 + 512], Wfo, mode="colscale",
                               arg=(gate2_p.ap[:, c0:c0 + 512], gate2_p), engs=["pool", "dve"], q="act")

        def p2_loads(c):
            load(hf_ring[c % 2], hf_s[c].rearrange("p (h d) -> p h d", h=4), rd=[hf_b[c]])
            load(hb_ring[c % 2], hb_s[NT - 1 - c], rd=[hb_b[NT - 1 - c]])
            load(th_ring[c % 2], th_s[c], rd=[th_b[c]])
            load(x_ring[c % 2], xp[3, 128 + c * 128:256 + c * 128, :])
            mt = mixT_ring[c % 2]
            P.op("sp", lambda e: e.dma_start(out=mt.ap[:, 0:4, :], in_=at_s[c].rearrange("p (a b) -> p a b", a=4)),
                 reads=[at_b[c]], writes=[mt], dma=True)

        p2_loads(0)
        for c in range(NT):
            if c + 1 < NT:
                p2_loads(c + 1)
            do_wjobs(7)
            hfl, hbl, thl, xl, mt = hf_ring[c % 2], hb_ring[c % 2], th_ring[c % 2], x_ring[c % 2], mixT_ring[c % 2]
            pt, pb = ps1()
            P.op("pe", lambda e, pt=pt, hbl=hbl: e.matmul(pt.ap, lhsT=J_f, rhs=hbl.ap, start=True, stop=True),
                 reads=[cm_f, hbl], writes=pb)
            hs = hsum_ring[c % 2]
            P.op("dve", lambda e, pt=pt, hfl=hfl, hs=hs: e.tensor_tensor(out=hs.ap, in0=pt.ap.rearrange("p (h d) -> p h d", h=4),
                                                                       in1=hfl.ap, op=ALU.add), reads=pb + [hfl], writes=[hs])
            pfree(pt)
            st = st2_ring[c % 2]
            for h in range(4):
                P.op("act", lambda e, h=h, hs=hs, st=st: e.activation(out=junk2.ap, in_=hs.ap[:, h, :], func=AF.Square,
                                                                   accum_out=st.ap[:, h:h + 1]), reads=[hs], writes=[junk2, st])
            rstd_from_ssq(st.ap[:, 0:4], st, 4, 128)
            P.op("dve", lambda e, st=st: e.tensor_scalar(out=st.ap[:, 0:4], in0=st.ap[:, 0:4], scalar1=0.5, scalar2=None, op0=ALU.mult),
                 reads=[st], writes=[st])
            P.op("dve", lambda e, hs=hs, st=st: e.tensor_tensor(out=hs.ap, in0=hs.ap, in1=bc(st.ap[:, 0:4].unsqueeze(2), [128, 4, 128]),
                                                              op=ALU.mult), reads=[hs, st], writes=[hs])
            hm2 = hm2_ring[c % 2]
            P.op("dve", lambda e, hs=hs, thl=thl, hm2=hm2: e.scalar_tensor_tensor(out=hm2.ap, in0=thl.ap, scalar=1.0,
                                                                                in1=hs.ap.rearrange("p h d -> p (h d)"),
                                                                                op0=ALU.add, op1=ALU.mult), reads=[hs, thl], writes=[hm2])
            pt, pb = ps1()
            pv = pt.ap.bitcast(BF16)[:, 0:512].rearrange("p (a b) -> p a b", a=4)
            for kc in range(4):
                P.op("pe", lambda e, kc=kc, pv=pv, hm2=hm2: e.transpose(out=pv[:, kc, :], in_=hm2.ap[:, kc * 128:(kc + 1) * 128],
                                                                      identity=ident_b), reads=[hm2, cm_b], writes=pb)
            P.op("act", lambda e, pv=pv, mt=mt: e.copy(out=mt.ap[:, 4:8, :], in_=pv), reads=pb, writes=[mt])
            pfree(pt)
            x1o = xl
            for half in range(2):
                pt, pb = ps1()
                for kc in range(8):
                    P.op("pe", lambda e, kc=kc, pt=pt, mt=mt, half=half: e.matmul(pt.ap, lhsT=mt.ap[:, kc, :],
                                                                                rhs=wout_b.ap[:, kc, half * 512:(half + 1) * 512],
                                                                                start=(kc == 0), stop=(kc == 7)),
                         reads=[mt, wout_b], writes=pb)
                P.op("dve", lambda e, pt=pt, half=half, xl=xl, x1o=x1o: e.tensor_tensor(out=x1o.ap[:, half * 512:(half + 1) * 512], in0=pt.ap,
                                                                                      in1=xl.ap[:, half * 512:(half + 1) * 512], op=ALU.add),
                     reads=pb + [xl], writes=[x1o])
                pfree(pt)
            store(x1_s[c], x1_b[c], x1o)
        do_wjobs(1000)
        join(Wfi)
        join(Wfo)

        A.release(p2_mark)
        P.fence()
        gfin_t = A.tile([128, D], F32, "gfin")
        load(gfin_t, gfin_b)
        A2_p = A.tile([128, D], F32, "A2p")
        B2_p = A.tile([128, D], F32, "B2p")
        load(A2_p, mod_s[2], rd=[mod_sb[2]])
        load(B2_p, mod_s[3], rd=[mod_sb[3]])
        x1_ring = [A.tile([128, D], F32, "x1l%d" % i) for i in range(4)]
        junk3 = A.tile([128, D], BF16, "junk3")
        t1_ring[:] = [A.tile([128, D], F32, "t1b%d" % i) for i in range(1)]
        hm_ring[:] = [A.tile([128, D], BF16, "hmb%d" % i) for i in range(2)]
        st_ring[:] = [A.tile([128, 4], F32, "statb%d" % i) for i in range(4)]
        hffT_ring = [A.tile([128, 8, 256], BF16, "hffT%d" % i) for i in range(2)]
        gu_ring = [A.tile([128, NFC, 256], BF16, "gu%d" % i) for i in range(1)]
        sg_ring = [A.tile([128, 256], F32, "sg%d" % i) for i in range(3)]
        x2_ring = [A.tile([128, D], F32, "x2_%d" % i) for i in range(1)]
        oo_ring = [A.tile([128, D], F32, "oo%d" % i) for i in range(1)]
        st3_ring = [A.tile([128, 4], F32, "st3_%d" % i) for i in range(2)]
        junk_holder = [junk3]

        def ffn_loads(gi):
            for t in range(2):
                c = 2 * gi + t
                load(x1_ring[c % 4], x1_s[c], rd=[x1_b[c]])

        def ffn_group(gi):
            hffT = hffT_ring[gi % 2]
            xts = []
            if gi + 1 < NT // 2:
                ffn_loads(gi + 1)
            for t in range(2):
                c = 2 * gi + t
                i = cnt["tile"]
                xt = x1_ring[c % 4]
                cnt["tile"] += 1
                st = st_ring[i % 4]
                P.op("act", lambda e, xt=xt, st=st: e.activation(out=junk_holder[0].ap, in_=xt.ap, func=AF.Square, accum_out=st.ap[:, 0:1]),
                     reads=[xt], writes=[junk_holder[0], st])
                rstd_from_ssq(st.ap[:, 0:1], st, 1, D)
                t1 = t1_ring[0]
                P.op("dve", lambda e, xt=xt, st=st, t1=t1: e.scalar_tensor_tensor(out=t1.ap, in0=xt.ap, scalar=st.ap[:, 0:1], in1=A2_p.ap,
                                                                                op0=ALU.mult, op1=ALU.mult), reads=[xt, st, A2_p], writes=[t1])
                hm = hm_ring[i % 2]
                P.op("pool", lambda e, t1=t1, hm=hm: e.tensor_tensor(out=hm.ap, in0=t1.ap, in1=B2_p.ap, op=ALU.add),
                     reads=[t1, B2_p], writes=[hm])
                pt, pb = ps1()
                pv = pt.ap.bitcast(BF16).rearrange("p (a b) -> p a b", a=8)
                for kc in range(8):
                    P.op("pe", lambda e, kc=kc, pv=pv, hm=hm: e.transpose(out=pv[:, kc, :], in_=hm.ap[:, kc * 128:(kc + 1) * 128],
                                                                        identity=ident_b), reads=[hm, cm_b], writes=pb)
                P.op("act", lambda e, pv=pv, t=t: e.copy(out=hffT.ap[:, :, t * 128:(t + 1) * 128], in_=pv), reads=pb, writes=[hffT])
                pfree(pt)
                xts.append(xt)
            gu = gu_ring[0]
            for f in range(NFC):
                pt, pb = ps1()
                for half in range(2):
                    col = half * DFF + f * 128
                    for kc in range(8):
                        P.op("pe", lambda e, kc=kc, pt=pt, half=half, col=col: e.matmul(pt.ap[:, half * 256:(half + 1) * 256],
                                                                                      lhsT=Wfi.ap[:, kc, col:col + 128], rhs=hffT.ap[:, kc, :],
                                                                                      start=(kc == 0), stop=(kc == 7)),
                             reads=[Wfi, hffT], writes=pb)
                sg = sg_ring[f % 3]
                P.op("act", lambda e, pt=pt, sg=sg: e.activation(out=sg.ap, in_=pt.ap[:, 0:256], func=AF.Silu), reads=pb, writes=[sg])
                P.op("dve", lambda e, pt=pt, sg=sg, f=f: e.tensor_tensor(out=gu.ap[:, f, :], in0=pt.ap[:, 256:512], in1=sg.ap, op=ALU.mult),
                     reads=pb + [sg], writes=[gu])
                pfree(pt)
            for t in range(2):
                c = 2 * gi + t
                xt = xts[t]
                x2 = x2_ring[0]
                for half in range(2):
                    pt, pb = ps1()
                    for f in range(NFC):
                        P.op("pe", lambda e, f=f, pt=pt, half=half, t=t: e.matmul(pt.ap, lhsT=gu.ap[:, f, t * 128:(t + 1) * 128],
                                                                                rhs=Wfo.ap[:, f, half * 512:(half + 1) * 512],
                                                                                start=(f == 0), stop=(f == NFC - 1)),
                             reads=[gu, Wfo], writes=pb)
                    P.op("dve", lambda e, pt=pt, half=half, xt=xt, x2=x2: e.tensor_tensor(out=x2.ap[:, half * 512:(half + 1) * 512], in0=pt.ap,
                                                                                        in1=xt.ap[:, half * 512:(half + 1) * 512], op=ALU.add),
                         reads=pb + [xt], writes=[x2])
                    pfree(pt)
                st = st3_ring[c % 2]
                oo = oo_ring[0]
                P.op("act", lambda e, x2=x2, st=st, oo=oo: e.activation(out=oo.ap, in_=x2.ap, func=AF.Square, accum_out=st.ap[:, 0:1]),
                     reads=[x2], writes=[oo, st])
                rstd_from_ssq(st.ap[:, 0:1], st, 1, D)
                P.op("dve", lambda e, x2=x2, st=st, oo=oo: e.scalar_tensor_tensor(out=oo.ap, in0=x2.ap, scalar=st.ap[:, 0:1], in1=gfin_t.ap,
                                                                                op0=ALU.mult, op1=ALU.mult), reads=[x2, st, gfin_t], writes=[oo])
                store(out[c * 128:(c + 1) * 128, :], out_b[c], oo)

        ffn_loads(0)
        for gi in range(NT // 2):
            ffn_group(gi)
        P.op("sp", None, reads=out_b)

    except StopBuild:
        P.op("sp", None, reads=[dbg_b])
    with contextlib.ExitStack() as stack:
        P.emit(stack)
    return nc


def _consts():
    s = np.arange(128)[:, None]
    t = np.arange(128)[None, :]
    ident = (s == t).astype(np.float32)
    J = (s + t == 127).astype(np.float32)
    U = (s <= t).astype(np.float32)
    negU = -U
    maskf = U * np.float32(0.25 / math.sqrt(128.0))
    cmat = np.concatenate([ident, J, U, negU, maskf], axis=1).astype(np.float32)
    slopes = (2.0 ** (-8.0 * (np.arange(8, dtype=np.float32) + 1.0) / 8.0)).astype(np.float32)
    eb = np.zeros((128, 3, 8, 128), np.float32)
    for kb in range(3):
        kpos = (kb - 1) * 128 + np.arange(128)[:, None]
        qpos = np.arange(128)[None, :]
        dist = np.abs(kpos - qpos).astype(np.float32)
        valid = dist <= 128
        for h in range(8):
            eb[:, kb, h, :] = np.where(valid, np.exp(-slopes[h] * dist), 0.0)
    return cmat, eb.reshape(128, -1)


def _rep(v):
    return np.ascontiguousarray(np.broadcast_to(np.asarray(v, np.float32).reshape(1, -1), (128, v.size)))


def _prep_inputs(x, c, w_mod, b_mod, g_norm1, w_in, conv_w, conv_b, b_gates, sink, g_attn_out, g_mlstm_out,
                 w_out, g_norm2, w_ffn_in, w_ffn_out, g_final):
    f32 = np.float32
    x = np.asarray(x, f32)
    w_in0 = np.asarray(w_in, f32)[0]
    cmat, ebt = _consts()
    ka = w_in0[:, 512:640].reshape(D, 2, 64)
    kdup = np.concatenate([ka, ka], axis=2).reshape(D, 256)
    w_main = np.ascontiguousarray(np.concatenate([w_in0[:, 0:512], kdup, w_in0[:, 640:768], w_in0[:, 768:2816]], axis=1))
    gcols = w_in0[:, 2816:2832]
    bgv = np.asarray(b_gates, f32)[0]
    cwv = np.asarray(conv_w, f32)[0]
    cbv = np.asarray(conv_b, f32)[0]
    shared = {
        "w_mod": np.ascontiguousarray(np.asarray(w_mod, f32)[0]),
        "bmod_b": _rep(np.asarray(b_mod, f32)[0]),
        "g1_b": _rep(np.asarray(g_norm1, f32)[0]),
        "g2_b": _rep(np.asarray(g_norm2, f32)[0]),
        "gfin_b": _rep(np.asarray(g_final, f32)),
        "w_main": w_main,
        "cb": np.ascontiguousarray(cbv.reshape(8, 128).T),
        "sink_b": _rep(np.asarray(sink, f32)[0]),
        "rowsc": None,
        "w_out": np.ascontiguousarray(np.asarray(w_out, f32)[0]),
        "w_fi": np.ascontiguousarray(np.asarray(w_ffn_in, f32)[0]),
        "w_fo": np.ascontiguousarray(np.asarray(w_ffn_out, f32)[0]),
        "cmat": cmat,
        "ebt": ebt,
    }
    ga = np.asarray(g_attn_out, f32)[0].reshape(4, 128).T
    gm = np.asarray(g_mlstm_out, f32)[0].reshape(4, 128).T
    shared["rowsc"] = np.ascontiguousarray(np.concatenate([ga, gm], axis=1))
    in_maps = []
    for r in range(8):
        b, j = r // 4, r % 4
        xs = x[b]

        def ext(q, flip):
            lo, hi = q * 2048 - 128, q * 2048 + 2176
            buf = np.zeros((2304, D), f32)
            a, z = max(lo, 0), min(hi, SEQ)
            buf[a - lo:z - lo] = xs[a:z]
            return buf[::-1] if flip else buf

        passes = [(q, False) for q in range(j)] + [(q, True) for q in range(3, j, -1)] + [(j, False), (j, True)]
        xpa = np.stack([ext(q, fl_) for (q, fl_) in passes]).astype(f32)
        wg = np.zeros((D, 5, 8), f32)
        bg = np.zeros((5, 8), f32)
        cwa = np.zeros((128, 5, 8, 3), f32)
        flags = np.zeros((24,), f32)
        for pi, (q, fl_) in enumerate(passes):
            o = 8 if fl_ else 0
            wg[:, pi, :] = gcols[:, o:o + 8]
            bg[pi] = bgv[o:o + 8]
            taps = cwv[::-1] if fl_ else cwv
            cwa[:, pi, :, :] = taps.reshape(3, 8, 128).transpose(2, 1, 0)
            vl, vr = (q > 0), (q < 3)
            if fl_:
                vl, vr = vr, vl
            flags[9 + pi] = float(vl)
            flags[14 + pi] = float(vr)
        dirs = [p[1] for p in passes[:3]]
        for s in range(3):
            flags[s] = 1.0 if (s > 0 and dirs[s] == dirs[s - 1]) else 0.0
        lastf = max([s for s in range(3) if not dirs[s]], default=None)
        lastb = max([s for s in range(3) if dirs[s]], default=None)
        if lastf is not None:
            flags[3 + lastf] = 1.0
        if lastb is not None:
            flags[6 + lastb] = 1.0
        m = dict(shared)
        m["xp"] = np.ascontiguousarray(xpa)
        m["cvec"] = np.ascontiguousarray(np.asarray(c, f32)[b].reshape(8, 128).T)
        m["wg"] = np.ascontiguousarray(wg.reshape(D, 40))
        m["bg_b"] = _rep(bg.reshape(-1))
        m["cw"] = np.ascontiguousarray(cwa.reshape(128, -1))
        m["flags"] = _rep(flags)
        in_maps.append(m)
    return in_maps


_NC_CACHE = []


def kernel(**inputs):
    in_maps = _prep_inputs(**inputs)
    if not _NC_CACHE:
        _NC_CACHE.append(build_program())
    nc = _NC_CACHE[0]
    res = run_bass_kernel_spmd(nc, in_maps, core_ids=list(range(8)))
    outs = [np.asarray(r["out"], np.float32) for r in res.results]
    full = np.stack(outs).reshape(2, 4 * 2048, D)
    return full
```

```python
import contextlib
import math
import os
import numpy as np
import concourse.bass as bass
import concourse.mybir as mybir
from concourse.bass_utils import run_bass_kernel_spmd

F32 = mybir.dt.float32
BF16 = mybir.dt.bfloat16
AF = mybir.ActivationFunctionType
ALU = mybir.AluOpType

D = 1024
SEQ = 8192
NT = 16
EPS = 1e-6
DFF = 2816
NFC = DFF // 128
QA, KA, VA, QM, KM, VM, OM = 0, 512, 768, 896, 1408, 1920, 2432
NWM = 2944
LN_HALF = math.log(0.5)
LN_EB = math.log(0.5 / math.sqrt(128.0))


class Buf:
    __slots__ = ("name", "w", "r")

    def __init__(self, name):
        self.name = name
        self.w = None
        self.r = []


class Op:
    __slots__ = ("eng", "fn", "deps", "order", "sem", "val", "signal", "isdma", "idx", "seg", "cost", "fin", "done")


class Tile:
    def __init__(self, ap, name):
        self.ap = ap
        self.buf = Buf(name)


class Prog:
    ENGS = ["pe", "act", "dve", "pool", "sp"]
    SAME_RAW = ("act", "dve", "pool")
    EPOCH = 20000
    COST = {"pe": 0.12, "act": 0.5, "dve": 0.5, "pool": 1.0, "sp": 0.05}
    WINDOW = 160

    def __init__(self, nc):
        self.nc = nc
        self.all = []
        self.seg = 0
        self.ndma_sems = {"sp": 40, "pool": 4, "act": 8}

    def op(self, eng, fn, reads=(), writes=(), dma=False, c=None):
        o = Op()
        o.eng, o.fn, o.isdma, o.signal = eng, fn, dma, dma
        o.sem = o.val = None
        o.idx, o.seg = len(self.all), self.seg
        o.cost = (2.5 if dma else (0.0 if fn is None else self.COST[eng])) if c is None else c
        deps = {}
        for t in reads:
            b = t.buf if isinstance(t, Tile) else t
            if b.w is not None:
                deps[id(b.w)] = (b.w, True)
        for t in writes:
            b = t.buf if isinstance(t, Tile) else t
            if b.w is not None and id(b.w) not in deps:
                deps[id(b.w)] = (b.w, False)
            for r in b.r:
                if id(r) not in deps:
                    deps[id(r)] = (r, False)
        keep, order = [], []
        for d, raw in deps.values():
            if d is o:
                continue
            order.append(d)
            if d.isdma or dma or d.eng != eng:
                keep.append(d)
            elif eng in self.SAME_RAW:
                keep.append(d)
        o.deps, o.order = keep, order
        for d in keep:
            d.signal = True
        for t in writes:
            b = t.buf if isinstance(t, Tile) else t
            b.w = o
            b.r = []
        for t in reads:
            b = t.buf if isinstance(t, Tile) else t
            b.r.append(o)
        self.all.append(o)
        return o

    def fence(self):
        self.seg += 1

    def _schedule(self, ops):
        pend = {e: [o for o in ops if o.eng == e] for e in self.ENGS}
        head = {e: 0 for e in self.ENGS}
        free = {e: 0.0 for e in self.ENGS}
        out = {e: [] for e in self.ENGS}
        for o in ops:
            o.done = False
        left = len(ops)
        while left:
            best = None
            for e in self.ENGS:
                lst = pend[e]
                h = head[e]
                while h < len(lst) and lst[h].done:
                    h += 1
                head[e] = h
                n = 0
                j = h
                while j < len(lst) and n < self.WINDOW:
                    o = lst[j]
                    j += 1
                    if o.done:
                        continue
                    n += 1
                    rdy = 0.0
                    ok = True
                    for d in o.order:
                        if d.seg == o.seg and not d.done:
                            ok = False
                            break
                        if d.seg == o.seg and d.fin > rdy:
                            rdy = d.fin
                    if not ok:
                        continue
                    st = max(rdy, free[e])
                    if best is None or st < best[0] - 1e-9 or (abs(st - best[0]) <= 1e-9 and o.idx < best[1].idx):
                        best = (st, o)
                    if st <= free[e] + 1e-9:
                        break
            st, o = best
            o.done = True
            if o.isdma:
                o.fin = st + o.cost
                free[o.eng] = st + 0.05
            else:
                o.fin = st + o.cost
                free[o.eng] = o.fin
            out[o.eng].append(o)
            left -= 1
        return out

    def emit(self, stack):
        nc = self.nc
        sems = {}

        def getsem(key):
            if key not in sems:
                sems[key] = stack.enter_context(nc.semaphore("s_%s" % "_".join(str(k) for k in key)))
            return sems[key]

        nseg = self.seg + 1
        final = {e: [] for e in self.ENGS}
        dma_last = {q: [None] * n for q, n in self.ndma_sems.items()}
        dma_cnt = {q: [0] * n for q, n in self.ndma_sems.items()}
        dma_rr = {q: 0 for q in self.ndma_sems}
        last_compute = {e: None for e in self.ENGS}
        for sg in range(nseg):
            ops = [o for o in self.all if o.seg == sg]
            if sg > 0:
                lasts = [o for o in last_compute.values() if o is not None]
                dmas = [o for q in dma_last for o in dma_last[q] if o is not None]
                for e in self.ENGS:
                    f = Op()
                    f.eng, f.fn, f.isdma, f.signal = e, None, False, False
                    f.sem = f.val = None
                    f.deps = list(lasts) + dmas
                    for d in f.deps:
                        d.signal = True
                    final[e].append(f)
            sched = self._schedule(ops)
            for e in self.ENGS:
                for o in sched[e]:
                    if o.isdma:
                        k = dma_rr[e]
                        dma_rr[e] = (k + 1) % self.ndma_sems[e]
                        prev = dma_last[e][k]
                        if prev is not None:
                            o.deps = o.deps + [prev]
                        dma_last[e][k] = o
                        dma_cnt[e][k] += 1
                        o.sem = ("dma", e, k)
                        o.val = 16 * dma_cnt[e][k]
                    elif o.fn is not None:
                        last_compute[e] = o
                    final[e].append(o)
        self.ops = final
        for e in self.ENGS:
            n = 0
            for o in self.ops[e]:
                if o.isdma or not o.signal:
                    continue
                assert o.fn is not None
                o.sem = ("eng", e, n // self.EPOCH)
                o.val = n % self.EPOCH + 1
                n += 1
        for e in self.ENGS:
            for o in self.ops[e]:
                if o.signal:
                    getsem(o.sem)
                for d in o.deps:
                    assert d.sem is not None
        block = stack.enter_context(nc.Block())

        def run(e, engobj):
            waited = {}
            for o in self.ops[e]:
                for d in o.deps:
                    if waited.get(d.sem, 0) < d.val:
                        engobj.wait_ge(sems[d.sem], d.val)
                        waited[d.sem] = d.val
                if o.fn is None:
                    continue
                ins = o.fn(engobj)
                if o.signal:
                    ins.then_inc(sems[o.sem], 16 if o.isdma else 1)

        @block.tensor
        def _(eng):
            run("pe", eng)

        @block.scalar
        def _(eng):
            run("act", eng)

        @block.vector
        def _(eng):
            run("dve", eng)

        @block.gpsimd
        def _(eng):
            run("pool", eng)

        @block.sync
        def _(eng):
            run("sp", eng)


class StopBuild(Exception):
    pass


class Arena:
    def __init__(self, nc, nbytes):
        self.t = nc.alloc_sbuf_tensor("arena", [128, nbytes // 4], F32)
        self.ap = self.t.ap()
        self.size = nbytes
        self.top = 0
        self.n = 0

    def mark(self):
        return self.top

    def release(self, m):
        self.top = m

    def tile(self, shape, dtype, name=None):
        esz = 4 if dtype == F32 else 2
        free = int(np.prod(shape[1:]))
        nb = (free * esz + 31) // 32 * 32
        assert self.top + nb <= self.size, ("SBUF arena overflow", name, self.top, nb)
        a = self.ap[:, self.top // 4:(self.top + nb) // 4]
        if dtype != F32:
            a = a.bitcast(dtype)
        a = a[:, 0:free]
        if len(shape) == 3:
            a = a.rearrange("p (a b) -> p a b", a=shape[1])
        elif len(shape) == 4:
            a = a.rearrange("p (a b c) -> p a b c", a=shape[1], b=shape[2])
        if shape[0] != 128:
            a = a[0:shape[0]]
        self.top += nb
        self.n += 1
        return Tile(a, name or ("t%d" % self.n))


def bc(ap, shape):
    return ap.broadcast_to(shape)


def build_program():
    nc = bass.Bass("TRN2", target_bir_lowering=False)

    def din(name, shape, dt=F32):
        return nc.dram_tensor(name, list(shape), dt, kind="ExternalInput").ap()

    xp = din("xp", [5, 18 * 128, D])
    cvec = din("cvec", [128, 8])
    w_mod = din("w_mod", [D, 6 * D])
    bmod_b = din("bmod_b", [128, 6 * D])
    g1_b = din("g1_b", [128, D])
    g2_b = din("g2_b", [128, D])
    gfin_b = din("gfin_b", [128, D])
    w_main = din("w_main", [D, NWM])
    wg = din("wg", [D, 40])
    bg_b = din("bg_b", [128, 40])
    cw = din("cw", [128, 5 * 8 * 3])
    cb = din("cb", [128, 8])
    sink_b = din("sink_b", [128, 8])
    rowsc = din("rowsc", [128, 8])
    w_out = din("w_out", [D, D])
    w_fi = din("w_fi", [D, 2 * DFF])
    w_fo = din("w_fo", [DFF, D])
    flags = din("flags", [128, 24])
    cmat = din("cmat", [128, 5 * 128])
    ebt = din("ebt", [128, 3 * 8 * 128])
    out = nc.dram_tensor("out", [NT * 128, D], F32, kind="ExternalOutput").ap()

    hf_s = nc.dram_tensor("hf_s", [NT, 128, 512], F32).ap()
    hb_s = nc.dram_tensor("hb_s", [NT, 128, 512], F32).ap()
    th_s = nc.dram_tensor("th_s", [NT, 128, 512], F32).ap()
    at_s = nc.dram_tensor("at_s", [NT, 128, 512], BF16).ap()
    x1_s = nc.dram_tensor("x1_s", [NT, 128, D], F32).ap()
    hf_b = [Buf("hf%d" % i) for i in range(NT)]
    hb_b = [Buf("hb%d" % i) for i in range(NT)]
    th_b = [Buf("th%d" % i) for i in range(NT)]
    at_b = [Buf("at%d" % i) for i in range(NT)]
    x1_b = [Buf("x1%d" % i) for i in range(NT)]
    out_b = [Buf("out%d" % i) for i in range(NT)]

    P = Prog(nc)
    A = Arena(nc, 207 * 1024)
    KSTOP = os.environ.get("KSTOP", "")
    dbg = nc.dram_tensor("dbg", [128, 4096], F32, kind="ExternalOutput").ap() if KSTOP else None
    dbg_b = Buf("dbg")
    dbg_off = [0]

    def dump(tile_or_ap, rd, ncols):
        if not KSTOP:
            return
        o = dbg_off[0]
        dbg_off[0] += ncols
        src = tile_or_ap.ap if isinstance(tile_or_ap, Tile) else tile_or_ap
        P.op("sp", lambda e: e.dma_start(out=dbg[:, o:o + ncols], in_=src), reads=rd, writes=[dbg_b], dma=True)

    def stop(tag):
        if KSTOP == tag:
            raise StopBuild()

    psum_t = nc.alloc_psum_tensor("psum", [128, 4096], F32)
    psum_ap = psum_t.ap()
    banks = [Tile(psum_ap[:, k * 512:(k + 1) * 512], "bank%d" % k) for k in range(8)]
    bank2 = [Tile(psum_ap[:, k * 1024:(k + 1) * 1024], "bankpair%d" % k) for k in range(4)]
    for k in range(4):
        bank2[k].bufs = [banks[2 * k].buf, banks[2 * k + 1].buf]
    rr = [0]
    busy = [False] * 8

    def ps1():
        for _ in range(8):
            k = rr[0] % 8
            rr[0] += 1
            if not busy[k]:
                busy[k] = True
                banks[k].held = [k]
                return banks[k], [banks[k].buf]
        raise RuntimeError("no free PSUM bank")

    def ps2():
        for _ in range(8):
            if rr[0] % 2:
                rr[0] += 1
            k = (rr[0] % 8) // 2
            rr[0] += 2
            if not busy[2 * k] and not busy[2 * k + 1]:
                busy[2 * k] = busy[2 * k + 1] = True
                bank2[k].held = [2 * k, 2 * k + 1]
                return bank2[k], bank2[k].bufs
        raise RuntimeError("no free PSUM bank pair")

    def pfree(pt):
        for k in pt.held:
            assert busy[k]
            busy[k] = False

    def load(dst, src, wr=None, rd=(), q="sp"):
        P.op(q, lambda e: e.dma_start(out=dst.ap if isinstance(dst, Tile) else dst, in_=src),
             reads=list(rd), writes=[dst] if wr is None else wr, dma=True)

    def store(dst_ap, dst_buf, src, q="sp"):
        P.op(q, lambda e: e.dma_start(out=dst_ap, in_=src.ap if isinstance(src, Tile) else src),
             reads=[src], writes=[dst_buf], dma=True)

    try:
        cm_f = A.tile([128, 5, 128], F32, "cmat")
        load(cm_f, cmat.rearrange("p (a b) -> p a b", a=5))
        ident_f, J_f, U_f, negU_f, maskf_f = (cm_f.ap[:, i, :] for i in range(5))
        ones_f = A.tile([128, 128], F32, "ones")
        P.op("pool", lambda e: e.memset(ones_f.ap, 1.0), writes=[ones_f])
        mhalf = A.tile([128, 8], F32, "mhalf")
        P.op("pool", lambda e: e.memset(mhalf.ap, -0.5), writes=[mhalf])
        cm_b = A.tile([128, 2, 128], BF16, "cmat_bf")
        P.op("dve", lambda e: e.tensor_copy(out=cm_b.ap[:, 0, :], in_=ident_f), reads=[cm_f], writes=[cm_b])
        P.op("dve", lambda e: e.tensor_copy(out=cm_b.ap[:, 1, :], in_=maskf_f), reads=[cm_f], writes=[cm_b])
        ident_b = cm_b.ap[:, 0, :]
        maskf_b = cm_b.ap[:, 1, :]
        fl = A.tile([128, 24], F32, "flags")
        load(fl, flags)
        cw_t = A.tile([128, 5, 8, 3], F32, "cw")
        load(cw_t, cw.rearrange("p (a b c) -> p a b c", a=5, b=8))
        cb_t = A.tile([128, 8], F32, "cb")
        load(cb_t, cb)
        bg_t = A.tile([128, 5, 8], F32, "bg")
        load(bg_t, bg_b.rearrange("p (a b) -> p a b", a=5))
        rs_t = A.tile([128, 8], F32, "rowsc")
        load(rs_t, rowsc)
        esink = A.tile([128, 8], F32, "esink")
        load(esink, sink_b)
        P.op("act", lambda e: e.activation(out=esink.ap, in_=esink.ap, func=AF.Exp), reads=[esink], writes=[esink])

        pass_mark = A.mark()
        A1_b = A.tile([128, D], F32, "A1b")
        B1_b = A.tile([128, D], F32, "B1b")
        mod_s = nc.dram_tensor("mod_s", [4, 128, D], F32).ap()
        mod_sb = [Buf("mods%d" % i) for i in range(4)]

        stage = [None, None]
        stg_i = [0]
        cast_eng = ["pool", "dve", "act"]

        def cast_block(src_ap, ncols, dst_ap, dst_tile, mode=None, arg=None, engs=cast_eng, q="sp"):
            i = stg_i[0]
            stg_i[0] += 1
            st = stage[i % len(stage)]
            eng = engs[i % len(engs)]
            load(st.ap[:, 0:ncols], src_ap, wr=[st], q=q)
            sap = st.ap[:, 0:ncols]
            real_tile = dst_tile
            dst_tile = Buf("cast%d" % i)
            if not hasattr(real_tile, "cast_bufs"):
                real_tile.cast_bufs = []
            real_tile.cast_bufs.append(dst_tile)
            if mode is None:
                if eng == "act":
                    P.op("act", lambda e: e.copy(out=dst_ap, in_=sap), reads=[st], writes=[dst_tile])
                else:
                    P.op(eng, lambda e: e.tensor_copy(out=dst_ap, in_=sap), reads=[st], writes=[dst_tile])
            elif mode == "colscale":
                eng2 = "pool" if eng == "act" else eng
                P.op(eng2, lambda e: e.tensor_tensor(out=dst_ap, in0=sap, in1=arg[0], op=ALU.mult),
                     reads=[st, arg[1]], writes=[dst_tile])
            elif mode == "rowcol":
                P.op("dve", lambda e: e.scalar_tensor_tensor(out=dst_ap, in0=sap, scalar=arg[0], in1=arg[1],
                                                             op0=ALU.mult, op1=ALU.mult),
                     reads=[st, arg[2], arg[3]], writes=[dst_tile])

        def join(tile):
            P.op("pe", None, reads=tile.cast_bufs, writes=[tile])

        Wb = A.tile([128, 8, NWM], BF16, "w_main_bf")
        wmv2 = w_main.rearrange("(kc p) n -> p kc n", p=128)
        Wg_f = A.tile([128, 8, 40], F32, "wg_f")
        load(Wg_f, wg.rearrange("(kc p) n -> p kc n", p=128))
        Wg = A.tile([128, 8, 40], BF16, "wg_bf")
        P.op("dve", lambda e: e.tensor_copy(out=Wg.ap, in_=Wg_f.ap), reads=[Wg_f], writes=[Wg])
        EB = A.tile([128, 3, 8, 128], F32, "EB")
        load(EB, ebt.rearrange("p (a b c) -> p a b c", a=3, b=8))
        m0 = A.mark()
        gate1_b = A.tile([128, D], F32, "gate1b")
        gate2_b = A.tile([128, D], F32, "gate2b")
        A2_b = A.tile([128, D], F32, "A2b")
        B2_b = A.tile([128, D], F32, "B2b")
        mod_dst = [B1_b, A1_b, gate1_b, B2_b, A2_b, gate2_b]
        c_t = A.tile([128, 8], F32, "c")
        load(c_t, cvec)
        th_c = A.tile([128, 8], F32, "thc")
        sc_t = A.tile([128, 8], F32, "sc")
        P.op("act", lambda e: e.activation(out=th_c.ap, in_=c_t.ap, func=AF.Tanh, scale=0.5), reads=[c_t], writes=[th_c])
        P.op("dve", lambda e: e.scalar_tensor_tensor(out=sc_t.ap, in0=th_c.ap, scalar=1.0, in1=c_t.ap,
                                                     op0=ALU.add, op1=ALU.mult), reads=[th_c, c_t], writes=[sc_t])
        P.op("dve", lambda e: e.tensor_scalar(out=sc_t.ap, in0=sc_t.ap, scalar1=0.5, scalar2=None, op0=ALU.mult),
             reads=[sc_t], writes=[sc_t])
        scB = A.tile([128, 8, 128], F32, "scB")
        P.op("dve", lambda e: e.tensor_copy(out=scB.ap, in_=bc(sc_t.ap.unsqueeze(2), [128, 8, 128])),
             reads=[sc_t], writes=[scB])
        wm_ring = [A.tile([128, 8, 512], F32, "wmod%d" % i) for i in range(3)]
        bm_ring = [A.tile([128, 512], F32, "bmod%d" % i) for i in range(3)]
        wmv = w_mod.rearrange("(kc p) n -> p kc n", p=128)
        for nb in range(12):
            wt = wm_ring[nb % 3]
            bt = bm_ring[nb % 3]
            load(wt, wmv[:, :, nb * 512:(nb + 1) * 512])
            load(bt, bmod_b[:, nb * 512:(nb + 1) * 512])
            pt, pb = ps1()
            for kc in range(8):
                P.op("pe", lambda e, kc=kc, wt=wt, pt=pt: e.matmul(pt.ap, lhsT=scB.ap[:, kc, :], rhs=wt.ap[:, kc, :],
                                                                 start=(kc == 0), stop=(kc == 7)),
                     reads=[scB, wt], writes=pb)
            dst = mod_dst[nb // 2].ap[:, (nb % 2) * 512:(nb % 2 + 1) * 512]
            P.op("dve", lambda e, pt=pt, bt=bt, dst=dst: e.tensor_tensor(out=dst, in0=pt.ap, in1=bt.ap, op=ALU.add),
                 reads=pb + [bt], writes=[mod_dst[nb // 2]])
            pfree(pt)
        for (At, gsrc) in ((A1_b, g1_b), (A2_b, g2_b)):
            gt = A.tile([128, D], F32, "gtmp")
            load(gt, gsrc)
            P.op("dve", lambda e, At=At, gt=gt: e.scalar_tensor_tensor(out=At.ap, in0=At.ap, scalar=1.0, in1=gt.ap,
                                                                      op0=ALU.add, op1=ALU.mult),
                 reads=[At, gt], writes=[At])
        for i_, t_ in enumerate((gate1_b, gate2_b, A2_b, B2_b)):
            store(mod_s[i_], mod_sb[i_], t_)
        if KSTOP == "mod":
            dump(A1_b, [A1_b], 1024)
            dump(B1_b, [B1_b], 1024)
            dump(gate2_b, [gate2_b], 1024)
        stage[:] = [A.tile([128, 1472], F32, "stage%d" % i) for i in range(2)]
        for kc in range(8):
            for c0 in range(0, NWM, 1472):
                cast_block(wmv2[:, kc, c0:c0 + 1472], 1472, Wb.ap[:, kc, c0:c0 + 1472], Wb)
        join(Wb)
        stop("mod")
        A.release(m0)
        P.fence()
        stop("cast")

        xt_ring = [A.tile([128, D], F32, "xt%d" % i) for i in range(2)]
        plist = [4] if KSTOP.startswith("B_") else ([3, 4] if KSTOP.startswith("F_") else range(5))
        xseq = [(pi_, e_) for pi_ in plist for e_ in range(18)]
        xnext = [0]

        def ensure_x(upto):
            while xnext[0] <= min(upto, len(xseq) - 1):
                g = xnext[0]
                pi_, e_ = xseq[g]
                load(xt_ring[g % 2], xp[pi_, e_ * 128:(e_ + 1) * 128, :])
                xnext[0] += 1
        t1_ring = [A.tile([128, D], F32, "t1_%d" % i) for i in range(1)]
        hm_ring = [A.tile([128, D], BF16, "hm%d" % i) for i in range(2)]
        st_ring = [A.tile([128, 4], F32, "stat%d" % i) for i in range(4)]
        pair_ring = [A.tile([128, 8, 258], BF16, "pair%d" % i) for i in range(4)]
        halo_h = [A.tile([128, 8, 128], BF16, "haloh%d" % i) for i in range(2)]
        cv_ring = [A.tile([128, 3, 256], F32, "cv%d" % i) for i in range(2)]
        kT_ring = [A.tile([128, 4, 256], BF16, "kT%d" % i) for i in range(2)]
        qT_ring = [A.tile([128, 4, 256], BF16, "qT%d" % i) for i in range(2)]
        ktok_ring = [A.tile([128, 2, 512], BF16, "ktok%d" % i) for i in range(1)]
        vaug_ring = [A.tile([128, 2, 4, 129], BF16, "vaug%d" % i) for i in range(2)]
        wv_ring = [A.tile([128, 2, 4, 129], BF16, "wv%d" % i) for i in range(1)]
        for t in vaug_ring:
            P.op("pool", lambda e, t=t: e.memset(t.ap, 1.0), writes=[t])
        g_ring = [A.tile([128, 8, 2, 4], F32, "g%d" % i) for i in range(2)]
        nlfB_ring = [A.tile([128, 8, 128], F32, "nlfB%d" % i) for i in range(1)]
        E_ring = [A.tile([128, 4, 128], F32, "E%d" % i) for i in range(1)]
        Em_ring = [A.tile([128, 4, 128], BF16, "Em%d" % i) for i in range(2)]
        PT_ring = [A.tile([128, 4, 128], BF16, "PT%d" % i) for i in range(2)]
        tB_ring = [A.tile([128, 4, 129], F32, "tB%d" % i) for i in range(1)]
        R_ring = [A.tile([128, 4, 129], F32, "R%d" % i) for i in range(1)]
        dn_ring = [A.tile([128, 8], F32, "dn%d" % i) for i in range(2)]
        ho_ring = [A.tile([128, 4, 128], F32, "ho%d" % i) for i in range(2)]
        Cst = A.tile([128, 4, 129], F32, "Cstate")
        Cf = A.tile([128, 4, 129], F32, "Cf")
        Cb = A.tile([128, 4, 129], F32, "Cb")
        Cbf = A.tile([128, 4, 129], BF16, "Cbf")
        for t in (Cst, Cf, Cb):
            P.op("pool", lambda e, t=t: e.memset(t.ap, 0.0), writes=[t])
        qa_ring = [A.tile([128, 2, 4, 256], BF16, "qa%d" % i) for i in range(2)]
        for t in qa_ring:
            P.op("pool", lambda e, t=t: e.memset(t.ap, 0.0), writes=[t])
        kaT = A.tile([128, 2, 18 * 128], BF16, "kaT")
        vaa = A.tile([128, 18, 2, 65], BF16, "vaa")
        kaT_b = [Buf("kaT%d" % i) for i in range(18)]
        vaa_b = [Buf("vaa%d" % i) for i in range(18)]
        P.op("pool", lambda e: e.memset(vaa.ap, 1.0), writes=vaa_b)
        Ex_ring = [A.tile([128, 8, 128], F32, "Ex%d" % i) for i in range(1)]
        PA_ring = [A.tile([128, 8, 128], BF16, "PA%d" % i) for i in range(3)]
        att_ring = [A.tile([128, 8, 64], F32, "att%d" % i) for i in range(2)]
        attn_ring = [A.tile([128, 512], BF16, "attn%d" % i) for i in range(2)]
        attT_ring = [A.tile([128, 4, 128], BF16, "attT%d" % i) for i in range(2)]
        tho_ring = [A.tile([128, 512], F32, "tho%d" % i) for i in range(2)]
        cnt = {"tile": 0, "pair": 0, "chunk": 0, "att": 0, "cv": 0, "xg": 0}

        def rstd_from_ssq(ssq_ap, ssq_tile, n, width):
            P.op("dve", lambda e: e.tensor_scalar(out=ssq_ap, in0=ssq_ap, scalar1=1.0 / width, scalar2=EPS,
                                                  op0=ALU.mult, op1=ALU.add), reads=[ssq_tile], writes=[ssq_tile], c=0.15)
            P.op("pool", lambda e: e.tensor_tensor(out=ssq_ap, in0=ssq_ap, in1=mhalf.ap[:, 0:n], op=ALU.pow),
                 reads=[ssq_tile, mhalf], writes=[ssq_tile], c=0.5)

        def make_hmix_tile(src_ap, src_reads, Ab, Bb, flag_ap=None):
            i = cnt["tile"]
            cnt["tile"] += 1
            g = cnt["xg"]
            cnt["xg"] += 1
            ensure_x(g + 1)
            xt = xt_ring[g % 2]
            st = st_ring[i % 4]
            hm = hm_ring[i % 2]
            P.op("act", lambda e: e.activation(out=hm.ap, in_=xt.ap, func=AF.Square, accum_out=st.ap[:, 0:1]),
                 reads=[xt], writes=[hm, st], c=1.05)
            rstd_from_ssq(st.ap[:, 0:1], st, 1, D)
            t1 = t1_ring[0]
            P.op("dve", lambda e: e.scalar_tensor_tensor(out=t1.ap, in0=xt.ap, scalar=st.ap[:, 0:1], in1=Ab.ap,
                                                         op0=ALU.mult, op1=ALU.mult), reads=[xt, st, Ab], writes=[t1], c=1.3)
            P.op("pool", lambda e: e.tensor_tensor(out=hm.ap, in0=t1.ap, in1=Bb.ap, op=ALU.add),
                 reads=[t1, Bb], writes=[hm], c=2.4)
            if flag_ap is not None:
                P.op("pool", lambda e: e.tensor_scalar(out=hm.ap, in0=hm.ap, scalar1=flag_ap, scalar2=None, op0=ALU.mult),
                     reads=[hm, fl], writes=[hm])
            pt, pb = ps1()
            pv = pt.ap.bitcast(BF16).rearrange("p (a b) -> p a b", a=8)
            for kc in range(8):
                P.op("pe", lambda e, kc=kc: e.transpose(out=pv[:, kc, :], in_=hm.ap[:, kc * 128:(kc + 1) * 128],
                                                        identity=ident_b), reads=[hm, cm_b], writes=pb)
            return pv, pb, pt

        def run_pass(pi, mode):
            outs = mode in ("F", "B")
            full = mode == "F"
            sc_dst, sc_buf = (hf_s, hf_b) if mode == "F" else (hb_s, hb_b)

            def feat_proj(col0, hp, n, off=0):
                pt, pb = ps1()
                for kc in range(8):
                    P.op("pe", lambda e, kc=kc: e.matmul(pt.ap[:, 0:n], lhsT=Wb.ap[:, kc, col0:col0 + 128],
                                                         rhs=hp.ap[:, kc, off:off + n], start=(kc == 0), stop=(kc == 7)),
                         reads=[Wb, hp], writes=pb, c=n / 2000.0)
                return pt, pb

            def tok_proj(col0, ncols, lhs_ap, lhs_tile, wt=None):
                wt = Wb if wt is None else wt
                pt, pb = ps1()
                for kc in range(8):
                    P.op("pe", lambda e, kc=kc: e.matmul(pt.ap[:, 0:ncols], lhsT=lhs_ap[:, kc, :],
                                                         rhs=wt.ap[:, kc, col0:col0 + ncols], start=(kc == 0), stop=(kc == 7)),
                         reads=[wt, lhs_tile], writes=pb, c=max(0.04, ncols / 2000.0))
                return pt, pb

            def conv_silu(pt, pb, chunk, dst_ap, dst_tile):
                i = cnt["cv"]
                cnt["cv"] += 1
                cv = cv_ring[i % 2]
                w0, w1, w2 = (cw_t.ap[:, pi, chunk, k:k + 1] for k in range(3))
                P.op("act", lambda e: e.activation(out=cv.ap[:, 0, :], in_=pt.ap[:, 0:256], func=AF.Identity,
                                                   scale=w0, bias=cb_t.ap[:, chunk:chunk + 1]),
                     reads=pb + [cw_t, cb_t], writes=[cv])
                P.op("dve", lambda e: e.scalar_tensor_tensor(out=cv.ap[:, 1, :], in0=pt.ap[:, 1:257], scalar=w1,
                                                             in1=cv.ap[:, 0, :], op0=ALU.mult, op1=ALU.add),
                     reads=pb + [cw_t, cv], writes=[cv])
                P.op("dve", lambda e: e.scalar_tensor_tensor(out=cv.ap[:, 0, :], in0=pt.ap[:, 2:258], scalar=w2,
                                                             in1=cv.ap[:, 1, :], op0=ALU.mult, op1=ALU.add),
                     reads=pb + [cw_t, cv], writes=[cv])
                pfree(pt)
                P.op("act", lambda e: e.activation(out=cv.ap[:, 2, :], in_=cv.ap[:, 0, :], func=AF.Tanh, scale=0.5),
                     reads=[cv], writes=[cv])
                P.op("dve", lambda e: e.scalar_tensor_tensor(out=dst_ap, in0=cv.ap[:, 2, :], scalar=1.0,
                                                             in1=cv.ap[:, 0, :], op0=ALU.add, op1=ALU.mult),
                     reads=[cv], writes=[dst_tile])

            def ka_va_proj(e_idx, lhs_ap, lhs_tile, ntok_off=None, hp=None):
                for g in range(2):
                    pt, pb = ps1()
                    for kc in range(8):
                        P.op("pe", lambda e, kc=kc, g=g, pt=pt: e.matmul(pt.ap[:, 0:128], lhsT=Wb.ap[:, kc, KA + g * 128:KA + (g + 1) * 128],
                                                                  rhs=lhs_ap[:, kc, :], start=(kc == 0), stop=(kc == 7)),
                             reads=[Wb, lhs_tile], writes=pb)
                    P.op("act", lambda e, g=g, pt=pt: e.copy(out=kaT.ap[:, g, e_idx * 128:(e_idx + 1) * 128], in_=pt.ap[:, 0:128]),
                         reads=pb, writes=[kaT_b[e_idx]])
                    pfree(pt)
                pt, pb = tok_proj(VA, 128, lhs_ap, lhs_tile)
                P.op("dve", lambda e: e.tensor_copy(out=vaa.ap[:, e_idx, :, 0:64],
                                                    in_=pt.ap[:, 0:128].rearrange("p (g d) -> p g d", g=2)),
                     reads=pb, writes=[vaa_b[e_idx]])
                pfree(pt)

            def attention_block(n):
                i = cnt["att"]
                cnt["att"] += 1
                qa = qa_ring[(n // 2) % 2]
                qc = (n % 2) * 128
                PAs = []
                for kb in range(3):
                    ek = n + kb
                    pt, pb = ps2()
                    pv = pt.ap.rearrange("p (h q) -> p h q", h=8)
                    for h in range(8):
                        par, hp4, g = h % 2, h // 2, h // 4
                        P.op("pe", lambda e, h=h, par=par, hp4=hp4, g=g, pv=pv, ek=ek: e.matmul(
                            pv[:, h, :], lhsT=kaT.ap[:, g, ek * 128:(ek + 1) * 128],
                            rhs=qa.ap[:, par, hp4, qc:qc + 128], start=True, stop=True),
                            reads=[kaT_b[ek], qa], writes=pb)
                    Ex = Ex_ring[0]
                    for hh in range(2):
                        P.op("act", lambda e, hh=hh, pv=pv, Ex=Ex: e.activation(out=Ex.ap[:, hh * 4:hh * 4 + 4, :],
                                                                              in_=pv[:, hh * 4:hh * 4 + 4, :],
                                                                              func=AF.Exp, scale=0.125),
                             reads=pb, writes=[Ex])
                    pfree(pt)
                    PA = PA_ring[kb]
                    P.op("pool", lambda e, Ex=Ex, PA=PA, kb=kb: e.tensor_tensor(out=PA.ap, in0=Ex.ap, in1=EB.ap[:, kb, :, :],
                                                                             op=ALU.mult), reads=[Ex, EB], writes=[PA], c=2.0)
                    PAs.append(PA)
                    yield
                pto, pb = ps2()
                po = pto.ap.rearrange("p (h q) -> p h q", h=8)
                for h in range(8):
                    g = h // 4
                    for kb in range(3):
                        ek = n + kb
                        P.op("pe", lambda e, h=h, g=g, kb=kb, ek=ek, po=po: e.matmul(
                            po[:, h, 0:65], lhsT=PAs[kb].ap[:, h, :], rhs=vaa.ap[:, ek, g, :],
                            start=(kb == 0), stop=(kb == 2)), reads=[PAs[kb], vaa_b[ek]], writes=pb)
                yield
                dn = dn_ring[i % 2]
                P.op("dve", lambda e: e.tensor_tensor(out=dn.ap, in0=po[:, :, 64], in1=esink.ap, op=ALU.add),
                     reads=pb + [esink], writes=[dn])
                P.op("dve", lambda e: e.reciprocal(out=dn.ap, in_=dn.ap), reads=[dn], writes=[dn])
                att = att_ring[i % 2]
                P.op("dve", lambda e: e.tensor_tensor(out=att.ap, in0=po[:, :, 0:64],
                                                      in1=bc(dn.ap.unsqueeze(2), [128, 8, 64]), op=ALU.mult),
                     reads=pb + [dn], writes=[att])
                pfree(pto)
                st = st_ring[cnt["tile"] % 4]
                cnt["tile"] += 1
                attn = attn_ring[i % 2]
                P.op("act", lambda e: e.activation(out=attn.ap, in_=att.ap.rearrange("p h d -> p (h d)"), func=AF.Square,
                                                   accum_out=st.ap[:, 0:1]), reads=[att], writes=[attn, st])
                rstd_from_ssq(st.ap[:, 0:1], st, 1, 512)
                P.op("act", lambda e: e.activation(out=attn.ap, in_=att.ap.rearrange("p h d -> p (h d)"), func=AF.Identity,
                                                   scale=st.ap[:, 0:1]), reads=[att, st], writes=[attn])
                pt, pb = ps1()
                pv = pt.ap.bitcast(BF16)[:, 0:512].rearrange("p (a b) -> p a b", a=4)
                for kc in range(4):
                    P.op("pe", lambda e, kc=kc: e.transpose(out=pv[:, kc, :], in_=attn.ap[:, kc * 128:(kc + 1) * 128],
                                                            identity=ident_b), reads=[attn, cm_b], writes=pb)
                aT = attT_ring[i % 2]
                P.op("act", lambda e: e.copy(out=aT.ap, in_=pv), reads=pb, writes=[aT])
                pfree(pt)
                store(at_s[n].rearrange("p (a b) -> p a b", a=4), at_b[n], aT)
                yield

            def gen_A(u):
                hp = pair_ring[u % 4]
                kT = kT_ring[u % 2]
                qT = qT_ring[u % 2]
                vaug = vaug_ring[u % 2]
                gt = g_ring[u % 2]
                for h in range(4):
                    pt, pb = feat_proj(KM + h * 128, hp, 258)
                    conv_silu(pt, pb, 4 + h, kT.ap[:, h, :], kT)
                    yield
                if outs:
                    for h in range(4):
                        pt, pb = feat_proj(QM + h * 128, hp, 258)
                        conv_silu(pt, pb, h, qT.ap[:, h, :], qT)
                        yield
                ptg, pbg = ps1()
                gv = ptg.ap[:, 0:16].rearrange("p (c g) -> p c g", c=2)
                for ci in range(2):
                    lhs = hp.ap[:, :, 1 + ci * 128:1 + (ci + 1) * 128]
                    pt, pb = tok_proj(VM, 512, lhs, hp)
                    P.op("act", lambda e, ci=ci, pt=pt: e.copy(out=vaug.ap[:, ci, :, 0:128],
                                                             in_=pt.ap.rearrange("p (h d) -> p h d", h=4)),
                         reads=pb, writes=[vaug])
                    pfree(pt)
                    for kc in range(8):
                        P.op("pe", lambda e, kc=kc, ci=ci, lhs=lhs: e.matmul(gv[:, ci, :], lhsT=lhs[:, kc, :],
                                                                           rhs=Wg.ap[:, kc, pi * 8:pi * 8 + 8],
                                                                           start=(kc == 0), stop=(kc == 7)),
                             reads=[Wg, hp], writes=pbg, c=0.03)
                    yield
                    if full:
                        pt, pb = tok_proj(OM, 512, lhs, hp)
                        c = 2 * u + ci
                        tho = tho_ring[c % 2]
                        P.op("act", lambda e, pt=pt, tho=tho: e.activation(out=tho.ap, in_=pt.ap, func=AF.Tanh, scale=0.5),
                             reads=pb, writes=[tho])
                        pfree(pt)
                        store(th_s[c], th_b[c], tho)
                        yield
                gi, gf = gt.ap[:, 0, :, :], gt.ap[:, 1, :, :]
                bgi = bc(bg_t.ap[:, pi, 0:4].unsqueeze(1), [128, 2, 4])
                bgf = bc(bg_t.ap[:, pi, 4:8].unsqueeze(1), [128, 2, 4])
                P.op("dve", lambda e: e.tensor_tensor(out=gi, in0=gv[:, :, 0:4], in1=bgi, op=ALU.add),
                     reads=pbg + [bg_t], writes=[gt], c=0.15)
                P.op("dve", lambda e: e.tensor_tensor(out=gf, in0=gv[:, :, 4:8], in1=bgf, op=ALU.add),
                     reads=pbg + [bg_t], writes=[gt], c=0.1)
                pfree(ptg)
                yield
                if full:
                    qa = qa_ring[u % 2]
                    for hp4 in range(4):
                        pt, pb = feat_proj(QA + hp4 * 128, hp, 256, off=1)
                        P.op("dve", lambda e, hp4=hp4, pt=pt: e.tensor_copy(out=qa.ap[0:64, 0, hp4, :], in_=pt.ap[0:64, 0:256]),
                             reads=pb, writes=[qa])
                        P.op("dve", lambda e, hp4=hp4, pt=pt: e.tensor_copy(out=qa.ap[64:128, 1, hp4, :], in_=pt.ap[64:128, 0:256]),
                             reads=pb, writes=[qa])
                        pfree(pt)
                        yield
                    for ci in range(2):
                        ka_va_proj(2 * u + 1 + ci, hp.ap[:, :, 1 + ci * 128:1 + (ci + 1) * 128], hp)
                        yield

            def gen_B(u):
                kT = kT_ring[u % 2]
                qT = qT_ring[u % 2]
                vaug = vaug_ring[u % 2]
                gt = g_ring[u % 2]
                wv = wv_ring[0]
                ktok = ktok_ring[0]
                gi, gf = gt.ap[:, 0, :, :], gt.ap[:, 1, :, :]
                P.op("act", lambda e: e.activation(out=gt.ap[:, 2, :, :], in_=gf, func=AF.Exp, scale=-1.0),
                     reads=[gt], writes=[gt], c=0.12)
                P.op("act", lambda e: e.activation(out=gf, in_=gt.ap[:, 2, :, :], func=AF.Ln, bias=1.0),
                     reads=[gt], writes=[gt], c=0.2)
                nlf2 = gt.ap[:, 1, :, :].rearrange("p c h -> p (c h)")
                ptc, pbc = ps1()
                P.op("pe", lambda e: e.matmul(ptc.ap[:, 0:8], lhsT=U_f, rhs=nlf2, start=True, stop=True),
                     reads=[cm_f, gt], writes=pbc, c=0.1)
                P.op("pe", lambda e: e.matmul(ptc.ap[:, 8:16], lhsT=ones_f.ap, rhs=nlf2, start=True, stop=True),
                     reads=[ones_f, gt], writes=pbc, c=0.1)
                Pn = ptc.ap[:, 0:8].rearrange("p (c h) -> p c h", c=2)
                Tn = ptc.ap[:, 8:16].rearrange("p (c h) -> p c h", c=2)
                biasE = gt.ap[:, 2, :, :]
                P.op("dve", lambda e: e.tensor_tensor(out=biasE, in0=Pn, in1=gi, op=ALU.add), reads=pbc + [gt], writes=[gt], c=0.1)
                P.op("dve", lambda e: e.scalar_tensor_tensor(out=gt.ap[:, 3, :, :], in0=biasE, scalar=LN_HALF, in1=Tn,
                                                             op0=ALU.add, op1=ALU.subtract), reads=pbc + [gt], writes=[gt], c=0.15)
                wq = gt.ap[:, 4, :, :]
                ebq = gt.ap[:, 5, :, :]
                dcy = gt.ap[:, 6, :, :]
                lneb = gt.ap[:, 7, :, :]
                P.op("pool", lambda e: e.memset(lneb, LN_EB), writes=[gt], c=0.1)
                P.op("act", lambda e: e.activation(out=wq, in_=gt.ap[:, 3, :, :], func=AF.Exp), reads=[gt], writes=[gt], c=0.2)
                P.op("act", lambda e: e.activation(out=ebq.rearrange("p c h -> p (c h)"), in_=ptc.ap[:, 0:8], func=AF.Exp,
                                                   scale=-1.0, bias=lneb[:, 0, 0:1]), reads=pbc + [gt], writes=[gt], c=0.27)
                P.op("act", lambda e: e.activation(out=dcy.rearrange("p c h -> p (c h)"), in_=ptc.ap[:, 8:16], func=AF.Exp,
                                                   scale=-1.0), reads=pbc, writes=[gt], c=0.1)
                pfree(ptc)
                nlfB = nlfB_ring[0]
                if outs:
                    P.op("pool", lambda e: e.tensor_copy(out=nlfB.ap, in_=bc(nlf2.unsqueeze(2), [128, 8, 128])),
                         reads=[gt], writes=[nlfB])
                yield
                for ci in range(2):
                    pt, pb = ps1()
                    pv = pt.ap.bitcast(BF16)[:, 0:512].rearrange("p (a b) -> p a b", a=4)
                    for h in range(4):
                        P.op("pe", lambda e, h=h, ci=ci, pv=pv: e.transpose(out=pv[:, h, :], in_=kT.ap[:, h, ci * 128:(ci + 1) * 128],
                                                                          identity=ident_b), reads=[kT, cm_b], writes=pb)
                    P.op("act", lambda e, ci=ci, pv=pv: e.copy(out=ktok.ap[:, ci, :], in_=pv.rearrange("p a b -> p (a b)")),
                         reads=pb, writes=[ktok])
                    pfree(pt)
                    P.op("pool", lambda e, ci=ci: e.tensor_tensor(out=wv.ap[:, ci, :, :], in0=vaug.ap[:, ci, :, :],
                                                                in1=bc(wq[:, ci, :].unsqueeze(2), [128, 4, 129]), op=ALU.mult),
                         reads=[vaug, gt], writes=[wv])
                    yield
                for ci in range(2):
                    c = 2 * u + ci
                    k = c
                    cs = slice(ci * 128, (ci + 1) * 128)
                    if outs:
                        ptr, pbr = ps1()
                        rv = ptr.ap.rearrange("p (h t) -> p h t", h=4)
                        for h in range(4):
                            P.op("pe", lambda e, h=h, ci=ci, rv=rv: e.matmul(rv[:, h, :], lhsT=nlfB.ap[:, ci * 4 + h, :], rhs=negU_f,
                                                                           start=True, stop=True), reads=[nlfB, cm_f], writes=pbr)
                        E = E_ring[0]
                        for h in range(4):
                            P.op("act", lambda e, h=h, ci=ci, rv=rv, E=E: e.activation(out=E.ap[:, h, :], in_=rv[:, h, :], func=AF.Exp,
                                                                                    bias=biasE[:, ci, h:h + 1]),
                                 reads=pbr + [gt], writes=[E])
                        pfree(ptr)
                        Em = Em_ring[k % 2]
                        P.op("pool", lambda e, E=E, Em=Em: e.tensor_tensor(out=Em.ap, in0=E.ap,
                                                                           in1=bc(maskf_f.unsqueeze(1), [128, 4, 128]), op=ALU.mult),
                             reads=[E, cm_f], writes=[Em])
                        yield
                        pts, pbs = ps1()
                        sv = pts.ap.rearrange("p (h t) -> p h t", h=4)
                        for h in range(4):
                            P.op("pe", lambda e, h=h, cs=cs, sv=sv: e.matmul(sv[:, h, :], lhsT=kT.ap[:, h, cs], rhs=qT.ap[:, h, cs],
                                                                           start=True, stop=True), reads=[kT, qT], writes=pbs)
                        PT = PT_ring[k % 2]
                        P.op("dve", lambda e, sv=sv, Em=Em, PT=PT: e.tensor_tensor(out=PT.ap, in0=sv, in1=Em.ap, op=ALU.mult),
                             reads=pbs + [Em], writes=[PT])
                        pfree(pts)
                        yield
                        ptb, pbb = ps2()
                        bv = ptb.ap.rearrange("p (h n) -> p h n", h=4)
                        for h in range(4):
                            P.op("pe", lambda e, h=h, cs=cs, bv=bv: e.matmul(bv[:, h, 0:129], lhsT=qT.ap[:, h, cs], rhs=Cbf.ap[:, h, :],
                                                                           start=True, stop=True), reads=[qT, Cbf], writes=pbb)
                        tB = tB_ring[0]
                        P.op("dve", lambda e, ci=ci, bv=bv, tB=tB: e.tensor_tensor(out=tB.ap, in0=bv[:, :, 0:129],
                                                                                 in1=bc(ebq[:, ci, :].unsqueeze(2), [128, 4, 129]),
                                                                                 op=ALU.mult), reads=pbb + [gt], writes=[tB])
                        pfree(ptb)
                        pta, pba = ps2()
                        av = pta.ap.rearrange("p (h n) -> p h n", h=4)
                        for h in range(4):
                            P.op("pe", lambda e, h=h, ci=ci, av=av, PT=PT: e.matmul(av[:, h, 0:129], lhsT=PT.ap[:, h, :],
                                                                                  rhs=vaug.ap[:, ci, h, :], start=True, stop=True),
                                 reads=[PT, vaug], writes=pba)
                        R = R_ring[0]
                        P.op("dve", lambda e, av=av, tB=tB, R=R: e.tensor_tensor(out=R.ap, in0=av[:, :, 0:129], in1=tB.ap, op=ALU.add),
                             reads=pba + [tB], writes=[R])
                        pfree(pta)
                        yield
                    ptk, pbk = ps2()
                    kv = ptk.ap.rearrange("p (h n) -> p h n", h=4)
                    for h in range(4):
                        P.op("pe", lambda e, h=h, ci=ci, kv=kv: e.matmul(kv[:, h, 0:129], lhsT=ktok.ap[:, ci, h * 128:(h + 1) * 128],
                                                                       rhs=wv.ap[:, ci, h, :], start=True, stop=True),
                             reads=[ktok, wv], writes=pbk)
                    P.op("pool", lambda e, ci=ci: e.tensor_tensor(out=Cst.ap, in0=Cst.ap, in1=bc(dcy[:, ci, :].unsqueeze(2), [128, 4, 129]),
                                                                 op=ALU.mult), reads=[Cst, gt], writes=[Cst], c=1.0)
                    P.op("dve", lambda e, kv=kv: e.tensor_tensor(out=Cst.ap, in0=kv[:, :, 0:129], in1=Cst.ap, op=ALU.add),
                         reads=pbk + [Cst], writes=[Cst])
                    pfree(ptk)
                    if outs:
                        P.op("act", lambda e: e.copy(out=Cbf.ap, in_=Cst.ap), reads=[Cst], writes=[Cbf])
                        dn = dn_ring[k % 2]
                        P.op("act", lambda e, R=R, dn=dn: e.activation(out=dn.ap[:, 0:4], in_=R.ap[:, :, 128], func=AF.Abs),
                             reads=[R], writes=[dn])
                        P.op("dve", lambda e, dn=dn: e.tensor_scalar(out=dn.ap[:, 0:4], in0=dn.ap[:, 0:4], scalar1=1.0, scalar2=None,
                                                                     op0=ALU.max), reads=[dn], writes=[dn])
                        P.op("dve", lambda e, dn=dn: e.reciprocal(out=dn.ap[:, 0:4], in_=dn.ap[:, 0:4]), reads=[dn], writes=[dn])
                        ho = ho_ring[k % 2]
                        P.op("dve", lambda e, R=R, dn=dn, ho=ho: e.tensor_tensor(out=ho.ap, in0=R.ap[:, :, 0:128],
                                                                               in1=bc(dn.ap[:, 0:4].unsqueeze(2), [128, 4, 128]),
                                                                               op=ALU.mult), reads=[R, dn], writes=[ho])
                        store(sc_dst[c].rearrange("p (h d) -> p h d", h=4), sc_buf[c], ho)
                    yield

            def gen_C(u):
                if u >= 1:
                    yield from attention_block(2 * u - 1)
                yield from attention_block(2 * u)

            def gen_T(e_idx):
                flag_ap = None
                if e_idx == 0:
                    flag_ap = fl.ap[:, 9 + pi:10 + pi]
                elif e_idx == 17:
                    flag_ap = fl.ap[:, 14 + pi:15 + pi]
                pv, pb, ptt = make_hmix_tile(xp[pi, e_idx * 128:(e_idx + 1) * 128, :], [], A1_b, B1_b, flag_ap)
                yield
                if 1 <= e_idx <= 16:
                    u, half = (e_idx - 1) // 2, (e_idx - 1) % 2
                    dstt = pair_ring[u % 4]
                    P.op("act", lambda e, dstt=dstt, half=half, pv=pv: e.copy(out=dstt.ap[:, :, 1 + half * 128:1 + (half + 1) * 128], in_=pv),
                         reads=pb, writes=[dstt])
                    src_t, src_first, src_last = dstt, dstt.ap[:, :, 1 + half * 128:2 + half * 128], dstt.ap[:, :, 128 + half * 128:129 + half * 128]
                else:
                    hh = halo_h[0 if e_idx == 0 else 1]
                    P.op("act", lambda e, hh=hh, pv=pv: e.copy(out=hh.ap, in_=pv), reads=pb, writes=[hh])
                    src_t, src_first, src_last = hh, hh.ap[:, :, 0:1], hh.ap[:, :, 127:128]
                pfree(ptt)
                if e_idx % 2 == 1 and e_idx >= 3:
                    dstt = pair_ring[((e_idx - 3) // 2) % 4]
                    P.op("pool", lambda e, dstt=dstt, src_first=src_first: e.tensor_copy(out=dstt.ap[:, :, 257:258], in_=src_first),
                         reads=[src_t], writes=[dstt], c=0.2)
                if e_idx % 2 == 0 and e_idx <= 14:
                    dstt = pair_ring[(e_idx // 2) % 4]
                    P.op("pool", lambda e, dstt=dstt, src_last=src_last: e.tensor_copy(out=dstt.ap[:, :, 0:1], in_=src_last),
                         reads=[src_t], writes=[dstt], c=0.2)
                if full and e_idx in (0, 17):
                    ka_va_proj(e_idx, hh.ap, hh)
                    fa = fl.ap[:, 9 + pi:10 + pi] if e_idx == 0 else fl.ap[:, 14 + pi:15 + pi]
                    P.op("pool", lambda e, e_idx=e_idx, fa=fa: e.tensor_scalar(out=vaa.ap[:, e_idx, :, :], in0=vaa.ap[:, e_idx, :, :],
                                                                             scalar1=fa, scalar2=None, op0=ALU.mult),
                         reads=[vaa_b[e_idx], fl], writes=[vaa_b[e_idx]])
                yield

            def chain(*gens):
                for g_ in gens:
                    yield from g_

            def interleave(gens):
                gens = [g_ for g_ in gens if g_ is not None]
                while gens:
                    for g_ in list(gens):
                        try:
                            next(g_)
                        except StopIteration:
                            gens.remove(g_)

            return {"T": gen_T, "A": gen_A, "B": gen_B, "C": gen_C, "att": attention_block, "full": full}

        def chain(*gens):
            for g_ in gens:
                yield from g_

        def interleave(gens):
            gens = [g_ for g_ in gens if g_ is not None]
            while gens:
                for g_ in list(gens):
                    try:
                        next(g_)
                    except StopIteration:
                        gens.remove(g_)

        modes = ["slot", "slot", "slot", "F", "B"]
        objs = [run_pass(pi_, modes[pi_]) for pi_ in range(5)]

        def pre_state(p):
            if p < 3:
                P.op("dve", lambda e: e.tensor_scalar(out=Cst.ap, in0=Cst.ap, scalar1=fl.ap[:, p:p + 1], scalar2=None, op0=ALU.mult),
                     reads=[Cst, fl], writes=[Cst])
            else:
                Csrc = Cf if p == 3 else Cb
                P.op("dve", lambda e: e.tensor_copy(out=Cst.ap, in_=Csrc.ap), reads=[Csrc], writes=[Cst])
                P.op("act", lambda e: e.copy(out=Cbf.ap, in_=Cst.ap), reads=[Cst], writes=[Cbf])

        def post_state(p):
            if p < 3:
                P.op("dve", lambda e: e.scalar_tensor_tensor(out=Cf.ap, in0=Cst.ap, scalar=fl.ap[:, 3 + p:4 + p], in1=Cf.ap,
                                                             op0=ALU.mult, op1=ALU.add), reads=[Cst, fl, Cf], writes=[Cf])
                P.op("dve", lambda e: e.scalar_tensor_tensor(out=Cb.ap, in0=Cst.ap, scalar=fl.ap[:, 6 + p:7 + p], in1=Cb.ap,
                                                             op0=ALU.mult, op1=ALU.add), reads=[Cst, fl, Cb], writes=[Cb])

        def gT(p, es):
            return chain(*[objs[p]["T"](e_) for e_ in es if e_ <= 17])

        interleave([gT(0, range(5))])
        for p in range(5):
            o = objs[p]
            for k_ in range(8):
                streams = []
                if k_ >= 1:
                    streams.append(o["B"](k_ - 1))
                    if o["full"]:
                        streams.append(o["C"](k_ - 1))
                elif p >= 1:
                    prev = objs[p - 1]
                    streams.append(prev["B"](7))
                    if prev["full"]:
                        streams.append(chain(prev["C"](7), prev["att"](15)))
                streams.append(o["A"](k_))
                if k_ < 7:
                    streams.append(gT(p, (2 * k_ + 5, 2 * k_ + 6)))
                elif p < 4:
                    streams.append(gT(p + 1, range(5)))
                interleave(streams)
                if k_ == 0:
                    if p >= 1:
                        post_state(p - 1)
                    pre_state(p)
        interleave([objs[4]["B"](7)])

        A.release(pass_mark)
        P.fence()
        Wfi = A.tile([128, 8, 2 * DFF], BF16, "wfi_bf")
        Wfo = A.tile([128, NFC, D], BF16, "wfo_bf")
        p2_mark = A.mark()
        wout_b = A.tile([128, 8, D], BF16, "wout_bf")
        stage[:] = [A.tile([128, 704], F32, "stageB%d" % i) for i in range(4)]
        gate1_p = A.tile([128, D], F32, "gate1p")
        gate2_p = A.tile([128, D], F32, "gate2p")
        load(gate1_p, mod_s[0], rd=[mod_sb[0]])
        load(gate2_p, mod_s[1], rd=[mod_sb[1]])
        wov = w_out.rearrange("(kc p) n -> p kc n", p=128)
        for kc in range(8):
            for c0 in (0, 512):
                cast_block(wov[:, kc, c0:c0 + 512], 512, wout_b.ap[:, kc, c0:c0 + 512], wout_b, mode="rowcol",
                           arg=(rs_t.ap[:, kc:kc + 1], gate1_p.ap[:, c0:c0 + 512], rs_t, gate1_p))
        join(wout_b)
        hf_ring = [A.tile([128, 4, 128], F32, "hfl%d" % i) for i in range(2)]
        hb_ring = [A.tile([128, 512], F32, "hbl%d" % i) for i in range(2)]
        th_ring = [A.tile([128, 512], F32, "thl%d" % i) for i in range(2)]
        x_ring = [A.tile([128, D], F32, "xl%d" % i) for i in range(2)]
        mixT_ring = [A.tile([128, 8, 128], BF16, "mixT%d" % i) for i in range(2)]
        hsum_ring = [A.tile([128, 4, 128], F32, "hsum%d" % i) for i in range(2)]
        hm2_ring = [A.tile([128, 512], BF16, "hm2_%d" % i) for i in range(2)]
        st2_ring = [A.tile([128, 4], F32, "st2_%d" % i) for i in range(2)]
        junk2 = A.tile([128, 128], BF16, "junk2")
        wfiv = w_fi.rearrange("(kc p) n -> p kc n", p=128)
        wfov = w_fo.rearrange("(f p) n -> p f n", p=128)
        wjobs = []
        for kc in range(8):
            for c0 in range(0, 2 * DFF, 704):
                wjobs.append(("fi", kc, c0))
        for f in range(NFC):
            for c0 in (0, 512):
                wjobs.append(("fo", f, c0))

        def do_wjobs(n):
            for _ in range(n):
                if not wjobs:
                    return
                kind, a, c0 = wjobs.pop(0)
                if kind == "fi":
                    cast_block(wfiv[:, a, c0:c0 + 704], 704, Wfi.ap[:, a, c0:c0 + 704], Wfi, engs=["pool", "act", "dve", "act"], q="act")
                else:
                    cast_block(wfov[:, a, c0:c0 + 512], 512, Wfo.ap[:, a, c0:c0 + 512], Wfo, mode="colscale",
                               arg=(gate2_p.ap[:, c0:c0 + 512], gate2_p), engs=["pool", "dve"], q="act")

        def p2_loads(c):
            load(hf_ring[c % 2], hf_s[c].rearrange("p (h d) -> p h d", h=4), rd=[hf_b[c]])
            load(hb_ring[c % 2], hb_s[NT - 1 - c], rd=[hb_b[NT - 1 - c]])
            load(th_ring[c % 2], th_s[c], rd=[th_b[c]])
            load(x_ring[c % 2], xp[3, 128 + c * 128:256 + c * 128, :])
            mt = mixT_ring[c % 2]
            P.op("sp", lambda e: e.dma_start(out=mt.ap[:, 0:4, :], in_=at_s[c].rearrange("p (a b) -> p a b", a=4)),
                 reads=[at_b[c]], writes=[mt], dma=True)

        p2_loads(0)
        for c in range(NT):
            if c + 1 < NT:
                p2_loads(c + 1)
            do_wjobs(7)
            hfl, hbl, thl, xl, mt = hf_ring[c % 2], hb_ring[c % 2], th_ring[c % 2], x_ring[c % 2], mixT_ring[c % 2]
            pt, pb = ps1()
            P.op("pe", lambda e, pt=pt, hbl=hbl: e.matmul(pt.ap, lhsT=J_f, rhs=hbl.ap, start=True, stop=True),
                 reads=[cm_f, hbl], writes=pb)
            hs = hsum_ring[c % 2]
            P.op("dve", lambda e, pt=pt, hfl=hfl, hs=hs: e.tensor_tensor(out=hs.ap, in0=pt.ap.rearrange("p (h d) -> p h d", h=4),
                                                                       in1=hfl.ap, op=ALU.add), reads=pb + [hfl], writes=[hs])
            pfree(pt)
            st = st2_ring[c % 2]
            for h in range(4):
                P.op("act", lambda e, h=h, hs=hs, st=st: e.activation(out=junk2.ap, in_=hs.ap[:, h, :], func=AF.Square,
                                                                   accum_out=st.ap[:, h:h + 1]), reads=[hs], writes=[junk2, st])
            rstd_from_ssq(st.ap[:, 0:4], st, 4, 128)
            P.op("dve", lambda e, st=st: e.tensor_scalar(out=st.ap[:, 0:4], in0=st.ap[:, 0:4], scalar1=0.5, scalar2=None, op0=ALU.mult),
                 reads=[st], writes=[st])
            P.op("dve", lambda e, hs=hs, st=st: e.tensor_tensor(out=hs.ap, in0=hs.ap, in1=bc(st.ap[:, 0:4].unsqueeze(2), [128, 4, 128]),
                                                              op=ALU.mult), reads=[hs, st], writes=[hs])
            hm2 = hm2_ring[c % 2]
            P.op("dve", lambda e, hs=hs, thl=thl, hm2=hm2: e.scalar_tensor_tensor(out=hm2.ap, in0=thl.ap, scalar=1.0,
                                                                                in1=hs.ap.rearrange("p h d -> p (h d)"),
                                                                                op0=ALU.add, op1=ALU.mult), reads=[hs, thl], writes=[hm2])
            pt, pb = ps1()
            pv = pt.ap.bitcast(BF16)[:, 0:512].rearrange("p (a b) -> p a b", a=4)
            for kc in range(4):
                P.op("pe", lambda e, kc=kc, pv=pv, hm2=hm2: e.transpose(out=pv[:, kc, :], in_=hm2.ap[:, kc * 128:(kc + 1) * 128],
                                                                      identity=ident_b), reads=[hm2, cm_b], writes=pb)
            P.op("act", lambda e, pv=pv, mt=mt: e.copy(out=mt.ap[:, 4:8, :], in_=pv), reads=pb, writes=[mt])
            pfree(pt)
            x1o = xl
            for half in range(2):
                pt, pb = ps1()
                for kc in range(8):
                    P.op("pe", lambda e, kc=kc, pt=pt, mt=mt, half=half: e.matmul(pt.ap, lhsT=mt.ap[:, kc, :],
                                                                                rhs=wout_b.ap[:, kc, half * 512:(half + 1) * 512],
                                                                                start=(kc == 0), stop=(kc == 7)),
                         reads=[mt, wout_b], writes=pb)
                P.op("dve", lambda e, pt=pt, half=half, xl=xl, x1o=x1o: e.tensor_tensor(out=x1o.ap[:, half * 512:(half + 1) * 512], in0=pt.ap,
                                                                                      in1=xl.ap[:, half * 512:(half + 1) * 512], op=ALU.add),
                     reads=pb + [xl], writes=[x1o])
                pfree(pt)
            store(x1_s[c], x1_b[c], x1o)
        do_wjobs(1000)
        join(Wfi)
        join(Wfo)

        A.release(p2_mark)
        P.fence()
        gfin_t = A.tile([128, D], F32, "gfin")
        load(gfin_t, gfin_b)
        A2_p = A.tile([128, D], F32, "A2p")
        B2_p = A.tile([128, D], F32, "B2p")
        load(A2_p, mod_s[2], rd=[mod_sb[2]])
        load(B2_p, mod_s[3], rd=[mod_sb[3]])
        x1_ring = [A.tile([128, D], F32, "x1l%d" % i) for i in range(4)]
        junk3 = A.tile([128, D], BF16, "junk3")
        t1_ring[:] = [A.tile([128, D], F32, "t1b%d" % i) for i in range(1)]
        hm_ring[:] = [A.tile([128, D], BF16, "hmb%d" % i) for i in range(2)]
        st_ring[:] = [A.tile([128, 4], F32, "statb%d" % i) for i in range(4)]
        hffT_ring = [A.tile([128, 8, 256], BF16, "hffT%d" % i) for i in range(2)]
        gu_ring = [A.tile([128, NFC, 256], BF16, "gu%d" % i) for i in range(1)]
        sg_ring = [A.tile([128, 256], F32, "sg%d" % i) for i in range(3)]
        x2_ring = [A.tile([128, D], F32, "x2_%d" % i) for i in range(1)]
        oo_ring = [A.tile([128, D], F32, "oo%d" % i) for i in range(1)]
        st3_ring = [A.tile([128, 4], F32, "st3_%d" % i) for i in range(2)]
        junk_holder = [junk3]

        def ffn_loads(gi):
            for t in range(2):
                c = 2 * gi + t
                load(x1_ring[c % 4], x1_s[c], rd=[x1_b[c]])

        def ffn_group(gi):
            hffT = hffT_ring[gi % 2]
            xts = []
            if gi + 1 < NT // 2:
                ffn_loads(gi + 1)
            for t in range(2):
                c = 2 * gi + t
                i = cnt["tile"]
                xt = x1_ring[c % 4]
                cnt["tile"] += 1
                st = st_ring[i % 4]
                P.op("act", lambda e, xt=xt, st=st: e.activation(out=junk_holder[0].ap, in_=xt.ap, func=AF.Square, accum_out=st.ap[:, 0:1]),
                     reads=[xt], writes=[junk_holder[0], st])
                rstd_from_ssq(st.ap[:, 0:1], st, 1, D)
                t1 = t1_ring[0]
                P.op("dve", lambda e, xt=xt, st=st, t1=t1: e.scalar_tensor_tensor(out=t1.ap, in0=xt.ap, scalar=st.ap[:, 0:1], in1=A2_p.ap,
                                                                                op0=ALU.mult, op1=ALU.mult), reads=[xt, st, A2_p], writes=[t1])
                hm = hm_ring[i % 2]
                P.op("pool", lambda e, t1=t1, hm=hm: e.tensor_tensor(out=hm.ap, in0=t1.ap, in1=B2_p.ap, op=ALU.add),
                     reads=[t1, B2_p], writes=[hm])
                pt, pb = ps1()
                pv = pt.ap.bitcast(BF16).rearrange("p (a b) -> p a b", a=8)
                for kc in range(8):
                    P.op("pe", lambda e, kc=kc, pv=pv, hm=hm: e.transpose(out=pv[:, kc, :], in_=hm.ap[:, kc * 128:(kc + 1) * 128],
                                                                        identity=ident_b), reads=[hm, cm_b], writes=pb)
                P.op("act", lambda e, pv=pv, t=t: e.copy(out=hffT.ap[:, :, t * 128:(t + 1) * 128], in_=pv), reads=pb, writes=[hffT])
                pfree(pt)
                xts.append(xt)
            gu = gu_ring[0]
            for f in range(NFC):
                pt, pb = ps1()
                for half in range(2):
                    col = half * DFF + f * 128
                    for kc in range(8):
                        P.op("pe", lambda e, kc=kc, pt=pt, half=half, col=col: e.matmul(pt.ap[:, half * 256:(half + 1) * 256],
                                                                                      lhsT=Wfi.ap[:, kc, col:col + 128], rhs=hffT.ap[:, kc, :],
                                                                                      start=(kc == 0), stop=(kc == 7)),
                             reads=[Wfi, hffT], writes=pb)
                sg = sg_ring[f % 3]
                P.op("act", lambda e, pt=pt, sg=sg: e.activation(out=sg.ap, in_=pt.ap[:, 0:256], func=AF.Silu), reads=pb, writes=[sg])
                P.op("dve", lambda e, pt=pt, sg=sg, f=f: e.tensor_tensor(out=gu.ap[:, f, :], in0=pt.ap[:, 256:512], in1=sg.ap, op=ALU.mult),
                     reads=pb + [sg], writes=[gu])
                pfree(pt)
            for t in range(2):
                c = 2 * gi + t
                xt = xts[t]
                x2 = x2_ring[0]
                for half in range(2):
                    pt, pb = ps1()
                    for f in range(NFC):
                        P.op("pe", lambda e, f=f, pt=pt, half=half, t=t: e.matmul(pt.ap, lhsT=gu.ap[:, f, t * 128:(t + 1) * 128],
                                                                                rhs=Wfo.ap[:, f, half * 512:(half + 1) * 512],
                                                                                start=(f == 0), stop=(f == NFC - 1)),
                             reads=[gu, Wfo], writes=pb)
                    P.op("dve", lambda e, pt=pt, half=half, xt=xt, x2=x2: e.tensor_tensor(out=x2.ap[:, half * 512:(half + 1) * 512], in0=pt.ap,
                                                                                        in1=xt.ap[:, half * 512:(half + 1) * 512], op=ALU.add),
                         reads=pb + [xt], writes=[x2])
                    pfree(pt)
                st = st3_ring[c % 2]
                oo = oo_ring[0]
                P.op("act", lambda e, x2=x2, st=st, oo=oo: e.activation(out=oo.ap, in_=x2.ap, func=AF.Square, accum_out=st.ap[:, 0:1]),
                     reads=[x2], writes=[oo, st])
                rstd_from_ssq(st.ap[:, 0:1], st, 1, D)
                P.op("dve", lambda e, x2=x2, st=st, oo=oo: e.scalar_tensor_tensor(out=oo.ap, in0=x2.ap, scalar=st.ap[:, 0:1], in1=gfin_t.ap,
                                                                                op0=ALU.mult, op1=ALU.mult), reads=[x2, st, gfin_t], writes=[oo])
                store(out[c * 128:(c + 1) * 128, :], out_b[c], oo)

        ffn_loads(0)
        for gi in range(NT // 2):
            ffn_group(gi)
        P.op("sp", None, reads=out_b)

    except StopBuild:
        P.op("sp", None, reads=[dbg_b])
    with contextlib.ExitStack() as stack:
        P.emit(stack)
    return nc


def _consts():
    s = np.arange(128)[:, None]
    t = np.arange(128)[None, :]
    ident = (s == t).astype(np.float32)
    J = (s + t == 127).astype(np.float32)
    U = (s <= t).astype(np.float32)
    negU = -U
    maskf = U * np.float32(0.25 / math.sqrt(128.0))
    cmat = np.concatenate([ident, J, U, negU, maskf], axis=1).astype(np.float32)
    slopes = (2.0 ** (-8.0 * (np.arange(8, dtype=np.float32) + 1.0) / 8.0)).astype(np.float32)
    eb = np.zeros((128, 3, 8, 128), np.float32)
    for kb in range(3):
        kpos = (kb - 1) * 128 + np.arange(128)[:, None]
        qpos = np.arange(128)[None, :]
        dist = np.abs(kpos - qpos).astype(np.float32)
        valid = dist <= 128
        for h in range(8):
            eb[:, kb, h, :] = np.where(valid, np.exp(-slopes[h] * dist), 0.0)
    return cmat, eb.reshape(128, -1)


def _rep(v):
    return np.ascontiguousarray(np.broadcast_to(np.asarray(v, np.float32).reshape(1, -1), (128, v.size)))


def _prep_inputs(x, c, w_mod, b_mod, g_norm1, w_in, conv_w, conv_b, b_gates, sink, g_attn_out, g_mlstm_out,
                 w_out, g_norm2, w_ffn_in, w_ffn_out, g_final):
    f32 = np.float32
    x = np.asarray(x, f32)
    w_in0 = np.asarray(w_in, f32)[0]
    cmat, ebt = _consts()
    ka = w_in0[:, 512:640].reshape(D, 2, 64)
    kdup = np.concatenate([ka, ka], axis=2).reshape(D, 256)
    w_main = np.ascontiguousarray(np.concatenate([w_in0[:, 0:512], kdup, w_in0[:, 640:768], w_in0[:, 768:2816]], axis=1))
    gcols = w_in0[:, 2816:2832]
    bgv = np.asarray(b_gates, f32)[0]
    cwv = np.asarray(conv_w, f32)[0]
    cbv = np.asarray(conv_b, f32)[0]
    shared = {
        "w_mod": np.ascontiguousarray(np.asarray(w_mod, f32)[0]),
        "bmod_b": _rep(np.asarray(b_mod, f32)[0]),
        "g1_b": _rep(np.asarray(g_norm1, f32)[0]),
        "g2_b": _rep(np.asarray(g_norm2, f32)[0]),
        "gfin_b": _rep(np.asarray(g_final, f32)),
        "w_main": w_main,
        "cb": np.ascontiguousarray(cbv.reshape(8, 128).T),
        "sink_b": _rep(np.asarray(sink, f32)[0]),
        "rowsc": None,
        "w_out": np.ascontiguousarray(np.asarray(w_out, f32)[0]),
        "w_fi": np.ascontiguousarray(np.asarray(w_ffn_in, f32)[0]),
        "w_fo": np.ascontiguousarray(np.asarray(w_ffn_out, f32)[0]),
        "cmat": cmat,
        "ebt": ebt,
    }
    ga = np.asarray(g_attn_out, f32)[0].reshape(4, 128).T
    gm = np.asarray(g_mlstm_out, f32)[0].reshape(4, 128).T
    shared["rowsc"] = np.ascontiguousarray(np.concatenate([ga, gm], axis=1))
    in_maps = []
    for r in range(8):
        b, j = r // 4, r % 4
        xs = x[b]

        def ext(q, flip):
            lo, hi = q * 2048 - 128, q * 2048 + 2176
            buf = np.zeros((2304, D), f32)
            a, z = max(lo, 0), min(hi, SEQ)
            buf[a - lo:z - lo] = xs[a:z]
            return buf[::-1] if flip else buf

        passes = [(q, False) for q in range(j)] + [(q, True) for q in range(3, j, -1)] + [(j, False), (j, True)]
        xpa = np.stack([ext(q, fl_) for (q, fl_) in passes]).astype(f32)
        wg = np.zeros((D, 5, 8), f32)
        bg = np.zeros((5, 8), f32)
        cwa = np.zeros((128, 5, 8, 3), f32)
        flags = np.zeros((24,), f32)
        for pi, (q, fl_) in enumerate(passes):
            o = 8 if fl_ else 0
            wg[:, pi, :] = gcols[:, o:o + 8]
            bg[pi] = bgv[o:o + 8]
            taps = cwv[::-1] if fl_ else cwv
            cwa[:, pi, :, :] = taps.reshape(3, 8, 128).transpose(2, 1, 0)
            vl, vr = (q > 0), (q < 3)
            if fl_:
                vl, vr = vr, vl
            flags[9 + pi] = float(vl)
            flags[14 + pi] = float(vr)
        dirs = [p[1] for p in passes[:3]]
        for s in range(3):
            flags[s] = 1.0 if (s > 0 and dirs[s] == dirs[s - 1]) else 0.0
        lastf = max([s for s in range(3) if not dirs[s]], default=None)
        lastb = max([s for s in range(3) if dirs[s]], default=None)
        if lastf is not None:
            flags[3 + lastf] = 1.0
        if lastb is not None:
            flags[6 + lastb] = 1.0
        m = dict(shared)
        m["xp"] = np.ascontiguousarray(xpa)
        m["cvec"] = np.ascontiguousarray(np.asarray(c, f32)[b].reshape(8, 128).T)
        m["wg"] = np.ascontiguousarray(wg.reshape(D, 40))
        m["bg_b"] = _rep(bg.reshape(-1))
        m["cw"] = np.ascontiguousarray(cwa.reshape(128, -1))
        m["flags"] = _rep(flags)
        in_maps.append(m)
    return in_maps


_NC_CACHE = []


def kernel(**inputs):
    in_maps = _prep_inputs(**inputs)
    if not _NC_CACHE:
        _NC_CACHE.append(build_program())
    nc = _NC_CACHE[0]
    res = run_bass_kernel_spmd(nc, in_maps, core_ids=list(range(8)))
    outs = [np.asarray(r["out"], np.float32) for r in res.results]
    full = np.stack(outs).reshape(2, 4 * 2048, D)
    return full
```
